# Optimizing a Trainium2 kernel written in Bass

```python
import jax
import jax.numpy as jnp
from jax import lax
import numpy as np

D_MODEL = 4096
BATCH = 2
SEQ = 4096
DEPTH = 2
DEC_BATCH = 8
DEC_SEQ = 16
PAST_LEN = 1024

CHUNK = 64
N_AB = (DEPTH + 1) // 2
N_C = DEPTH // 2
D_A = D_MODEL // 2
A_GROUPS = 8
A_GROUP_DIM = D_A // A_GROUPS
GMLP_CHUNK = 128
D_B = D_MODEL // 2
H_B = 16
DH_B = D_B // H_B
Q_BLOCK = 128
AB_IN = 2 * D_A + 3 * D_B + H_B
H_C = 8
DK_C = D_MODEL // (2 * H_C)
DV_C = D_MODEL // H_C
C_IN = 2 * H_C * DK_C + 2 * H_C * DV_C + 2 * H_C
D_FF = -(-8 * D_MODEL // (3 * 256)) * 256
EPS = 1e-6
FOX_FORGET_BIAS = 3.0
MLSTM_FORGET_BIAS = 3.0

kernel_name = 'streaming_gmlp_fox_mlstm_hybrid'


def rmsnorm(x, g):
    x32 = x.astype(jnp.float32)
    y = x32 * lax.rsqrt(jnp.mean(x32 * x32, axis=-1, keepdims=True) + EPS)
    return (y * g.astype(jnp.float32)).astype(x.dtype)


def swiglu(h, w_gate, w_up, w_down):
    return (jax.nn.silu(h @ w_gate) * (h @ w_up)) @ w_down


def gmlp_spatial_gate(u, v, w_s, b):
    B, T, _ = u.shape
    L = min(T, GMLP_CHUNK)
    nc = T // L
    w = jnp.tril(w_s[:, :L, :L])
    vc = v.reshape(B, nc, L, A_GROUPS, A_GROUP_DIM)
    gate = jnp.einsum('grs,bcsgd->bcrgd', w, vc) + b[:, :L].T[None, None, :, :, None]
    return (u.reshape(B, nc, L, A_GROUPS, A_GROUP_DIM) * gate).reshape(B, T, D_A)


def fox_block(qb, cq, k, v, ck, qpos):
    s = jnp.einsum('bqhd,bkhd->bhqk', qb, k, preferred_element_type=jnp.float32) * (DH_B ** -0.5)
    s = s + cq[..., :, None] - ck[..., None, :]
    kpos = jnp.arange(k.shape[1])
    s = jnp.where(kpos[None, :] <= qpos[:, None], s, -jnp.inf)
    p = jax.nn.softmax(s, axis=-1)
    return jnp.einsum('bhqk,bkhd->bqhd', p.astype(v.dtype), v)


def ab_project(h, w_in, b_f):
    B, T, _ = h.shape
    cuts = [D_A, 2 * D_A, 2 * D_A + D_B, 2 * D_A + 2 * D_B, 2 * D_A + 3 * D_B]
    u, va, q, k, vb, f_pre = jnp.split(h @ w_in, cuts, axis=-1)
    logf = jax.nn.log_sigmoid(f_pre.astype(jnp.float32) + b_f.astype(jnp.float32))
    q = q.reshape(B, T, H_B, DH_B)
    k = k.reshape(B, T, H_B, DH_B)
    vb = vb.reshape(B, T, H_B, DH_B)
    return u, va, q, k, vb, logf


def ab_merge(a, o, w_out):
    B, T, _ = a.shape
    return jnp.concatenate([a.astype(o.dtype), o.reshape(B, T, D_B)], axis=-1) @ w_out


def ab_prompt(h, w_in, w_s, b_gm, b_f, w_out):
    B, T, _ = h.shape
    u, va, q, k, v, logf = ab_project(h, w_in, b_f)
    a = gmlp_spatial_gate(u, va, w_s, b_gm)
    ck = jnp.cumsum(logf, axis=1).transpose(0, 2, 1)

    def query_block(i):
        start = i * Q_BLOCK
        qb = lax.dynamic_slice_in_dim(q, start, Q_BLOCK, axis=1)
        cq = lax.dynamic_slice_in_dim(ck, start, Q_BLOCK, axis=2)
        return fox_block(qb, cq, k, v, ck, start + jnp.arange(Q_BLOCK))

    o = lax.map(query_block, jnp.arange(T // Q_BLOCK))
    o = jnp.moveaxis(o, 0, 1).reshape(B, T, H_B, DH_B)
    y = ab_merge(a, o, w_out).astype(h.dtype)
    return y, k, v, logf


def ab_sample(h, cache_k, cache_v, cache_logf, w_in, w_s, b_gm, b_f, w_out):
    B, S, _ = h.shape
    P = cache_k.shape[1]
    u, va, q, k, v, logf = ab_project(h, w_in, b_f)
    a = gmlp_spatial_gate(u, va, w_s, b_gm)
    k_all = jnp.concatenate([cache_k.astype(k.dtype), k], axis=1)
    v_all = jnp.concatenate([cache_v.astype(v.dtype), v], axis=1)
    lf_all = jnp.concatenate([cache_logf.astype(jnp.float32), logf], axis=1)
    ck = jnp.cumsum(lf_all, axis=1).transpose(0, 2, 1)
    o = fox_block(q, ck[:, :, P:], k_all, v_all, ck, P + jnp.arange(S))
    y = ab_merge(a, o, w_out).astype(h.dtype)
    return y, k, v, logf, va


def mlstm_chunk(carry, inp):
    c, n, m = carry
    q, k, v, ig, lf = inp
    L = q.shape[2]
    causal = jnp.tril(jnp.ones((L, L), dtype=bool))
    b = jnp.cumsum(lf, axis=-1)
    dlog = jnp.where(causal, b[..., :, None] - b[..., None, :] + ig[..., None, :], -jnp.inf)
    g = b + m[..., None]
    m_t = jnp.maximum(g, jnp.max(dlog, axis=-1))
    w_intra = jnp.exp(dlog - m_t[..., None])
    w_inter = jnp.exp(g - m_t)
    s = jnp.einsum('bhtd,bhsd->bhts', q, k) * w_intra
    num = jnp.einsum('bhts,bhsv->bhtv', s, v) + w_inter[..., None] * jnp.einsum('bhtd,bhdv->bhtv', q, c)
    den = jnp.sum(s, axis=-1) + w_inter * jnp.einsum('bhtd,bhd->bht', q, n)
    h = num / jnp.maximum(jnp.abs(den), jnp.exp(-m_t))[..., None]
    b_end = b[..., -1]
    to_end = b_end[..., None] - b + ig
    m_new = jnp.maximum(b_end + m, jnp.max(to_end, axis=-1))
    w_s = jnp.exp(to_end - m_new[..., None])
    w_c = jnp.exp(b_end + m - m_new)
    c_new = w_c[..., None, None] * c + jnp.einsum('bhsd,bhsv->bhdv', k * w_s[..., None], v)
    n_new = w_c[..., None] * n + jnp.einsum('bhs,bhsd->bhd', w_s, k)
    return (c_new, n_new, m_new), h


def mlstm_mixer(h, c0, n0, m0, w_in, b_i, b_f, g_norm, w_out):
    B, T, _ = h.shape
    cuts = [H_C * DK_C, 2 * H_C * DK_C, 2 * H_C * DK_C + H_C * DV_C,
            2 * H_C * DK_C + 2 * H_C * DV_C, 2 * H_C * DK_C + 2 * H_C * DV_C + H_C]
    q, k, v, o_pre, i_pre, f_pre = jnp.split(h @ w_in, cuts, axis=-1)
    q = q.reshape(B, T, H_C, DK_C).transpose(0, 2, 1, 3).astype(jnp.float32)
    k = k.reshape(B, T, H_C, DK_C).transpose(0, 2, 1, 3).astype(jnp.float32) * (DK_C ** -0.5)
    v = v.reshape(B, T, H_C, DV_C).transpose(0, 2, 1, 3).astype(jnp.float32)
    ig = (i_pre.astype(jnp.float32) + b_i.astype(jnp.float32)).transpose(0, 2, 1)
    lf = jax.nn.log_sigmoid(f_pre.astype(jnp.float32) + b_f.astype(jnp.float32)).transpose(0, 2, 1)
    L = min(T, CHUNK)
    nc = T // L

    def blocks(t):
        return jnp.moveaxis(t.reshape((B, H_C, nc, L) + t.shape[3:]), 2, 0)

    init = (c0.astype(jnp.float32), n0.astype(jnp.float32), m0.astype(jnp.float32))
    (c, n, m), hs = lax.scan(mlstm_chunk, init, (blocks(q), blocks(k), blocks(v), blocks(ig), blocks(lf)))
    hs = jnp.moveaxis(hs, 0, 2).reshape(B, H_C, T, DV_C).transpose(0, 2, 1, 3)
    hn = hs * lax.rsqrt(jnp.mean(hs * hs, axis=-1, keepdims=True) + EPS) * g_norm.astype(jnp.float32).reshape(H_C, DV_C)
    out = jax.nn.sigmoid(o_pre.astype(jnp.float32)) * hn.reshape(B, T, H_C * DV_C)
    y = out.astype(h.dtype) @ w_out
    return y, c, n, m


def setup_inputs(seed: int = 0) -> dict:
    key = jax.random.key(seed)
    ks = jax.random.split(key, 24)

    def nrm(k, shape, scale):
        return jax.random.normal(k, shape, jnp.float32) * scale

    return {
        'x_prompt': nrm(ks[0], (BATCH, SEQ, D_MODEL), 1.0),
        'x_sample': nrm(ks[1], (DEC_BATCH, DEC_SEQ, D_MODEL), 1.0),
        'cache_fox_k': nrm(ks[2], (N_AB, DEC_BATCH, PAST_LEN, H_B, DH_B), 1.0),
        'cache_fox_v': nrm(ks[3], (N_AB, DEC_BATCH, PAST_LEN, H_B, DH_B), 1.0),
        'cache_fox_logf': jax.nn.log_sigmoid(FOX_FORGET_BIAS + nrm(ks[4], (N_AB, DEC_BATCH, PAST_LEN, H_B), 1.0)),
        'state_mlstm_c': nrm(ks[5], (N_C, DEC_BATCH, H_C, DK_C, DV_C), 0.1),
        'state_mlstm_n': nrm(ks[6], (N_C, DEC_BATCH, H_C, DK_C), 0.1),
        'state_mlstm_m': nrm(ks[7], (N_C, DEC_BATCH, H_C), 1.0),
        'norm_mix': 1.0 + nrm(ks[8], (DEPTH, D_MODEL), 0.02),
        'norm_ffn': 1.0 + nrm(ks[9], (DEPTH, D_MODEL), 0.02),
        'norm_final': 1.0 + nrm(ks[10], (D_MODEL,), 0.02),
        'ab_w_in': nrm(ks[11], (N_AB, D_MODEL, AB_IN), D_MODEL ** -0.5),
        'ab_w_out': nrm(ks[12], (N_AB, D_A + D_B, D_MODEL), (D_A + D_B) ** -0.5),
        'gmlp_w_s': nrm(ks[13], (N_AB, A_GROUPS, GMLP_CHUNK, GMLP_CHUNK), GMLP_CHUNK ** -0.5),
        'gmlp_b': 1.0 + nrm(ks[14], (N_AB, A_GROUPS, GMLP_CHUNK), 0.02),
        'fox_b_f': FOX_FORGET_BIAS + nrm(ks[15], (N_AB, H_B), 0.1),
        'c_w_in': nrm(ks[16], (N_C, D_MODEL, C_IN), D_MODEL ** -0.5),
        'c_b_i': nrm(ks[17], (N_C, H_C), 0.1),
        'c_b_f': MLSTM_FORGET_BIAS + nrm(ks[18], (N_C, H_C), 0.1),
        'c_head_norm': 1.0 + nrm(ks[19], (N_C, H_C * DV_C), 0.02),
        'c_w_out': nrm(ks[20], (N_C, H_C * DV_C, D_MODEL), (H_C * DV_C) ** -0.5),
        'ffn_w_gate': nrm(ks[21], (DEPTH, D_MODEL, D_FF), D_MODEL ** -0.5),
        'ffn_w_up': nrm(ks[22], (DEPTH, D_MODEL, D_FF), D_MODEL ** -0.5),
        'ffn_w_down': nrm(ks[23], (DEPTH, D_FF, D_MODEL), D_FF ** -0.5),
    }


def reference(x_prompt, x_sample, cache_fox_k, cache_fox_v, cache_fox_logf, state_mlstm_c, state_mlstm_n,
              state_mlstm_m, norm_mix, norm_ffn, norm_final, ab_w_in, ab_w_out, gmlp_w_s, gmlp_b, fox_b_f,
              c_w_in, c_b_i, c_b_f, c_head_norm, c_w_out, ffn_w_gate, ffn_w_up, ffn_w_down):
    yp, ys = x_prompt, x_sample
    pk, pv, plf, pc, pn, pm = [], [], [], [], [], []
    sk, sv, slf, sgv, sc, sn, sm = [], [], [], [], [], [], []
    for layer in range(DEPTH):
        j = layer // 2
        hp = rmsnorm(yp, norm_mix[layer])
        hs = rmsnorm(ys, norm_mix[layer])
        if layer % 2 == 0:
            mp, k_p, v_p, lf_p = ab_prompt(hp, ab_w_in[j], gmlp_w_s[j], gmlp_b[j], fox_b_f[j], ab_w_out[j])
            ms, k_s, v_s, lf_s, gv_s = ab_sample(hs, cache_fox_k[j], cache_fox_v[j], cache_fox_logf[j],
                                                 ab_w_in[j], gmlp_w_s[j], gmlp_b[j], fox_b_f[j], ab_w_out[j])
            pk.append(k_p); pv.append(v_p); plf.append(lf_p)
            sk.append(k_s); sv.append(v_s); slf.append(lf_s); sgv.append(gv_s)
        else:
            bp = yp.shape[0]
            c0 = jnp.zeros((bp, H_C, DK_C, DV_C), jnp.float32)
            n0 = jnp.zeros((bp, H_C, DK_C), jnp.float32)
            m0 = jnp.zeros((bp, H_C), jnp.float32)
            mp, c_p, n_p, m_p = mlstm_mixer(hp, c0, n0, m0, c_w_in[j], c_b_i[j], c_b_f[j], c_head_norm[j], c_w_out[j])
            ms, c_s, n_s, m_s = mlstm_mixer(hs, state_mlstm_c[j], state_mlstm_n[j], state_mlstm_m[j],
                                            c_w_in[j], c_b_i[j], c_b_f[j], c_head_norm[j], c_w_out[j])
            pc.append(c_p); pn.append(n_p); pm.append(m_p)
            sc.append(c_s); sn.append(n_s); sm.append(m_s)
        yp = yp + mp
        ys = ys + ms
        yp = yp + swiglu(rmsnorm(yp, norm_ffn[layer]), ffn_w_gate[layer], ffn_w_up[layer], ffn_w_down[layer])
        ys = ys + swiglu(rmsnorm(ys, norm_ffn[layer]), ffn_w_gate[layer], ffn_w_up[layer], ffn_w_down[layer])
    yp = rmsnorm(yp, norm_final)
    ys = rmsnorm(ys, norm_final)
    return (yp, ys, jnp.stack(pk), jnp.stack(pv), jnp.stack(plf), jnp.stack(pc), jnp.stack(pn), jnp.stack(pm),
            jnp.stack(sk), jnp.stack(sv), jnp.stack(slf), jnp.stack(sgv), jnp.stack(sc), jnp.stack(sn), jnp.stack(sm))
```

```python
import os
import numpy as np
import ml_dtypes
from contextlib import ExitStack
import concourse.bass as bass
import concourse.mybir as mybir
from concourse.bass_utils import run_bass_kernel_spmd

F32 = mybir.dt.float32
BF16 = mybir.dt.bfloat16
AF = mybir.ActivationFunctionType
ALU = mybir.AluOpType

P = 128
D = 4096
KT = 32
TP = 1024
TS = 16
T = TP + TS
CH = [(0, 512), (512, 512), (1024, 16)]
TT = [(i * 128, 128) for i in range(8)] + [(1024, 16)]
D_A = 2048
D_B = 2048
H_B = 16
AB_IN = 10256
H_C = 8
DK = 256
DV = 512
C_IN = 12304
D_FF = 11008
EPS = 1e-6
NCORES = int(os.environ.get('KCORES', '8'))
GROUPS = [list(range(g * 4, g * 4 + 4)) for g in range(NCORES // 4)]

STAGE = int(os.environ.get('KSTAGE', '99'))
SKIP = set(os.environ.get('KSKIP', '').split(','))


class Buf:
    __slots__ = ("name", "w", "wp", "r", "dsem", "excl")

    def __init__(self, name, dsem=None):
        self.name = name
        self.excl = False
        self.w = {}
        self.wp = {}
        self.r = {}
        self.dsem = dsem


class Q:
    def __init__(self, ker, eng, name, is_pe=False):
        self.ker = ker
        self.eng = eng
        self.name = name
        self.is_pe = is_pe
        self.key = ker.new_sem("q_" + name)
        self.seen = {}

    def wait(self, k, v):
        if self.seen.get(k, 0) >= v:
            return
        assert v <= self.ker.issued[k], (self.name, k, v, self.ker.issued[k])
        self.eng.wait_ge(self.ker.sems[k], v)
        self.seen[k] = v


class Ker:
    def __init__(self, nc, es):
        self.nc = nc
        self.es = es
        self.sems = []
        self.issued = []
        self.pe = Q(self, nc.tensor, "pe", True)
        self.act = Q(self, nc.scalar, "act")
        self.dve = Q(self, nc.vector, "dve")
        self.pool = Q(self, nc.gpsimd, "pool")
        self.sp = Q(self, nc.sync, "sp")
        self.queues = [self.pe, self.act, self.dve, self.pool, self.sp]
        self.dma_pool = [self.new_sem(f"d{i}") for i in range(36)]
        self.dma_next = 0
        self.sw_pool = [self.new_sem(f"w{i}") for i in range(12)]
        self.sw_next = 0
        self.dma_last = {}
        self.uid = 0

    def new_sem(self, name):
        s = self.es.enter_context(self.nc.semaphore(name))
        self.sems.append(s)
        self.issued.append(0)
        return len(self.sems) - 1

    def buf(self, name, dma=False):
        ds = None
        if dma == "sw":
            ds = self.sw_pool[self.sw_next % len(self.sw_pool)]
            self.sw_next += 1
        elif dma:
            ds = self.dma_pool[self.dma_next % len(self.dma_pool)]
            self.dma_next += 1
        return Buf(name, ds)

    def _deps(self, q, reads, writes, pwrites=()):
        deps = {}

        def add(d):
            for k, v in d.items():
                if deps.get(k, 0) < v:
                    deps[k] = v
        for b in reads:
            add(b.w)
            add(b.wp)
            if b.excl:
                add({k: v for k, v in b.r.items() if k != q.key})
        for b in writes:
            add(b.w)
            add(b.wp)
            add(b.r)
        for b in pwrites:
            add(b.w)
            add(b.r)
        for k, v in deps.items():
            if q.is_pe and k == q.key:
                continue
            q.wait(k, v)

    def _record(self, ev, reads, writes, pwrites=()):
        k, v = ev
        for b in reads:
            if b.r.get(k, 0) < v:
                b.r[k] = v
        for b in writes:
            b.w = {k: v}
            b.wp = {}
            b.r = {}
        for b in pwrites:
            if b.wp.get(k, 0) < v:
                b.wp[k] = v

    def op(self, q, fn, reads=(), writes=(), pwrites=(), signal=True):
        self._deps(q, reads, writes, pwrites)
        ins = fn()
        if signal:
            self.issued[q.key] += 1
            ins.then_inc(self.sems[q.key], 1)
            ev = (q.key, self.issued[q.key])
        else:
            ev = (q.key, self.issued[q.key] + 1)
        self._record(ev, reads, writes, pwrites)
        return ins

    def dma(self, q, out, in_, reads=(), writes=(), pwrites=(), sbuf=None, **kw):
        k = sbuf.dsem
        self._deps(q, reads, writes, pwrites)
        if self.issued[k] > 0:
            q.wait(k, self.issued[k])
        ins = q.eng.dma_start(out=out, in_=in_, **kw)
        self.issued[k] += 16
        ins.then_inc(self.sems[k], 16)
        self._record((k, self.issued[k]), reads, writes, pwrites)
        return ins

    def barrier(self):
        for q in self.queues:
            for k in range(len(self.sems)):
                if self.issued[k] > 0:
                    q.wait(k, self.issued[k])


def build_program():
    nc = bass.Bass("TRN2", target_bir_lowering=False)
    declared = []

    def dt_in(name, shape, dt=F32):
        declared.append(name)
        return nc.dram_tensor(name, list(shape), dt, kind="ExternalInput").ap()
    dt_out = lambda name, shape, dt=F32: nc.dram_tensor(name, list(shape), dt, kind="ExternalOutput").ap()
    dt_tmp = lambda name, shape, dt=F32: nc.dram_tensor(name, list(shape), dt).ap()

    x_tok = dt_in("x_tok", [T, D])
    cache_k = dt_in("cache_k", [1024, 2048])
    cache_v = dt_in("cache_v", [1024, 2048])
    cache_lf = dt_in("cache_lf", [1024, 16])
    st_c = dt_in("st_c", [H_C, DK, DV])
    st_n = dt_in("st_n", [H_C, DK])
    st_m_b = dt_in("st_m_b", [P, H_C])
    norms = dt_in("norms", [P, 5, KT])
    ab_w_in_l = lambda: dt_in("ab_w_in", [D, AB_IN])
    ab_w_out_l = lambda: dt_in("ab_w_out", [D, D])
    gmlp_ws = dt_in("gmlp_ws", [8, P, P])
    gmlp_bb = dt_in("gmlp_bb", [P, 8, P])
    fox_bf = dt_in("fox_bf", [16, 1])
    c_w_in_l = lambda: dt_in("c_w_in", [D, C_IN])
    c_bif = dt_in("c_bif", [16, 1])
    c_hn = dt_in("c_hn", [P, 32])
    c_w_out_l = lambda: dt_in("c_w_out", [D, D])
    ffn_wg_l = lambda: dt_in("ffn_wg", [2, D, D_FF])
    ffn_wu_l = lambda: dt_in("ffn_wu", [2, D, D_FF])
    ffn_wd_l = lambda: dt_in("ffn_wd", [2, D_FF, D])
    consts = dt_in("consts", [P, 5, P])
    qpos_b = dt_in("qpos_b", [P, T])
    kpos = dt_in("kpos", [P, 40])
    pmask_b = dt_in("pmask_b", [P, 8])

    y_tok = dt_out("y_tok", [T, D])
    o_k = dt_out("o_k", [T, 2048])
    o_v = dt_out("o_v", [T, 2048])
    o_lf = dt_out("o_lf", [T, 16])
    o_gv = dt_out("o_gv", [TS, 2048])
    o_c = dt_out("o_c", [2, H_C, DK, DV])
    o_n = dt_out("o_n", [2, H_C, DK])
    o_m = dt_out("o_m", [2, H_C])

    xres = dt_tmp("xres", [KT, P, T])
    uT_d = dt_tmp("uT_d", [16, P, T], BF16)
    va_d = dt_tmp("va_d", [T, 2048], BF16)
    qT_d = dt_tmp("qT_d", [16, P, T], BF16)
    kT_in = [dt_tmp(f"kT_in{j}", [4 * P, TP], BF16) for j in range(4)]
    v_in = [dt_tmp(f"v_in{j}", [TP, 4 * P], BF16) for j in range(4)]
    kT_all = [dt_tmp(f"kT_all{j}", [4 * 4 * P, TP], BF16) for j in range(4)]
    v_all = [dt_tmp(f"v_all{j}", [4 * TP, 4 * P], BF16) for j in range(4)]
    kTs_d = dt_tmp("kTs_d", [16, P, TS], BF16)
    vs_d = dt_tmp("vs_d", [TS, 2048], BF16)
    c_in = dt_tmp("c_in", [16, TP])
    c_all = dt_tmp("c_all", [64, TP])
    ffn_wg, ffn_wu, ffn_wd = (ffn_wg_l(), ffn_wu_l(), ffn_wd_l()) if STAGE >= 5 else (None, None, None)

    es = ExitStack()
    with es:
        ker = Ker(nc, es)
        pe, act, dve, pool, sp = ker.pe, ker.act, ker.dve, ker.pool, ker.sp
        _uid = [0]

        def sb(st, name, shape, dt=F32):
            _uid[0] += 1
            return st.enter_context(nc.sbuf_tensor(f"{name}_{_uid[0]}", list(shape), dt))

        cst = sb(es, "cst", [P, 5, P])
        ident = cst[:, 0, :]
        ones_f = sb(es, "ones_f", [P, P])
        ones_b = sb(es, "ones_b", [P, P], BF16)
        nrm = sb(es, "nrm", [P, 5, KT])
        lfT = sb(es, "lfT", [16, T])
        b_cst = ker.buf("cst", dma=True)
        b_nrm = ker.buf("nrm", dma=True)
        b_ones = ker.buf("ones")
        b_lfT = ker.buf("lfT")
        banks = [es.enter_context(nc.psum_tensor(f"bank{i}", [P, 512], F32)) for i in range(8)]
        bbank = [ker.buf(f"bank{i}") for i in range(8)]
        for b_ in bbank:
            b_.excl = True

        ker.dma(sp, cst[:], consts, writes=[b_cst], sbuf=b_cst)
        ker.dma(sp, nrm[:], norms, writes=[b_nrm], sbuf=b_nrm)
        ker.op(dve, lambda: nc.vector.memset(ones_f[:], 1.0), writes=[b_ones])
        ker.op(dve, lambda: nc.vector.memset(ones_b[:], 1.0), writes=[b_ones])

        acc_ring = [0]

        def next_acc():
            i = acc_ring[0] % 6
            acc_ring[0] += 1
            return banks[i], bbank[i]

        tr_ring = [0]

        def tr_bank():
            i = tr_ring[0] % 2
            tr_ring[0] += 1
            return banks[6 + i], bbank[6 + i]

        def transpose_group(srcs, dst_writer):
            bk, bb = tr_bank()
            for j, (in_ap, m, k) in enumerate(srcs):
                reads_j = in_ap[1]
                ker.op(pe, lambda: nc.tensor.transpose(bk[:m, j * P:j * P + k], in_ap[0], ident[:k, :k]),
                       reads=list(reads_j) + [b_cst], writes=[bb] if j == 0 else (), pwrites=() if j == 0 else [bb])
            dst_writer(bk, bb)

        ev_alt = [0]

        def evac(out_ap, in_ap, reads, pwrites, eng=None):
            if eng is None:
                eng = act if ev_alt[0] % 2 == 0 else dve
                ev_alt[0] += 1
            if eng is act:
                return ker.op(act, lambda: nc.scalar.copy(out=out_ap, in_=in_ap), reads=reads, pwrites=pwrites)
            return ker.op(dve, lambda: nc.vector.tensor_copy(out=out_ap, in_=in_ap), reads=reads, pwrites=pwrites)

        xres_v = xres.rearrange("kt p t -> p kt t")
        b_xres = [ker.buf(f"xres{kt}") for kt in range(KT)]
        with ExitStack() as ps:
            xin = [sb(ps, f"xin{i}", [P, D]) for i in range(2)]
            b_xin = [ker.buf(f"xin{i}", dma=True) for i in range(2)]
            xst = [sb(ps, f"xst{i}", [P, KT, P]) for i in range(2)]
            b_xst = [ker.buf(f"xst{i}", dma=True) for i in range(2)]
            for ti, (t0, tn) in enumerate(TT):
                s = ti % 2
                ker.dma(sp, xin[s][:tn, :], x_tok[t0:t0 + tn, :], writes=[b_xin[s]], sbuf=b_xin[s])
                for g4 in range(KT // 4):
                    srcs = [((xin[s][:tn, (g4 * 4 + j) * P:(g4 * 4 + j + 1) * P], [b_xin[s]]), P, tn) for j in range(4)]

                    def wr(bk, bb, g4=g4, s=s, tn=tn):
                        evac(xst[s][:, g4 * 4:g4 * 4 + 4, :tn], bk[:, :].rearrange("p (j n) -> p j n", n=P)[:, :, :tn], [bb], [b_xst[s]])
                    transpose_group(srcs, wr)
                ker.dma(sp, xres_v[:, :, t0:t0 + tn], xst[s][:, :, :tn], reads=[b_xst[s]], writes=b_xres, sbuf=b_xst[s])
            ker.barrier()

        def rmsnorm_to(ps, dst, b_dst, gi, out_f32_cb=None):
            xs = [sb(ps, f"xs{i}", [P, T]) for i in range(3)]
            b_xs = [ker.buf(f"xs{i}", dma=True) for i in range(3)]
            sq = [sb(ps, f"sq{i}", [P, T]) for i in range(2)]
            b_sq = [ker.buf(f"sq{i}") for i in range(2)]
            rstd = sb(ps, "rstd", [P, T])
            b_rstd = ker.buf("rstd")
            accs = [next_acc() for _ in range(3)]
            for kt in range(KT):
                s = kt % 3
                ker.dma(sp, xs[s][:], xres_v[:, kt, :], reads=[b_xres[kt]], writes=[b_xs[s]], sbuf=b_xs[s])
                q2 = kt % 2
                ker.op(act, lambda: nc.scalar.activation(out=sq[q2][:], in_=xs[s][:], func=AF.Square),
                       reads=[b_xs[s]], writes=[b_sq[q2]])
                for ci, (c0, cn) in enumerate(CH):
                    bk, bb = accs[ci]
                    ker.op(pe, lambda: nc.tensor.matmul(bk[:, :cn], ones_f[:], sq[q2][:, c0:c0 + cn],
                                                        start=(kt == 0), stop=(kt == KT - 1)),
                           reads=[b_sq[q2], b_ones], writes=[bb], signal=(kt == KT - 1 or True))
            for ci, (c0, cn) in enumerate(CH):
                bk, bb = accs[ci]
                ker.op(act, lambda: nc.scalar.activation(out=rstd[:, c0:c0 + cn], in_=bk[:, :cn], func=AF.Sqrt,
                                                         scale=1.0 / D, bias=eps_t[:, 0:1]),
                       reads=[bb, b_ones], pwrites=[b_rstd])
            ker.op(dve, lambda: nc.vector.reciprocal(out=rstd[:], in_=rstd[:]), reads=[b_rstd], writes=[b_rstd])
            for kt in range(KT):
                s = kt % 3
                ker.dma(sp, xs[s][:], xres_v[:, kt, :], reads=[b_xres[kt]], writes=[b_xs[s]], sbuf=b_xs[s])
                if out_f32_cb is None:
                    ker.op(dve, lambda: nc.vector.scalar_tensor_tensor(out=dst[:, kt, :], in0=xs[s][:], scalar=nrm[:, gi, kt:kt + 1],
                                                                      in1=rstd[:], op0=ALU.mult, op1=ALU.mult),
                           reads=[b_xs[s], b_rstd, b_nrm], pwrites=[b_dst])
                else:
                    out_f32_cb(kt, xs[s], b_xs[s], rstd, b_rstd)

        eps_t = sb(es, "eps_t", [P, 1])
        ker.op(dve, lambda: nc.vector.memset(eps_t[:], EPS), writes=[b_ones])

        WS_N = 4
        WSL = {}

        def alloc_wslots(ps):
            WSL["w"] = [sb(ps, f"wslot{i}", [P, 8192], BF16) for i in range(WS_N)]
            WSL["b"] = [ker.buf(f"wslot{i}", dma="sw") for i in range(WS_N)]

        def make_accum_epi(ost, b_ost):
            st = {"i": 0}

            def epi(tag, ci, c0, cn, m, bk, bb):
                o = tag[1]
                if ci == 0:
                    st["i"] += 1
                s = st["i"] % 2
                evac(ost[s][:, c0:c0 + cn], bk[:, :cn], [bb], [b_ost[s]])
                if ci == len(CH) - 1:
                    ker.dma(pool, xres_v[:, o, :], ost[s][:], reads=[b_ost[s]], writes=[b_xres[o]], sbuf=b_ost[s], accum_op=ALU.add)
            return epi

        def gemm(A_of, blocks, jobs, chunks, epilogue):
            wslots, b_wslots = WSL["w"], WSL["b"]
            nblk = len(blocks)
            loaded = [0]
            ring = [0]
            slot_of = {}

            def load_next():
                bi = loaded[0]
                if bi >= nblk:
                    return
                Wap, ktn, ncols = blocks[bi]
                s = ring[0] % WS_N
                ring[0] += 1
                slot_of[bi] = s
                dstv = wslots[s][:, 0:ktn * ncols].rearrange("p (kt n) -> p kt n", n=ncols)
                ker.dma(pool, dstv, Wap.rearrange("(kt p) n -> p kt n", p=P), writes=[b_wslots[s]], sbuf=b_wslots[s])
                loaded[0] += 1

            last_use = {}
            for ji, (bi, co, m, tag) in enumerate(jobs):
                last_use[bi] = ji
            for _ in range(min(WS_N - 1, nblk)):
                load_next()
            for ji, (bi, co, m, tag) in enumerate(jobs):
                while bi >= loaded[0]:
                    load_next()
                Wap, ktn, ncols = blocks[bi]
                s = slot_of[bi]
                wv = wslots[s][:, 0:ktn * ncols].rearrange("p (kt n) -> p kt n", n=ncols)
                for ci, (c0, cn) in enumerate(chunks):
                    bk, bb = next_acc()
                    for kt in range(ktn):
                        a_ap, a_b = A_of(tag, kt)
                        ker.op(pe, lambda: nc.tensor.matmul(bk[:m, :cn], wv[:, kt, co:co + m], a_ap[:, c0:c0 + cn],
                                                            start=(kt == 0), stop=(kt == ktn - 1)),
                               reads=[b_wslots[s], a_b], writes=[bb], signal=(kt == ktn - 1))
                    epilogue(tag, ci, c0, cn, m, bk, bb)
                if last_use[bi] == ji:
                    load_next()

        with ExitStack() as pA:
            A = sb(pA, "A", [P, KT, T], BF16)
            b_A = ker.buf("A")
            with ExitStack() as ps:
                rmsnorm_to(ps, A, b_A, 0)
                ker.barrier()
            if STAGE >= 1 and 'l0' not in SKIP:
                with ExitStack() as ps:
                    alloc_wslots(ps)
                    stg = [sb(ps, f"stg{i}", [P, T]) for i in range(2)]
                    b_stg = [ker.buf(f"stg{i}") for i in range(2)]
                    stb = [sb(ps, f"stb{i}", [P, T], BF16) for i in range(2)]
                    b_stb = [ker.buf(f"stb{i}", dma=True) for i in range(2)]
                    tko = [sb(ps, f"tko{i}", [P, 9, P]) for i in range(2)]
                    b_tko = [ker.buf(f"tko{i}", dma=True) for i in range(2)]
                    tkb = [sb(ps, f"tkb{i}", [P, 9, P], BF16) for i in range(2)]
                    b_tkb = [ker.buf(f"tkb{i}", dma=True) for i in range(2)]
                    nbf = sb(ps, "nbf", [16, 1])
                    b_nbf = ker.buf("nbf", dma=True)
                    lft = sb(ps, "lft", [P, 9, 16])
                    b_lft = ker.buf("lft", dma=True)
                    ker.dma(sp, nbf[:], fox_bf, writes=[b_nbf], sbuf=b_nbf)
                    ker.op(dve, lambda: nc.vector.tensor_scalar(out=nbf[:], in0=nbf[:], scalar1=-1.0, scalar2=None, op0=ALU.mult),
                           reads=[b_nbf], writes=[b_nbf])
                    ctr = [0]
                    cur = {}

                    def store_tokmajor(dst_dram, col0, src, b_src):
                        ker.dma(sp, dst_dram[0:TP, col0:col0 + P].rearrange("(tt p) n -> p tt n", p=P), src[:, 0:8, :],
                                reads=[b_src], sbuf=b_src)
                        ker.dma(sp, dst_dram[TP:T, col0:col0 + P], src[:TS, 8, :], reads=[b_src], sbuf=b_src)

                    def epi(tag, ci, c0, cn, m, bk, bb):
                        kind, idx = tag
                        if ci == 0:
                            cur["s"] = ctr[0] % 2
                            ctr[0] += 1
                        s = cur["s"]
                        if kind in ("u", "q"):
                            evac(stb[s][:, c0:c0 + cn], bk[:, :cn], [bb], [b_stb[s]])
                            if ci == 2:
                                dstd = uT_d if kind == "u" else qT_d
                                ker.dma(sp, dstd[idx], stb[s][:], reads=[b_stb[s]], sbuf=b_stb[s])
                            return
                        if kind == "f":
                            ker.op(act, lambda: nc.scalar.activation(out=lfT[:, c0:c0 + cn], in_=bk[:16, :cn], func=AF.Exp,
                                                                     scale=-1.0, bias=nbf[:, 0:1]),
                                   reads=[bb, b_nbf], pwrites=[b_lfT])
                            ker.op(act, lambda: nc.scalar.activation(out=lfT[:, c0:c0 + cn], in_=lfT[:, c0:c0 + cn], func=AF.Ln,
                                                                     scale=1.0, bias=ones_f[:16, 0:1]),
                                   reads=[b_lfT, b_ones], pwrites=[b_lfT])
                            ker.op(dve, lambda: nc.vector.tensor_scalar(out=lfT[:, c0:c0 + cn], in0=lfT[:, c0:c0 + cn], scalar1=-1.0,
                                                                        scalar2=None, op0=ALU.mult),
                                   reads=[b_lfT], pwrites=[b_lfT])
                            if ci == 2:
                                bk2, bb2 = tr_bank()
                                for ti, (t0, tn) in enumerate(TT):
                                    ker.op(pe, lambda: nc.tensor.transpose(bk2[:tn, ti * 16:(ti + 1) * 16], lfT[:, t0:t0 + tn], ident[:16, :16]),
                                           reads=[b_lfT, b_cst], writes=[bb2] if ti == 0 else (), pwrites=() if ti == 0 else [bb2])
                                evac(lft[:, 0:8, :], bk2[:, 0:128].rearrange("p (j n) -> p j n", n=16), [bb2], [b_lft])
                                evac(lft[:TS, 8, :], bk2[:TS, 128:144], [bb2], [b_lft])
                                ker.dma(sp, o_lf[0:TP, :].rearrange("(tt p) n -> p tt n", p=P), lft[:, 0:8, :], reads=[b_lft], sbuf=b_lft)
                                ker.dma(sp, o_lf[TP:T, :], lft[:TS, 8, :], reads=[b_lft], sbuf=b_lft)
                            return
                        evac(stg[s][:, c0:c0 + cn], bk[:, :cn], [bb], [b_stg[s]])
                        if kind == "k":
                            ker.op(dve, lambda: nc.vector.tensor_copy(out=stb[s][:, c0:c0 + cn], in_=stg[s][:, c0:c0 + cn]),
                                   reads=[b_stg[s]], pwrites=[b_stb[s]])
                        if ci != 2:
                            return
                        if kind == "k":
                            ker.dma(sp, kT_in[idx // 4][(idx % 4) * P:(idx % 4 + 1) * P, :], stb[s][:, 0:TP], reads=[b_stb[s]], sbuf=b_stb[s])
                            ker.dma(sp, kTs_d[idx], stb[s][:, TP:T], reads=[b_stb[s]], sbuf=b_stb[s])
                        for g0, g1 in ((0, 4), (4, 8), (8, 9)):
                            srcs = [((stg[s][:, TT[ti][0]:TT[ti][0] + TT[ti][1]], [b_stg[s]]), TT[ti][1], P) for ti in range(g0, g1)]

                            def wr(bk, bb, g0=g0, g1=g1, s=s, kind=kind):
                                if g0 < 8:
                                    src = bk[:, :].rearrange("p (j n) -> p j n", n=P)
                                    if kind in ("k", "v"):
                                        ker.op(act, lambda: nc.scalar.copy(out=tko[s][:, g0:g1, :], in_=src), reads=[bb], pwrites=[b_tko[s]])
                                    if kind in ("va", "v"):
                                        ker.op(dve, lambda: nc.vector.tensor_copy(out=tkb[s][:, g0:g1, :], in_=src), reads=[bb], pwrites=[b_tkb[s]])
                                else:
                                    ker.op(act, lambda: nc.scalar.copy(out=tko[s][:TS, 8, :], in_=bk[:TS, 0:P]), reads=[bb], pwrites=[b_tko[s]])
                                    if kind in ("va", "v"):
                                        ker.op(dve, lambda: nc.vector.tensor_copy(out=tkb[s][:TS, 8, :], in_=bk[:TS, 0:P]), reads=[bb], pwrites=[b_tkb[s]])
                            transpose_group(srcs, wr)
                        col0 = idx * P
                        if kind == "k":
                            store_tokmajor(o_k, col0, tko[s], b_tko[s])
                        elif kind == "v":
                            store_tokmajor(o_v, col0, tko[s], b_tko[s])
                            ker.dma(sp, v_in[idx // 4][:, (idx % 4) * P:(idx % 4 + 1) * P].rearrange("(tt p) n -> p tt n", p=P), tkb[s][:, 0:8, :],
                                    reads=[b_tkb[s]], sbuf=b_tkb[s])
                            ker.dma(sp, vs_d[:, col0:col0 + P], tkb[s][:TS, 8, :], reads=[b_tkb[s]], sbuf=b_tkb[s])
                        else:
                            ker.dma(sp, o_gv[:, col0:col0 + P], tko[s][:TS, 8, :], reads=[b_tko[s]], sbuf=b_tko[s])
                            ker.dma(sp, va_d[0:TP, col0:col0 + P].rearrange("(tt p) n -> p tt n", p=P), tkb[s][:, 0:8, :],
                                    reads=[b_tkb[s]], sbuf=b_tkb[s])
                            ker.dma(sp, va_d[TP:T, col0:col0 + P], tkb[s][:TS, 8, :], reads=[b_tkb[s]], sbuf=b_tkb[s])

                    ab_w_in = ab_w_in_l()
                    blocks, jobs = [], []
                    kinds = ["u"] * 16 + ["va"] * 16 + ["q"] * 16 + ["k"] * 16 + ["v"] * 16
                    for b in range(40):
                        blocks.append((ab_w_in[:, b * 256:(b + 1) * 256], KT, 256))
                        for j in range(2):
                            oi = b * 2 + j
                            jobs.append((b, j * 128, 128, (kinds[oi], oi % 16)))
                    blocks.append((ab_w_in[:, 10128:10256], KT, 128))
                    jobs.append((40, 112, 16, ("f", 0)))
                    if os.environ.get("KG1"):
                        keep = os.environ["KG1"].split(",")
                        jobs = [j for j in jobs if j[3][0] in keep and j[3][1] < int(os.environ.get("KG1N", "16"))]
                    gemm(lambda tag, kt: (A[:, kt, :], b_A), blocks, jobs, CH, epi)
                    ker.barrier()

        coll_sems = [ker.new_sem(f"cc{i}") for i in range(14)]
        coll_i = [0]

        def all_gather(src, dst, reads, writes, pwrites=()):
            k = coll_sems[coll_i[0]]
            coll_i[0] += 1
            ker._deps(pool, reads, writes, pwrites)
            ins = nc.gpsimd.collective_compute("AllGather", ALU.bypass, replica_groups=GROUPS,
                                               ins=[src.opt()], outs=[dst.opt()])
            ins.then_inc(ker.sems[k], 1)
            ker.issued[k] += 1
            ker._record((k, 1), reads, writes, pwrites)

        b_gath = {n: ker.buf(n) for n in ("kT_all", "v_all", "c_all", "sum_all")}
        if STAGE >= 2 and 'l0' not in SKIP:
          with ExitStack() as pB:
            aoT = sb(pB, "aoT", [P, KT, T], BF16)
            b_aoT = ker.buf("aoT")
            for j in range(4):
                all_gather(kT_in[j], kT_all[j], [], [b_gath["kT_all"]] if j == 0 else [], [] if j == 0 else [b_gath["kT_all"]])
                all_gather(v_in[j], v_all[j], [], [b_gath["v_all"]] if j == 0 else [], [] if j == 0 else [b_gath["v_all"]])
            with ExitStack() as ps:
                wsT = sb(ps, "wsT", [P, 8, P], BF16)
                b_wsT = ker.buf("wsT")
                wraw = sb(ps, "wraw", [P, 8, P])
                b_wraw = ker.buf("wraw", dma=True)
                bbt = sb(ps, "bbt", [P, 8, P])
                b_bbt = ker.buf("bbt", dma=True)
                vat = [sb(ps, f"vat{i}", [P, 2048], BF16) for i in range(2)]
                b_vat = [ker.buf(f"vat{i}", dma=True) for i in range(2)]
                ut = [sb(ps, f"ut{i}", [P, 16, P], BF16) for i in range(2)]
                b_ut = [ker.buf(f"ut{i}", dma=True) for i in range(2)]
                gtmp = [sb(ps, f"gtmp{i}", [P, P]) for i in range(2)]
                b_gtmp = [ker.buf(f"gtmp{i}") for i in range(2)]
                ker.dma(sp, wraw[:], gmlp_ws.rearrange("g r s -> r g s"), writes=[b_wraw], sbuf=b_wraw)
                ker.dma(sp, bbt[:], gmlp_bb, writes=[b_bbt], sbuf=b_bbt)
                for g in range(8):
                    ker.op(dve, lambda: nc.vector.tensor_tensor(out=wraw[:, g, :], in0=wraw[:, g, :], in1=cst[:, 1, :], op=ALU.mult),
                           reads=[b_wraw, b_cst], writes=[b_wraw])
                for gg in range(2):
                    srcs = [((wraw[:, gg * 4 + j, :], [b_wraw]), P, P) for j in range(4)]

                    def wr(bk, bb, gg=gg):
                        evac(wsT[:, gg * 4:gg * 4 + 4, :], bk[:, :].rearrange("p (j n) -> p j n", n=P), [bb], [b_wsT])
                    transpose_group(srcs, wr)
                uT_v = uT_d.rearrange("f p t -> p f t")
                for ti, (t0, tn) in enumerate(TT):
                    s2 = ti % 2
                    ker.dma(sp, vat[s2][:tn, :], va_d[t0:t0 + tn, :], writes=[b_vat[s2]], sbuf=b_vat[s2])
                    ker.dma(sp, ut[s2][:, :, :tn], uT_v[:, :, t0:t0 + tn], writes=[b_ut[s2]], sbuf=b_ut[s2])
                    for ft in range(16):
                        g = ft // 2
                        bk, bb = next_acc()
                        ker.op(pe, lambda: nc.tensor.matmul(bk[:, :tn], vat[s2][:tn, ft * P:(ft + 1) * P], wsT[:tn, g, :tn], start=True, stop=True),
                               reads=[b_vat[s2], b_wsT], writes=[bb])
                        s3 = ft % 2
                        ker.op(dve, lambda: nc.vector.tensor_tensor(out=gtmp[s3][:, :tn], in0=bk[:, :tn], in1=bbt[:, g, :tn], op=ALU.add),
                               reads=[bb, b_bbt], writes=[b_gtmp[s3]])
                        ker.op(dve, lambda: nc.vector.tensor_tensor(out=aoT[:, ft, t0:t0 + tn], in0=gtmp[s3][:, :tn], in1=ut[s2][:, ft, :tn], op=ALU.mult),
                               reads=[b_gtmp[s3], b_ut[s2]], pwrites=[b_aoT])
                ker.barrier()
            if STAGE >= 3:
             with ExitStack() as psT:
              biasT = sb(psT, "biasT", [P, 2, 32, 16])
              biasS = sb(psT, "biasS", [P, 9, 16])
              qpb = sb(psT, "qpb", [P, T])
              kps = sb(psT, "kps", [P, 40])
              ps = ExitStack()
              ps.__enter__()
              if True:
                one16 = sb(ps, "one16", [16, T])
                b_one16 = ker.buf("one16")
                cs = sb(ps, "cs", [16, TP])
                b_cs = ker.buf("cs", dma=True)
                lfs = sb(ps, "lfs", [16, T])
                b_lfs = ker.buf("lfs")
                css = sb(ps, "css", [16, T])
                b_css = ker.buf("css")
                clf = sb(ps, "clf", [P, 8, 16])
                b_clf = ker.buf("clf", dma=True)
                cg = sb(ps, "cg", [16, 4, TP])
                b_cg = ker.buf("cg", dma=True)
                offs = sb(ps, "offs", [16, 4])
                b_offs = ker.buf("offs")
                cql = sb(ps, "cql", [16, TP])
                b_cql = ker.buf("cql")
                pmk = sb(ps, "pmk", [P, 8])
                b_pmk = ker.buf("pmk", dma=True)
                ckT = sb(ps, "ckT", [P, 33, 16])
                b_ckT = ker.buf("ckT")
                ckTs = sb(ps, "ckTs", [P, 9, 16])
                b_ckTs = ker.buf("ckTs")
                cqe = sb(ps, "cqe", [P, 3, 16])
                b_cqe = ker.buf("cqe")
                cref = sb(ps, "cref", [P, 3, 16])
                b_cref = ker.buf("cref")
                b_biasT = ker.buf("biasT")
                b_biasS = ker.buf("biasS")
                b_qpb = ker.buf("qpb", dma=True)
                b_kps = ker.buf("kps", dma=True)
                ker.dma(sp, pmk[:], pmask_b, writes=[b_pmk], sbuf=b_pmk)
                ker.dma(sp, qpb[:], qpos_b, writes=[b_qpb], sbuf=b_qpb)
                ker.dma(sp, kps[:], kpos, writes=[b_kps], sbuf=b_kps)
                ker.dma(sp, clf[:], cache_lf.rearrange("(kt p) h -> p kt h", p=P), writes=[b_clf], sbuf=b_clf)
                ker.op(dve, lambda: nc.vector.memset(one16[:], 1.0), writes=[b_one16])
                ker.op(dve, lambda: nc.vector.tensor_tensor_scan(out=cs[:], data0=one16[:, 0:TP], data1=lfT[:, 0:TP], initial=0.0,
                                                               op0=ALU.mult, op1=ALU.add),
                       reads=[b_one16, b_lfT], writes=[b_cs])
                b_c_in = ker.buf("c_in")
                ker.dma(sp, c_in, cs[:], reads=[b_cs], writes=[b_c_in], sbuf=b_cs)
                all_gather(c_in, c_all, [b_c_in], [b_gath["c_all"]])
                bk2, bb2 = tr_bank()
                for kt in range(8):
                    ker.op(pe, lambda: nc.tensor.transpose(bk2[:16, kt * P:(kt + 1) * P] if kt < 4 else bk2[:16, (kt - 4) * P:(kt - 3) * P],
                                                           clf[:, kt, :], ident[:, :]),
                           reads=[b_clf, b_cst], writes=[bb2] if kt % 4 == 0 else (), pwrites=() if kt % 4 == 0 else [bb2])
                    if kt % 4 == 3:
                        evac(lfs[:, (kt - 3) * P:(kt + 1) * P], bk2[:16, :], [bb2], [b_lfs])
                        if kt == 3:
                            bk2, bb2 = tr_bank()
                ker.op(dve, lambda: nc.vector.tensor_copy(out=lfs[:, TP:T], in_=lfT[:, TP:T]), reads=[b_lfT], pwrites=[b_lfs])
                ker.op(dve, lambda: nc.vector.tensor_tensor_scan(out=css[:], data0=one16[:], data1=lfs[:], initial=0.0,
                                                               op0=ALU.mult, op1=ALU.add),
                       reads=[b_one16, b_lfs], writes=[b_css])
                ker.dma(sp, cg[:], c_all.rearrange("(r h) t -> h r t", h=16), reads=[b_gath["c_all"]], writes=[b_cg], sbuf=b_cg)
                ker.op(dve, lambda: nc.vector.tensor_copy(out=offs[:, 1:2], in_=cg[:, 0, TP - 1:TP]), reads=[b_cg], writes=[b_offs])
                for r in (2, 3):
                    ker.op(dve, lambda: nc.vector.tensor_tensor(out=offs[:, r:r + 1], in0=offs[:, r - 1:r], in1=cg[:, r - 1, TP - 1:TP], op=ALU.add),
                           reads=[b_cg, b_offs], writes=[b_offs])
                for r in (1, 2, 3):
                    ker.op(dve, lambda: nc.vector.tensor_scalar(out=cg[:, r, :], in0=cg[:, r, :], scalar1=offs[:, r:r + 1], scalar2=None, op0=ALU.add),
                           reads=[b_cg, b_offs], writes=[b_cg])
                ker.op(dve, lambda: nc.vector.tensor_scalar(out=cql[:], in0=cg[:, 0, :], scalar1=pmk[:16, 4:5], scalar2=None, op0=ALU.mult),
                       reads=[b_cg, b_pmk], writes=[b_cql])
                for r in (1, 2, 3):
                    ker.op(dve, lambda: nc.vector.scalar_tensor_tensor(out=cql[:], in0=cg[:, r, :], scalar=pmk[:16, 4 + r:5 + r], in1=cql[:],
                                                                      op0=ALU.mult, op1=ALU.add),
                           reads=[b_cg, b_pmk, b_cql], writes=[b_cql])
                bk2, bb2 = tr_bank()
                for kt in range(32):
                    ker.op(pe, lambda: nc.tensor.transpose(bk2[:, kt * 16:(kt + 1) * 16], cg[:, kt // 8, (kt % 8) * P:(kt % 8 + 1) * P], ident[:16, :16]),
                           reads=[b_cg, b_cst], writes=[bb2] if kt == 0 else (), pwrites=() if kt == 0 else [bb2])
                evac(ckT[:, 0:32, :], bk2[:, :].rearrange("p (j n) -> p j n", n=16), [bb2], [b_ckT])
                bk2, bb2 = tr_bank()
                ker.op(dve, lambda: nc.vector.memset(ckTs[:], 0.0), writes=[b_ckTs])
                ker.op(dve, lambda: nc.vector.memset(cqe[:], 0.0), writes=[b_cqe])
                for kt in range(9):
                    tn = 128 if kt < 8 else 16
                    ker.op(pe, lambda: nc.tensor.transpose(bk2[:tn, kt * 16:(kt + 1) * 16], css[:, kt * P:kt * P + tn], ident[:16, :16]),
                           reads=[b_css, b_cst], writes=[bb2] if kt == 0 else (), pwrites=() if kt == 0 else [bb2])
                for j, tl in enumerate((3, 7)):
                    ker.op(pe, lambda: nc.tensor.transpose(bk2[:, (9 + j) * 16:(10 + j) * 16], cql[:, tl * P:(tl + 1) * P], ident[:16, :16]),
                           reads=[b_cql, b_cst], pwrites=[bb2])
                evac(ckTs[:, 0:8, :], bk2[:, 0:128].rearrange("p (j n) -> p j n", n=16), [bb2], [b_ckTs])
                evac(ckTs[:16, 8, :], bk2[:16, 128:144], [bb2], [b_ckTs])
                evac(cqe[:, 0:2, :], bk2[:, 144:176].rearrange("p (j n) -> p j n", n=16), [bb2], [b_cqe])
                ker.op(dve, lambda: nc.vector.tensor_copy(out=cqe[:16, 2, :], in_=ckTs[:16, 8, :]), reads=[b_ckTs], pwrites=[b_cqe])
                bk3, bb3 = next_acc()
                for j in range(3):
                    selm = cst[:, 3, :] if j < 2 else cst[:, 4, :]
                    ker.op(pe, lambda: nc.tensor.matmul(bk3[:, j * 16:(j + 1) * 16], selm, cqe[:, j, :], start=True, stop=True),
                           reads=[b_cqe, b_cst], writes=[bb3] if j == 0 else (), pwrites=() if j == 0 else [bb3])
                evac(cref[:, :, :], bk3[:, 0:48].rearrange("p (j n) -> p j n", n=16), [bb3], [b_cref])
                for qb in range(2):
                    for kt in range(32):
                        ker.op(dve, lambda: nc.vector.tensor_tensor(out=biasT[:, qb, kt, :], in0=cref[:, qb, :], in1=ckT[:, kt, :], op=ALU.subtract),
                               reads=[b_cref, b_ckT], pwrites=[b_biasT])
                ker.op(dve, lambda: nc.vector.tensor_scalar(out=biasT[:], in0=biasT[:], scalar1=0.0, scalar2=None, op0=ALU.min),
                       reads=[b_biasT], writes=[b_biasT])
                for kt in range(9):
                    ker.op(dve, lambda: nc.vector.tensor_tensor(out=biasS[:, kt, :], in0=cref[:, 2, :], in1=ckTs[:, kt, :], op=ALU.subtract),
                           reads=[b_cref, b_ckTs], pwrites=[b_biasS])
                ker.op(dve, lambda: nc.vector.tensor_scalar(out=biasS[:], in0=biasS[:], scalar1=0.0, scalar2=None, op0=ALU.min),
                       reads=[b_biasS], writes=[b_biasS])

                ker.barrier()
                ps.__exit__(None, None, None)
                ps = psT
                SCALE = 128.0 ** -0.5
                pf = [sb(ps, f"pf{i}", [P, 512]) for i in range(2)]
                b_pf = [ker.buf(f"pf{i}") for i in range(2)]
                pm = [sb(ps, f"pm{i}", [P, 512], BF16) for i in range(2)]
                b_pm = [ker.buf(f"pm{i}") for i in range(2)]
                rl = sb(ps, "rl", [P, 512])
                b_rl = ker.buf("rl")

                def attention(qh_ap, b_qh, q0, nq, keys, qp0, out_ap, olb):
                    bO, bbO, bL, bbL = olb
                    nk = len(keys)
                    pend = None

                    def tail(it):
                        kt, i = it
                        _, v_ap, bufs, _, _ = keys[kt]
                        ker.op(pe, lambda: nc.tensor.matmul(bO[:, :nq], v_ap, pm[i][:, :nq], start=(kt == 0), stop=(kt == nk - 1)),
                               reads=[b_pm[i]] + bufs, writes=[bbO], signal=(kt == nk - 1))
                        ker.op(pe, lambda: nc.tensor.matmul(bL[:, :nq], ones_b[:, :], pm[i][:, :nq], start=(kt == 0), stop=(kt == nk - 1)),
                               reads=[b_pm[i], b_ones], writes=[bbL], signal=True)
                    for kt in range(nk):
                        kT_ap, v_ap, bufs, kp_ap, bias_ap = keys[kt]
                        i = kt % 2
                        bS, bbS = banks[i], bbank[i]
                        ker.op(pe, lambda: nc.tensor.matmul(bS[:, :nq], kT_ap, qh_ap[:, q0:q0 + nq], start=True, stop=True),
                               reads=[b_qh] + bufs, writes=[bbS])
                        if pend is not None:
                            tail(pend)
                        ker.op(act, lambda: nc.scalar.activation(out=pf[i][:, :nq], in_=bS[:, :nq], func=AF.Exp, scale=SCALE, bias=bias_ap),
                               reads=[bbS, b_biasT, b_biasS], writes=[b_pf[i]])
                        ker.op(dve, lambda: nc.vector.scalar_tensor_tensor(out=pm[i][:, :nq], in0=qpb[:, qp0:qp0 + nq], scalar=kp_ap, in1=pf[i][:, :nq],
                                                                          op0=ALU.is_ge, op1=ALU.mult),
                               reads=[b_pf[i], b_qpb, b_kps], writes=[b_pm[i]])
                        pend = (kt, i)
                    tail(pend)
                    ker.op(dve, lambda: nc.vector.reciprocal(out=rl[:, :nq], in_=bL[:, :nq]), reads=[bbL], writes=[b_rl])
                    ker.op(dve, lambda: nc.vector.tensor_tensor(out=out_ap, in0=bO[:, :nq], in1=rl[:, :nq], op=ALU.mult),
                           reads=[bbO, b_rl], pwrites=[b_aoT])

                olbs = [(banks[2], bbank[2], banks[3], bbank[3]), (banks[4], bbank[4], banks[5], bbank[5])]
                olb_i = [0]
                ps = ExitStack()
                ps.__enter__()
                kTh = [sb(ps, f"kTh{i}", [P, 4, TP], BF16) for i in range(2)]
                b_kTh = [ker.buf(f"kTh{i}", dma=True) for i in range(2)]
                vh = [sb(ps, f"vh{i}", [P, 32, P], BF16) for i in range(2)]
                b_vh = [ker.buf(f"vh{i}", dma=True) for i in range(2)]
                qh = [sb(ps, f"qh{i}", [P, T], BF16) for i in range(2)]
                b_qh = [ker.buf(f"qh{i}", dma=True) for i in range(2)]
                kT_all_v = [t_.rearrange("(r h d) t -> h d r t", h=4, d=P) for t_ in kT_all]
                v_all_v = [t_.rearrange("(kt p) n -> p kt n", p=P) for t_ in v_all]

                def load_head(h):
                    s2 = h % 2
                    ker.dma(sp, kTh[s2][:], kT_all_v[h // 4][h % 4], reads=[b_gath["kT_all"]], writes=[b_kTh[s2]], sbuf=b_kTh[s2])
                    ker.dma(sp, vh[s2][:], v_all_v[h // 4][:, :, (h % 4) * P:(h % 4 + 1) * P], reads=[b_gath["v_all"]], writes=[b_vh[s2]], sbuf=b_vh[s2])
                    ker.dma(sp, qh[s2][:], qT_d[h], writes=[b_qh[s2]], sbuf=b_qh[s2])
                load_head(0)
                for h in range(16):
                    s2 = h % 2
                    if h + 1 < 16:
                        load_head(h + 1)
                    for qb in range(2):
                        keys = [(kTh[s2][:, kt // 8, (kt % 8) * P:(kt % 8 + 1) * P], vh[s2][:, kt, :], [b_kTh[s2], b_vh[s2]],
                                 kps[:, kt:kt + 1], biasT[:, qb, kt, h:h + 1]) for kt in range(32)]
                        attention(qh[s2], b_qh[s2], qb * 512, 512, keys, qb * 512, aoT[:, 16 + h, qb * 512:(qb + 1) * 512], olbs[olb_i[0] % 2])
                        olb_i[0] += 1
                ker.barrier()
                ps.__exit__(None, None, None)
                psS = ExitStack()
                psS.__enter__()
                kTs_all = sb(psS, "kTs_all", [P, 16, 1152], BF16)
                b_kTs = ker.buf("kTs_all", dma=True)
                vS_all = sb(psS, "vS_all", [P, 9, 2048], BF16)
                b_vS = ker.buf("vS_all", dma="sw")
                ckin = [sb(psS, f"ckin{i}", [P, 2048]) for i in range(2)]
                b_ckin = [ker.buf(f"ckin{i}", dma=True) for i in range(2)]
                qs = sb(psS, "qs", [P, 16, TS], BF16)
                b_qs = ker.buf("qs", dma=True)
                ker.op(dve, lambda: nc.vector.memset(kTs_all[:, :, TP:1152], 0.0), writes=[b_kTs])
                ker.op(dve, lambda: nc.vector.memset(vS_all[:, 8, :], 0.0), writes=[b_vS])
                ker.dma(sp, kTs_all[:, :, TP:T], kTs_d.rearrange("h d t -> d h t"), pwrites=[b_kTs], sbuf=b_kTs)
                ker.dma(pool, vS_all[:, 0:8, :], cache_v.rearrange("(kt p) n -> p kt n", p=P), pwrites=[b_vS], sbuf=b_vS)
                ker.dma(sp, vS_all[:TS, 8, :], vs_d, pwrites=[b_vS], sbuf=b_qs)
                ker.dma(sp, qs[:], qT_d.rearrange("h d t -> d h t")[:, :, TP:T], writes=[b_qs], sbuf=b_qs)
                for kt in range(8):
                    s2 = kt % 2
                    ker.dma(sp, ckin[s2][:], cache_k[kt * P:(kt + 1) * P, :], writes=[b_ckin[s2]], sbuf=b_ckin[s2])
                    for g4 in range(4):
                        srcs = [((ckin[s2][:, (g4 * 4 + j) * P:(g4 * 4 + j + 1) * P], [b_ckin[s2]]), P, P) for j in range(4)]

                        def wr(bk, bb, g4=g4, kt=kt):
                            evac(kTs_all[:, g4 * 4:g4 * 4 + 4, kt * P:(kt + 1) * P], bk[:, :].rearrange("p (j n) -> p j n", n=P), [bb], [b_kTs])
                        transpose_group(srcs, wr)
                for h in range(16):
                    keys = [(kTs_all[:, h, kt * P:(kt + 1) * P], vS_all[:, kt, h * P:(h + 1) * P], [b_kTs, b_vS],
                             kps[:, kt:kt + 1] if kt < 8 else kps[:, 32:33], biasS[:, kt, h:h + 1]) for kt in range(9)]
                    attention(qs[:, h, :], b_qs, 0, TS, keys, TP, aoT[:, 16 + h, TP:T], olbs[olb_i[0] % 2])
                    olb_i[0] += 1
                ker.barrier()
                psS.__exit__(None, None, None)
                if os.environ.get("KDEBUG"):
                    b_dbg = ker.buf("dbg", dma=True)
                    ker.dma(sp, dt_tmp("dbg_ao", [KT, P, T], BF16).rearrange("k p t -> p k t"), aoT[:], reads=[b_aoT], sbuf=b_dbg)
                    ker.barrier()

            if STAGE >= 4:
              with ExitStack() as ps:
                alloc_wslots(ps)
                ost = [sb(ps, f"ost{i}", [P, T]) for i in range(2)]
                b_ost = [ker.buf(f"ost{i}", dma="sw") for i in range(2)]
                ab_w_out = ab_w_out_l()
                blocks = [(ab_w_out[:, b * 256:(b + 1) * 256], KT, 256) for b in range(16)]
                jobs = [(b, j * P, P, ("o", b * 2 + j)) for b in range(16) for j in range(2)]
                accum_epilogue = make_accum_epi(ost, b_ost)
                gemm(lambda tag, kt: (aoT[:, kt, :], b_aoT), blocks, jobs, CH, accum_epilogue)
                ker.barrier()

        def ffn(layer):
            with ExitStack() as pA:
                A = sb(pA, "Af", [P, KT, T], BF16)
                b_A = ker.buf("Af")
                with ExitStack() as ps:
                    rmsnorm_to(ps, A, b_A, 1 + 2 * layer)
                    ker.barrier()
                with ExitStack() as ps:
                    alloc_wslots(ps)
                    hid = sb(ps, "hid", [P, 8, T], BF16)
                    b_hid = [ker.buf(f"hid{j}") for j in range(8)]
                    sg = [sb(ps, f"sg{i}", [P, T]) for i in range(2)]
                    b_sg = [ker.buf(f"sg{i}") for i in range(2)]
                    ost = [sb(ps, f"ost{i}", [P, T]) for i in range(2)]
                    b_ost = [ker.buf(f"ost{i}", dma="sw") for i in range(2)]
                    acc_epi = make_accum_epi(ost, b_ost)
                    wg, wu, wd = ffn_wg[layer], ffn_wu[layer], ffn_wd[layer]
                    NJ = D_FF // P
                    blocks, jobs = [], []
                    for j0 in range(0, NJ, 8):
                        nj = min(8, NJ - j0)
                        for jj in range(0, nj, 2):
                            c0 = (j0 + jj) * P
                            nt = min(2, nj - jj)
                            bg = len(blocks)
                            blocks.append((wg[:, c0:c0 + nt * P], KT, nt * P))
                            blocks.append((wu[:, c0:c0 + nt * P], KT, nt * P))
                            for t2 in range(nt):
                                jobs.append((bg, t2 * P, P, ("g", jj + t2)))
                                jobs.append((bg + 1, t2 * P, P, ("u", jj + t2)))
                        for b4 in range(4):
                            bd = len(blocks)
                            blocks.append((wd[j0 * P:(j0 + nj) * P, b4 * 1024:(b4 + 1) * 1024], nj, 1024))
                            for o8 in range(8):
                                jobs.append((bd, o8 * P, P, ("d", b4 * 8 + o8)))
                    cur = {"i": 0}

                    def A_of(tag, kt):
                        if tag[0] == "d":
                            return hid[:, kt, :], b_hid[kt]
                        return A[:, kt, :], b_A

                    def epi(tag, ci, c0, cn, m, bk, bb):
                        kind, idx = tag
                        if kind == "d":
                            return acc_epi(tag, ci, c0, cn, m, bk, bb)
                        if kind == "g":
                            if ci == 0:
                                cur["i"] += 1
                            s = cur["i"] % 2
                            ker.op(act, lambda: nc.scalar.activation(out=sg[s][:, c0:c0 + cn], in_=bk[:, :cn], func=AF.Silu),
                                   reads=[bb], pwrites=[b_sg[s]] if ci > 0 else (), writes=[b_sg[s]] if ci == 0 else ())
                        else:
                            s = cur["i"] % 2
                            ker.op(dve, lambda: nc.vector.tensor_tensor(out=hid[:, idx, c0:c0 + cn], in0=bk[:, :cn], in1=sg[s][:, c0:c0 + cn], op=ALU.mult),
                                   reads=[bb, b_sg[s]], pwrites=[b_hid[idx]])
                    gemm(A_of, blocks, jobs, CH, epi)
                    ker.barrier()

        if STAGE >= 5 and 'ffn0' not in SKIP:
            ffn(0)

        def layer1():
            mq_d = dt_tmp("mq_d", [16, P, T], BF16)
            mkT_d = dt_tmp("mkT_d", [16, P, T], BF16)
            mk_d = dt_tmp("mk_d", [T, 2048], BF16)
            mv_d = dt_tmp("mv_d", [T, 4096], BF16)
            mo_d = dt_tmp("mo_d", [32, P, T])
            sumC_in = [dt_tmp(f"sumC_in{j}", [2 * DK, DV]) for j in range(4)]
            sumC_all = [dt_tmp(f"sumC_all{j}", [4 * 2 * DK, DV]) for j in range(4)]
            sumS_in = dt_tmp("sumS_in", [5, 512])
            sumS_all = dt_tmp("sumS_all", [4 * 5, 512])
            c_w_in = c_w_in_l()
            c_w_out = c_w_out_l()
            esel_d = dt_in("esel", [8, 8 * P])
            st_m_c = dt_in("st_m_c", [8, 1])
            pL = ExitStack()
            pL.__enter__()
            igT = sb(pL, "igT", [8, T])
            lfc = sb(pL, "lfc", [8, T])
            b_igT = ker.buf("igT")
            b_lfc = ker.buf("lfc")
            with ExitStack() as pA:
                A = sb(pA, "A1", [P, KT, T], BF16)
                b_A = ker.buf("A1")
                with ExitStack() as ps:
                    rmsnorm_to(ps, A, b_A, 2)
                    ker.barrier()
                with ExitStack() as ps:
                    alloc_wslots(ps)
                    stg = [sb(ps, f"stg{i}", [P, T]) for i in range(2)]
                    b_stg = [ker.buf(f"stg{i}", dma=True) for i in range(2)]
                    stb = [sb(ps, f"stb{i}", [P, T], BF16) for i in range(2)]
                    b_stb = [ker.buf(f"stb{i}", dma=True) for i in range(2)]
                    tkb = [sb(ps, f"tkb{i}", [P, 9, P], BF16) for i in range(2)]
                    b_tkb = [ker.buf(f"tkb{i}", dma=True) for i in range(2)]
                    bi_t = sb(ps, "bi_t", [8, 1])
                    nbf_t = sb(ps, "nbf_t", [8, 1])
                    b_bif = ker.buf("bif", dma=True)
                    ker.dma(sp, bi_t[:], c_bif[0:8, :], pwrites=[b_bif], sbuf=b_bif)
                    ker.dma(sp, nbf_t[:], c_bif[8:16, :], pwrites=[b_bif], sbuf=b_bif)
                    ker.op(dve, lambda: nc.vector.tensor_scalar(out=nbf_t[:], in0=nbf_t[:], scalar1=-1.0, scalar2=None, op0=ALU.mult),
                           reads=[b_bif], writes=[b_bif])
                    ctr = [0]
                    cur = {}

                    def epi(tag, ci, c0, cn, m, bk, bb):
                        kind, idx = tag
                        if ci == 0:
                            cur["s"] = ctr[0] % 2
                            ctr[0] += 1
                        s = cur["s"]
                        if kind == "q":
                            evac(stb[s][:, c0:c0 + cn], bk[:, :cn], [bb], [b_stb[s]])
                            if ci == 2:
                                ker.dma(sp, mq_d[idx], stb[s][:], reads=[b_stb[s]], sbuf=b_stb[s])
                            return
                        if kind == "o":
                            ker.op(act, lambda: nc.scalar.activation(out=stg[s][:, c0:c0 + cn], in_=bk[:, :cn], func=AF.Sigmoid),
                                   reads=[bb], pwrites=[b_stg[s]])
                            if ci == 2:
                                ker.dma(sp, mo_d[idx], stg[s][:], reads=[b_stg[s]], sbuf=b_stg[s])
                            return
                        if kind == "i":
                            ker.op(act, lambda: nc.scalar.activation(out=igT[:, c0:c0 + cn], in_=bk[:8, :cn], func=AF.Identity,
                                                                     scale=1.0, bias=bi_t[:, 0:1]),
                                   reads=[bb, b_bif], pwrites=[b_igT])
                            return
                        if kind == "f":
                            ker.op(act, lambda: nc.scalar.activation(out=lfc[:, c0:c0 + cn], in_=bk[:8, :cn], func=AF.Exp, scale=-1.0, bias=nbf_t[:, 0:1]),
                                   reads=[bb, b_bif], pwrites=[b_lfc])
                            ker.op(act, lambda: nc.scalar.activation(out=lfc[:, c0:c0 + cn], in_=lfc[:, c0:c0 + cn], func=AF.Ln, scale=1.0, bias=ones_f[:8, 0:1]),
                                   reads=[b_lfc, b_ones], pwrites=[b_lfc])
                            ker.op(dve, lambda: nc.vector.tensor_scalar(out=lfc[:, c0:c0 + cn], in0=lfc[:, c0:c0 + cn], scalar1=-1.0, scalar2=None, op0=ALU.mult),
                                   reads=[b_lfc], pwrites=[b_lfc])
                            return
                        if kind == "k":
                            ker.op(act, lambda: nc.scalar.mul(out=stg[s][:, c0:c0 + cn], in_=bk[:, :cn], mul=DK ** -0.5), reads=[bb], pwrites=[b_stg[s]])
                            ker.op(dve, lambda: nc.vector.tensor_copy(out=stb[s][:, c0:c0 + cn], in_=stg[s][:, c0:c0 + cn]),
                                   reads=[b_stg[s]], pwrites=[b_stb[s]])
                        else:
                            evac(stg[s][:, c0:c0 + cn], bk[:, :cn], [bb], [b_stg[s]])
                        if ci != 2:
                            return
                        if kind == "k":
                            ker.dma(sp, mkT_d[idx], stb[s][:], reads=[b_stb[s]], sbuf=b_stb[s])
                        for g0, g1 in ((0, 4), (4, 8), (8, 9)):
                            srcs = [((stg[s][:, TT[ti][0]:TT[ti][0] + TT[ti][1]], [b_stg[s]]), TT[ti][1], P) for ti in range(g0, g1)]

                            def wr(bk2, bb2, g0=g0, g1=g1, s=s):
                                if g0 < 8:
                                    evac(tkb[s][:, g0:g1, :], bk2[:, :].rearrange("p (j n) -> p j n", n=P), [bb2], [b_tkb[s]])
                                else:
                                    evac(tkb[s][:TS, 8, :], bk2[:TS, 0:P], [bb2], [b_tkb[s]])
                            transpose_group(srcs, wr)
                        dstd = mk_d if kind == "k" else mv_d
                        col0 = idx * P
                        ker.dma(sp, dstd[0:TP, col0:col0 + P].rearrange("(tt p) n -> p tt n", p=P), tkb[s][:, 0:8, :],
                                reads=[b_tkb[s]], sbuf=b_tkb[s])
                        ker.dma(sp, dstd[TP:T, col0:col0 + P], tkb[s][:TS, 8, :], reads=[b_tkb[s]], sbuf=b_tkb[s])

                    blocks, jobs = [], []
                    kinds = ["q"] * 16 + ["k"] * 16 + ["v"] * 32 + ["o"] * 32
                    base = {"q": 0, "k": 16, "v": 32, "o": 64}
                    for b in range(48):
                        blocks.append((c_w_in[:, b * 256:(b + 1) * 256], KT, 256))
                        for j in range(2):
                            oi = b * 2 + j
                            jobs.append((b, j * P, P, (kinds[oi], oi - base[kinds[oi]])))
                    blocks.append((c_w_in[:, 12176:12304], KT, 128))
                    jobs.append((48, 112, 8, ("i", 0)))
                    jobs.append((48, 120, 8, ("f", 0)))
                    gemm(lambda tag, kt: (A[:, kt, :], b_A), blocks, jobs, CH, epi)
                    ker.barrier()
            if STAGE < 7:
                pL.__exit__(None, None, None)
                return
            pC = ExitStack()
            pC.__enter__()
            moutT = sb(pC, "moutT", [P, KT, T], BF16)
            b_mout = ker.buf("moutT")
            negM = sb(pC, "negM", [8, T])
            negm = sb(pC, "negm", [8, T])
            wint = sb(pC, "wint", [8, T])
            b_rows = ker.buf("rows")
            a_tm = sb(pC, "a_tm", [P, 9, 8])
            wS_tm = sb(pC, "wS_tm", [P, 8, 8])
            b_tm = ker.buf("tm")
            esel = sb(pC, "esel", [8, 8, P])
            b_esel = ker.buf("esel", dma=True)
            hn = sb(pC, "hn", [P, 32])
            b_hn = ker.buf("hn", dma=True)
            pmk = sb(pC, "pmk1", [P, 8])
            b_pmk = ker.buf("pmk1", dma=True)
            stm = sb(pC, "stm", [P, 8])
            b_stm = ker.buf("stm", dma=True)
            msc = sb(pC, "msc", [8, 8])
            b_msc = ker.buf("msc", dma=True)
            minb = sb(pC, "minb", [P, 8])
            wrr = sb(pC, "wrr", [P, 4, 8])
            b_comb = ker.buf("comb")
            ker.dma(sp, esel[:], esel_d.rearrange("a (h m) -> a h m", m=P), writes=[b_esel], sbuf=b_esel)
            ker.dma(sp, hn[:], c_hn, writes=[b_hn], sbuf=b_hn)
            ker.dma(sp, pmk[:], pmask_b, writes=[b_pmk], sbuf=b_pmk)
            ker.dma(sp, stm[:], st_m_b, writes=[b_stm], sbuf=b_stm)
            ker.dma(sp, msc[:, 4:5], st_m_c, writes=[b_msc], sbuf=b_msc)
            pR = ExitStack()
            pR.__enter__()
            Bc = sb(pR, "Bc", [8, T])
            M0 = sb(pR, "M0", [8, T])
            b_BM = ker.buf("BM")
            with ExitStack() as ps:
                one8 = sb(ps, "one8", [8, T])
                av = sb(ps, "av", [8, T])
                wS = sb(ps, "wS", [8, TP])
                b_t = ker.buf("gtmp")
                ker.op(dve, lambda: nc.vector.memset(one8[:], 1.0), writes=[b_t])
                for (c0, cn) in ((0, TP), (TP, TS)):
                    ker.op(dve, lambda: nc.vector.tensor_tensor_scan(out=Bc[:, c0:c0 + cn], data0=one8[:, c0:c0 + cn], data1=lfc[:, c0:c0 + cn],
                                                                   initial=0.0, op0=ALU.mult, op1=ALU.add),
                           reads=[b_t, b_lfc], writes=[b_BM])
                ker.op(dve, lambda: nc.vector.tensor_tensor(out=av[:], in0=igT[:], in1=Bc[:], op=ALU.subtract), reads=[b_igT, b_BM], writes=[b_t])
                for (c0, cn) in ((0, TP), (TP, TS)):
                    ker.op(dve, lambda: nc.vector.tensor_tensor_scan(out=M0[:, c0:c0 + cn], data0=one8[:, c0:c0 + cn], data1=av[:, c0:c0 + cn],
                                                                   initial=-1e30, op0=ALU.mult, op1=ALU.max),
                           reads=[b_t], writes=[b_BM])
                ker.op(dve, lambda: nc.vector.tensor_copy(out=msc[:, 0:1], in_=Bc[:, TP - 1:TP]), reads=[b_BM], writes=[b_msc])
                ker.op(dve, lambda: nc.vector.tensor_copy(out=msc[:, 1:2], in_=M0[:, TP - 1:TP]), reads=[b_BM], writes=[b_msc])
                ker.op(dve, lambda: nc.vector.tensor_scalar(out=msc[:, 2:3], in0=M0[:, TP - 1:TP], scalar1=-1.0, scalar2=None, op0=ALU.mult),
                       reads=[b_BM], writes=[b_msc])
                ker.op(act, lambda: nc.scalar.activation(out=wS[:], in_=av[:, 0:TP], func=AF.Exp, scale=1.0, bias=msc[:, 2:3]),
                       reads=[b_t, b_msc], writes=[b_t])
                bk2, bb2 = tr_bank()
                for ti, (t0, tn) in enumerate(TT):
                    ker.op(pe, lambda: nc.tensor.transpose(bk2[:tn, ti * 8:(ti + 1) * 8], av[:, t0:t0 + tn], ident[:8, :8]),
                           reads=[b_t, b_cst], writes=[bb2] if ti == 0 else (), pwrites=() if ti == 0 else [bb2])
                ker.op(dve, lambda: nc.vector.memset(a_tm[:], 0.0), writes=[b_tm])
                evac(a_tm[:, 0:8, :], bk2[:, 0:64].rearrange("p (j n) -> p j n", n=8), [bb2], [b_tm])
                evac(a_tm[:TS, 8, :], bk2[:TS, 64:72], [bb2], [b_tm])
                bk2, bb2 = tr_bank()
                for ti in range(8):
                    ker.op(pe, lambda: nc.tensor.transpose(bk2[:, ti * 8:(ti + 1) * 8], wS[:, ti * P:(ti + 1) * P], ident[:8, :8]),
                           reads=[b_t, b_cst], writes=[bb2] if ti == 0 else (), pwrites=() if ti == 0 else [bb2])
                evac(wS_tm[:, :, :], bk2[:, 0:64].rearrange("p (j n) -> p j n", n=8), [bb2], [b_tm])
                b_sum_in = ker.buf("sum_in")
                zrow = sb(ps, "zrow", [1, 512])
                b_zrow = ker.buf("zrow", dma=True)
                ker.op(dve, lambda: nc.vector.memset(zrow[:], 0.0), writes=[b_zrow])
                ker.dma(sp, sumS_in[4:5, :], zrow[:], reads=[b_zrow], writes=[b_sum_in], sbuf=b_zrow)
                ker.dma(sp, sumS_in[4:5, 0:8].rearrange("a h -> h a"), msc[:, 0:1], reads=[b_msc], pwrites=[b_sum_in], sbuf=b_msc)
                ker.dma(sp, sumS_in[4:5, 8:16].rearrange("a h -> h a"), msc[:, 1:2], reads=[b_msc], pwrites=[b_sum_in], sbuf=b_msc)
                ker.barrier()
            with ExitStack() as ps:
                k1 = [sb(ps, f"k1_{i}", [P, 8, DK], BF16) for i in range(2)]
                v1 = [sb(ps, f"v1_{i}", [P, 8, DV], BF16) for i in range(2)]
                b_kv1 = [ker.buf(f"kv1_{i}", dma=True) for i in range(2)]
                kw1 = sb(ps, "kw1", [P, 8, DK], BF16)
                b_kw1 = ker.buf("kw1")
                cl = [sb(ps, f"cl{i}", [P, 2, DV]) for i in range(2)]
                b_cl = [ker.buf(f"cl{i}", dma=True) for i in range(2)]
                nl = [sb(ps, f"nl{i}", [P, 2]) for i in range(2)]
                b_nl = [ker.buf(f"nl{i}", dma=True) for i in range(2)]
                n_flat = sumS_in[0:4, :].rearrange("a (b p) -> p (a b)", p=P)
                for h in range(H_C):
                    s2 = h % 2
                    ker.dma(sp, k1[s2][:], mk_d[0:TP, h * DK:(h + 1) * DK].rearrange("(tt p) n -> p tt n", p=P), pwrites=[b_kv1[s2]], sbuf=b_kv1[s2])
                    ker.dma(sp, v1[s2][:], mv_d[0:TP, h * DV:(h + 1) * DV].rearrange("(tt p) n -> p tt n", p=P), pwrites=[b_kv1[s2]], sbuf=b_kv1[s2])
                    for tt in range(8):
                        ker.op(dve, lambda: nc.vector.tensor_scalar(out=kw1[:, tt, :], in0=k1[s2][:, tt, :], scalar1=wS_tm[:, tt, h:h + 1], scalar2=None, op0=ALU.mult),
                               reads=[b_kv1[s2], b_tm], writes=[b_kw1] if tt == 0 else (), pwrites=() if tt == 0 else [b_kw1])
                    for dkt in range(2):
                        bk, bb = next_acc()
                        for tt in range(8):
                            ker.op(pe, lambda: nc.tensor.matmul(bk[:, :DV], kw1[:, tt, dkt * P:(dkt + 1) * P], v1[s2][:, tt, :], start=(tt == 0), stop=(tt == 7)),
                                   reads=[b_kw1, b_kv1[s2]], writes=[bb], signal=(tt == 7))
                        evac(cl[s2][:, dkt, :], bk[:, :DV], [bb], [b_cl[s2]])
                    bk, bb = next_acc()
                    for dkt in range(2):
                        for tt in range(8):
                            ker.op(pe, lambda: nc.tensor.matmul(bk[:, dkt:dkt + 1], kw1[:, tt, dkt * P:(dkt + 1) * P], ones_b[:, 0:1], start=(tt == 0), stop=(tt == 7)),
                                   reads=[b_kw1, b_ones], writes=[bb] if (dkt == 0 and tt == 0) else (), pwrites=() if (dkt == 0 and tt == 0) else [bb],
                                   signal=(tt == 7))
                    evac(nl[s2][:, :], bk[:, 0:2], [bb], [b_nl[s2]])
                    ker.dma(sp, sumC_in[h // 2][(h % 2) * DK:(h % 2 + 1) * DK, :].rearrange("(t p) v -> p t v", p=P), cl[s2][:], reads=[b_cl[s2]], pwrites=[b_sum_in], sbuf=b_cl[s2])
                    ker.dma(sp, n_flat[:, h * 2:h * 2 + 2], nl[s2][:], reads=[b_nl[s2]], pwrites=[b_sum_in], sbuf=b_nl[s2], allow_slow_non_contiguous=True)
                ker.barrier()
            for j in range(4):
                all_gather(sumC_in[j], sumC_all[j], [b_sum_in], [b_gath["sum_all"]] if j == 0 else [], [] if j == 0 else [b_gath["sum_all"]])
            all_gather(sumS_in, sumS_all, [b_sum_in], [], [b_gath["sum_all"]])
            with ExitStack() as ps:
                scrow = sb(ps, "scrow", [P, 4, 16])
                b_scrow = ker.buf("scrow", dma=True)
                scb = sb(ps, "scb", [P, 4, 16])
                boffs = sb(ps, "boffs", [P, 5, 8])
                Lp = sb(ps, "Lp", [P, 4, 8])
                pen = sb(ps, "pen", [P, 4])
                mrel = sb(ps, "mrel", [P, 8])
                tmp8 = sb(ps, "tmp8", [P, 8])
                b_c = ker.buf("combtmp")
                ker.op(dve, lambda: nc.vector.memset(scrow[:], 0.0), writes=[b_scrow])
                ker.dma(sp, scrow[0:1, :, :], sumS_all.rearrange("(r a) v -> a r v", a=5)[4:5, :, 0:16], reads=[b_gath["sum_all"]], pwrites=[b_scrow], sbuf=b_scrow)
                bk, bb = next_acc()
                ker.op(pe, lambda: nc.tensor.matmul(bk[:, 0:64], ones_f[:, :], scrow[:, :, :].rearrange("p r v -> p (r v)"), start=True, stop=True),
                       reads=[b_scrow, b_ones], writes=[bb])
                ker.op(dve, lambda: nc.vector.tensor_copy(out=scb[:], in_=bk[:, 0:64].rearrange("p (r v) -> p r v", v=16)), reads=[bb], writes=[b_c])
                D_ = lambda f, **kw: ker.op(dve, f, reads=[b_c, b_pmk], writes=[b_c])
                D_(lambda: nc.vector.memset(boffs[:], 0.0))
                for r in range(3):
                    D_(lambda: nc.vector.tensor_tensor(out=boffs[:, r + 1, :], in0=boffs[:, r, :], in1=scb[:, r, 0:8], op=ALU.add))
                for r in range(4):
                    D_(lambda: nc.vector.scalar_tensor_tensor(out=boffs[:, 4, :], in0=scb[:, r, 0:8], scalar=pmk[:, r:r + 1], in1=boffs[:, 4, :],
                                                             op0=ALU.mult, op1=ALU.add))
                D_(lambda: nc.vector.tensor_scalar(out=pen[:], in0=pmk[:, 0:4], scalar1=-1.0, scalar2=1e30, op0=ALU.add, op1=ALU.mult))
                D_(lambda: nc.vector.memset(mrel[:], 0.0))
                for r in range(4):
                    D_(lambda: nc.vector.tensor_tensor(out=Lp[:, r, :], in0=scb[:, r, 8:16], in1=boffs[:, r, :], op=ALU.subtract))
                    D_(lambda: nc.vector.tensor_scalar(out=tmp8[:], in0=Lp[:, r, :], scalar1=pen[:, r:r + 1], scalar2=None, op0=ALU.add))
                    D_(lambda: nc.vector.tensor_tensor(out=mrel[:], in0=mrel[:], in1=tmp8[:], op=ALU.max))
                ker.op(dve, lambda: nc.vector.tensor_tensor(out=minb[:], in0=boffs[:, 4, :], in1=mrel[:], op=ALU.add), reads=[b_c], writes=[b_comb])
                for r in range(4):
                    D_(lambda: nc.vector.tensor_tensor(out=tmp8[:], in0=Lp[:, r, :], in1=mrel[:], op=ALU.subtract))
                    D_(lambda: nc.vector.tensor_scalar(out=tmp8[:], in0=tmp8[:], scalar1=0.0, scalar2=None, op0=ALU.min))
                    ker.op(act, lambda: nc.scalar.activation(out=tmp8[:], in_=tmp8[:], func=AF.Exp), reads=[b_c], writes=[b_c])
                    ker.op(dve, lambda: nc.vector.tensor_scalar(out=wrr[:, r, :], in0=tmp8[:], scalar1=pmk[:, r:r + 1], scalar2=None, op0=ALU.mult),
                           reads=[b_c, b_pmk], pwrites=[b_comb])
                dg = sb(ps, "dg", [8, 8])
                b_dg = ker.buf("dg")
                ker.op(dve, lambda: nc.vector.tensor_tensor(out=dg[:, :], in0=minb[:8, 0:8], in1=ident[:8, :8], op=ALU.mult),
                       reads=[b_comb, b_cst], writes=[b_dg])
                bk2, bb2 = next_acc()
                ker.op(pe, lambda: nc.tensor.matmul(bk2[:8, 0:1], dg[:, :], ones_f[:8, 0:1], start=True, stop=True), reads=[b_dg, b_ones], writes=[bb2])
                ker.op(dve, lambda: nc.vector.tensor_copy(out=msc[:, 3:4], in_=bk2[:8, 0:1]), reads=[bb2], writes=[b_msc])
                ker.barrier()
            for (c0, cn, mc) in ((0, TP, 3), (TP, TS, 4)):
                ker.op(dve, lambda: nc.vector.tensor_scalar(out=negM[:, c0:c0 + cn], in0=M0[:, c0:c0 + cn], scalar1=msc[:, mc:mc + 1], scalar2=-1.0,
                                                            op0=ALU.max, op1=ALU.mult),
                       reads=[b_BM, b_msc], pwrites=[b_rows])
                ker.op(dve, lambda: nc.vector.tensor_tensor(out=negm[:, c0:c0 + cn], in0=negM[:, c0:c0 + cn], in1=Bc[:, c0:c0 + cn], op=ALU.subtract),
                       reads=[b_BM, b_rows], pwrites=[b_rows])
                ker.op(act, lambda: nc.scalar.activation(out=wint[:, c0:c0 + cn], in_=negM[:, c0:c0 + cn], func=AF.Exp, scale=1.0, bias=msc[:, mc:mc + 1]),
                       reads=[b_rows, b_msc], pwrites=[b_rows])
                ker.op(dve, lambda: nc.vector.tensor_scalar(out=msc[:, mc + 2:mc + 3], in0=negm[:, c0 + cn - 1:c0 + cn], scalar1=-1.0, scalar2=None, op0=ALU.mult),
                       reads=[b_rows], writes=[b_msc])
            ker.dma(sp, o_m[0:1, :].rearrange("a h -> h a"), msc[:, 5:6], reads=[b_msc], sbuf=b_msc)
            ker.dma(sp, o_m[1:2, :].rearrange("a h -> h a"), msc[:, 6:7], reads=[b_msc], sbuf=b_msc)
            ker.barrier()
            pR.__exit__(None, None, None)
            with ExitStack() as ps:
                qT2 = [sb(ps, f"qT2_{i}", [P, 2, T], BF16) for i in range(2)]
                kT2 = [sb(ps, f"kT2_{i}", [P, 2, T], BF16) for i in range(2)]
                ktm = [sb(ps, f"ktm{i}", [P, 9, DK], BF16) for i in range(2)]
                vtm = [sb(ps, f"vtm{i}", [P, 9, DV], BF16) for i in range(2)]
                b_in2 = [ker.buf(f"in2_{i}", dma=True) for i in range(2)]
                so = sb(ps, "so", [P, 4, T])
                b_so = ker.buf("so", dma=True)
                Cst = sb(ps, "Cst", [P, 2, DV])
                Cb = sb(ps, "Cb", [P, 2, DV], BF16)
                nst = sb(ps, "nst", [P, 2])
                nb = sb(ps, "nb", [P, 2, P], BF16)
                Mprev = sb(ps, "Mprev", [P, 1])
                b_st = ker.buf("state", dma=True)
                clr = [sb(ps, f"clr{i}", [P, 2, DV]) for i in range(2)]
                b_clr = [ker.buf(f"clr{i}", dma=True) for i in range(2)]
                nlr = sb(ps, "nlr", [P, 4, 2])
                b_nlr = ker.buf("nlr", dma=True)
                bcs = sb(ps, "bcs", [P, 3, P])
                b_bcs = ker.buf("bcs")
                b_wib = ker.buf("wib")
                e1 = sb(ps, "e1", [P, P])
                Dm = sb(ps, "Dm", [P, P])
                Wt = sb(ps, "Wt", [P, P], BF16)
                qw = sb(ps, "qw", [P, 2, P], BF16)
                exm = sb(ps, "exm", [P, P])
                rd = sb(ps, "rd", [P, P])
                hT = sb(ps, "hT", [P, 4, P])
                hsq = sb(ps, "hsq", [P, 4, P])
                rs = sb(ps, "rs", [P, P])
                t1 = sb(ps, "t1", [P, P])
                wsc = sb(ps, "wsc", [P, 1])
                wcc = sb(ps, "wcc", [P, 1])
                kwt = sb(ps, "kwt", [P, DK], BF16)
                b_x = {n: ker.buf(n) for n in ("e1", "Dm", "Wt", "qw", "exm", "rd", "hT", "hsq", "rs", "t1", "wsc", "wcc", "kwt")}

                def load_head(h):
                    s2 = h % 2
                    ker.dma(sp, qT2[s2][:], mq_d[2 * h:2 * h + 2].rearrange("k p t -> p k t"), pwrites=[b_in2[s2]], sbuf=b_in2[s2])
                    ker.dma(sp, kT2[s2][:], mkT_d[2 * h:2 * h + 2].rearrange("k p t -> p k t"), pwrites=[b_in2[s2]], sbuf=b_in2[s2])
                    ker.dma(sp, ktm[s2][:, 0:8, :], mk_d[0:TP, h * DK:(h + 1) * DK].rearrange("(tt p) n -> p tt n", p=P), pwrites=[b_in2[s2]], sbuf=b_in2[s2])
                    ker.dma(sp, ktm[s2][:TS, 8, :], mk_d[TP:T, h * DK:(h + 1) * DK], pwrites=[b_in2[s2]], sbuf=b_in2[s2])
                    ker.dma(sp, vtm[s2][:, 0:8, :], mv_d[0:TP, h * DV:(h + 1) * DV].rearrange("(tt p) n -> p tt n", p=P), pwrites=[b_in2[s2]], sbuf=b_in2[s2])
                    ker.dma(sp, vtm[s2][:TS, 8, :], mv_d[TP:T, h * DV:(h + 1) * DV], pwrites=[b_in2[s2]], sbuf=b_in2[s2])

                def refresh_state_copies():
                    ker.op(act, lambda: nc.scalar.copy(out=Cb[:], in_=Cst[:]), reads=[b_st], pwrites=[b_st])
                    for dkt in range(2):
                        ker.op(dve, lambda: nc.vector.tensor_scalar(out=nb[:, dkt, :], in0=ones_f[:, :], scalar1=nst[:, dkt:dkt + 1], scalar2=None, op0=ALU.mult),
                               reads=[b_st, b_ones], pwrites=[b_st])

                def tile_step(h, s2, ti, t0, L):
                    bX, bbX = next_acc()
                    for j, row in enumerate((negM, negm)):
                        ker.op(pe, lambda: nc.tensor.matmul(bX[:, j * P:j * P + L], esel[:, h, :], row[:, t0:t0 + L], start=True, stop=True),
                               reads=[b_rows, b_esel], writes=[bbX] if j == 0 else (), pwrites=() if j == 0 else [bbX])
                    ker.op(act, lambda: nc.scalar.copy(out=bcs[:, 0:2, :L], in_=bX[:, 0:2 * P].rearrange("p (j n) -> p j n", n=P)[:, :, :L]),
                           reads=[bbX], writes=[b_bcs])
                    ker.op(dve, lambda: nc.vector.tensor_scalar(out=e1[:L, :L], in0=bcs[:L, 0, :L], scalar1=a_tm[:L, ti, h:h + 1], scalar2=0.0,
                                                                op0=ALU.add, op1=ALU.min),
                           reads=[b_bcs, b_tm], writes=[b_x["e1"]])
                    ker.op(act, lambda: nc.scalar.activation(out=e1[:L, :L], in_=e1[:L, :L], func=AF.Exp), reads=[b_x["e1"]], writes=[b_x["e1"]])
                    ker.op(dve, lambda: nc.vector.tensor_tensor(out=Dm[:L, :L], in0=e1[:L, :L], in1=cst[:L, 2, :L], op=ALU.mult),
                           reads=[b_x["e1"], b_cst], writes=[b_x["Dm"]])
                    bS, bbS = next_acc()
                    for dkt in range(2):
                        ker.op(pe, lambda: nc.tensor.matmul(bS[:L, :L], kT2[s2][:, dkt, t0:t0 + L], qT2[s2][:, dkt, t0:t0 + L], start=(dkt == 0), stop=(dkt == 1)),
                               reads=[b_in2[s2]], writes=[bbS], signal=(dkt == 1))
                    ker.op(dve, lambda: nc.vector.tensor_tensor(out=Wt[:L, :L], in0=bS[:L, :L], in1=Dm[:L, :L], op=ALU.mult),
                           reads=[bbS, b_x["Dm"]], writes=[b_x["Wt"]])
                    ker.op(act, lambda: nc.scalar.activation(out=bcs[:, 2, :L], in_=bcs[:, 0, :L], func=AF.Exp, scale=1.0, bias=Mprev[:, 0:1]),
                           reads=[b_bcs, b_st], writes=[b_wib])
                    for dkt in range(2):
                        ker.op(dve, lambda: nc.vector.tensor_tensor(out=qw[:, dkt, :L], in0=qT2[s2][:, dkt, t0:t0 + L], in1=bcs[:, 2, :L], op=ALU.mult),
                               reads=[b_in2[s2], b_wib], writes=[b_x["qw"]] if dkt == 0 else (), pwrites=() if dkt == 0 else [b_x["qw"]])
                    bN, bbN = next_acc()
                    first = True
                    for dvt in range(4):
                        reg = bN[:, dvt * P:dvt * P + L]
                        ker.op(pe, lambda: nc.tensor.matmul(reg, vtm[s2][:L, ti, dvt * P:(dvt + 1) * P], Wt[:L, :L], start=True, stop=False),
                               reads=[b_in2[s2], b_x["Wt"]], writes=[bbN] if first else (), pwrites=() if first else [bbN], signal=False)
                        first = False
                        for dkt in range(2):
                            ker.op(pe, lambda: nc.tensor.matmul(reg, Cb[:, dkt, dvt * P:(dvt + 1) * P], qw[:, dkt, :L], start=False, stop=(dkt == 1)),
                                   reads=[b_st, b_x["qw"]], pwrites=[bbN], signal=(dkt == 1 and dvt == 3))
                    bD, bbD = next_acc()
                    ker.op(pe, lambda: nc.tensor.matmul(bD[:, :L], ones_b[:L, :], Wt[:L, :L], start=True, stop=False),
                           reads=[b_ones, b_x["Wt"]], writes=[bbD], signal=False)
                    for dkt in range(2):
                        ker.op(pe, lambda: nc.tensor.matmul(bD[:, :L], nb[:, dkt, :], qw[:, dkt, :L], start=False, stop=(dkt == 1)),
                               reads=[b_st, b_x["qw"]], pwrites=[bbD], signal=(dkt == 1))
                    ker.op(act, lambda: nc.scalar.activation(out=exm[:, :L], in_=bcs[:, 1, :L], func=AF.Exp), reads=[b_bcs], writes=[b_x["exm"]])
                    ker.op(act, lambda: nc.scalar.activation(out=rd[:, :L], in_=bD[:, :L], func=AF.Abs),
                           reads=[bbD], writes=[b_x["rd"]])
                    ker.op(dve, lambda: nc.vector.tensor_tensor(out=rd[:, :L], in0=rd[:, :L], in1=exm[:, :L], op=ALU.max),
                           reads=[b_x["rd"], b_x["exm"]], writes=[b_x["rd"]])
                    ker.op(dve, lambda: nc.vector.reciprocal(out=rd[:, :L], in_=rd[:, :L]), reads=[b_x["rd"]], writes=[b_x["rd"]])
                    for dvt in range(4):
                        ker.op(dve, lambda: nc.vector.tensor_tensor(out=hT[:, dvt, :L], in0=bN[:, dvt * P:dvt * P + L], in1=rd[:, :L], op=ALU.mult),
                               reads=[bbN, b_x["rd"]], writes=[b_x["hT"]] if dvt == 0 else (), pwrites=() if dvt == 0 else [b_x["hT"]])
                    ker.op(dve, lambda: nc.vector.tensor_tensor(out=hsq[:, :, :L], in0=hT[:, :, :L], in1=hT[:, :, :L], op=ALU.mult),
                           reads=[b_x["hT"]], writes=[b_x["hsq"]])
                    bQ, bbQ = next_acc()
                    for dvt in range(4):
                        ker.op(pe, lambda: nc.tensor.matmul(bQ[:, :L], ones_f[:, :], hsq[:, dvt, :L], start=(dvt == 0), stop=(dvt == 3)),
                               reads=[b_x["hsq"], b_ones], writes=[bbQ], signal=(dvt == 3))
                    ker.op(act, lambda: nc.scalar.activation(out=rs[:, :L], in_=bQ[:, :L], func=AF.Ln, scale=1.0 / DV, bias=eps_t[:, 0:1]),
                           reads=[bbQ, b_ones], writes=[b_x["rs"]])
                    ker.op(act, lambda: nc.scalar.activation(out=rs[:, :L], in_=rs[:, :L], func=AF.Exp, scale=-0.5), reads=[b_x["rs"]], writes=[b_x["rs"]])
                    for dvt in range(4):
                        ker.op(dve, lambda: nc.vector.scalar_tensor_tensor(out=t1[:, :L], in0=hT[:, dvt, :L], scalar=hn[:, h * 4 + dvt:h * 4 + dvt + 1], in1=rs[:, :L],
                                                                          op0=ALU.mult, op1=ALU.mult),
                               reads=[b_x["hT"], b_x["rs"], b_hn], writes=[b_x["t1"]])
                        ker.op(dve, lambda: nc.vector.tensor_tensor(out=moutT[:, h * 4 + dvt, t0:t0 + L], in0=t1[:, :L], in1=so[:, dvt, t0:t0 + L], op=ALU.mult),
                               reads=[b_x["t1"], b_so], pwrites=[b_mout])
                    ker.op(act, lambda: nc.scalar.activation(out=wsc[:L, :], in_=a_tm[:L, ti, h:h + 1], func=AF.Exp, scale=1.0, bias=bcs[:L, 0, L - 1:L]),
                           reads=[b_tm, b_bcs], writes=[b_x["wsc"]])
                    ker.op(act, lambda: nc.scalar.activation(out=wcc[:, :], in_=bcs[:, 0, L - 1:L], func=AF.Exp, scale=1.0, bias=Mprev[:, 0:1]),
                           reads=[b_bcs, b_st], writes=[b_x["wcc"]])
                    ker.op(dve, lambda: nc.vector.tensor_scalar(out=kwt[:L, :], in0=ktm[s2][:L, ti, :], scalar1=wsc[:L, 0:1], scalar2=None, op0=ALU.mult),
                           reads=[b_in2[s2], b_x["wsc"]], writes=[b_x["kwt"]])
                    for dkt in range(2):
                        bC, bbC = next_acc()
                        ker.op(pe, lambda: nc.tensor.matmul(bC[:, :DV], kwt[:L, dkt * P:(dkt + 1) * P], vtm[s2][:L, ti, :], start=True, stop=True),
                               reads=[b_x["kwt"], b_in2[s2]], writes=[bbC])
                        ker.op(dve, lambda: nc.vector.scalar_tensor_tensor(out=Cst[:, dkt, :], in0=Cst[:, dkt, :], scalar=wcc[:, 0:1], in1=bC[:, :DV],
                                                                          op0=ALU.mult, op1=ALU.add),
                               reads=[bbC, b_x["wcc"], b_st], writes=[b_st])
                    bn_, bbn_ = next_acc()
                    for dkt in range(2):
                        ker.op(pe, lambda: nc.tensor.matmul(bn_[:, dkt:dkt + 1], kwt[:L, dkt * P:(dkt + 1) * P], ones_b[:L, 0:1], start=True, stop=True),
                               reads=[b_x["kwt"], b_ones], writes=[bbn_] if dkt == 0 else (), pwrites=() if dkt == 0 else [bbn_])
                    ker.op(dve, lambda: nc.vector.scalar_tensor_tensor(out=nst[:, :], in0=nst[:, :], scalar=wcc[:, 0:1], in1=bn_[:, 0:2], op0=ALU.mult, op1=ALU.add),
                           reads=[bbn_, b_x["wcc"], b_st], writes=[b_st])
                    ker.op(dve, lambda: nc.vector.tensor_scalar(out=Mprev[:, :], in0=bcs[:, 0, L - 1:L], scalar1=-1.0, scalar2=None, op0=ALU.mult),
                           reads=[b_bcs, b_x["wcc"]], writes=[b_st])
                    refresh_state_copies()

                load_head(0)
                for h in range(H_C):
                    s2 = h % 2
                    if h + 1 < H_C:
                        load_head(h + 1)
                    ker.dma(sp, so[:], mo_d[4 * h:4 * h + 4].rearrange("k p t -> p k t"), writes=[b_so], sbuf=b_so)
                    for r in range(4):
                        s3 = r % 2
                        n_r = sumS_all[r * 5:r * 5 + 4, :].rearrange("a (b p) -> p (a b)", p=P)
                        ker.dma(sp, nlr[:, r, :], n_r[:, 2 * h:2 * h + 2], reads=[b_gath["sum_all"]], writes=[b_nlr] if r == 0 else (),
                                pwrites=() if r == 0 else [b_nlr], sbuf=b_nlr, allow_slow_non_contiguous=True)
                        ker.dma(sp, clr[s3][:], sumC_all[h // 2][r * 2 * DK + (h % 2) * DK:r * 2 * DK + (h % 2 + 1) * DK, :].rearrange("(t p) v -> p t v", p=P), reads=[b_gath["sum_all"]],
                                writes=[b_clr[s3]], sbuf=b_clr[s3])
                        if r == 0:
                            ker.op(dve, lambda: nc.vector.tensor_scalar(out=Cst[:], in0=clr[s3][:], scalar1=wrr[:, r, h:h + 1], scalar2=None, op0=ALU.mult),
                                   reads=[b_clr[s3], b_comb], writes=[b_st])
                            ker.op(dve, lambda: nc.vector.tensor_scalar(out=nst[:], in0=nlr[:, r, :], scalar1=wrr[:, r, h:h + 1], scalar2=None, op0=ALU.mult),
                                   reads=[b_nlr, b_comb, b_st], writes=[b_st])
                        else:
                            ker.op(dve, lambda: nc.vector.scalar_tensor_tensor(out=Cst[:], in0=clr[s3][:], scalar=wrr[:, r, h:h + 1], in1=Cst[:],
                                                                              op0=ALU.mult, op1=ALU.add),
                                   reads=[b_clr[s3], b_comb, b_st], writes=[b_st])
                            ker.op(dve, lambda: nc.vector.scalar_tensor_tensor(out=nst[:], in0=nlr[:, r, :], scalar=wrr[:, r, h:h + 1], in1=nst[:],
                                                                              op0=ALU.mult, op1=ALU.add),
                                   reads=[b_nlr, b_comb, b_st], writes=[b_st])
                    ker.op(dve, lambda: nc.vector.tensor_copy(out=Mprev[:, :], in_=minb[:, h:h + 1]), reads=[b_comb, b_st], writes=[b_st])
                    refresh_state_copies()
                    for ti in range(8):
                        tile_step(h, s2, ti, ti * P, P)
                    ker.dma(sp, o_c[0, h].rearrange("(t p) v -> p t v", p=P), Cst[:], reads=[b_st], sbuf=b_st)
                    ker.dma(sp, o_n[0, h:h + 1, :].rearrange("a (t p) -> p (a t)", p=P), nst[:], reads=[b_st], sbuf=b_st, allow_slow_non_contiguous=True)
                    ker.dma(sp, Cst[:], st_c[h].rearrange("(t p) v -> p t v", p=P), writes=[b_st], sbuf=b_st)
                    ker.dma(sp, nst[:], st_n[h:h + 1, :].rearrange("a (t p) -> p (a t)", p=P), reads=[b_st], writes=[b_st], sbuf=b_st, allow_slow_non_contiguous=True)
                    ker.op(dve, lambda: nc.vector.tensor_copy(out=Mprev[:, :], in_=stm[:, h:h + 1]), reads=[b_stm, b_st], writes=[b_st])
                    refresh_state_copies()
                    tile_step(h, s2, 8, TP, TS)
                    ker.dma(sp, o_c[1, h].rearrange("(t p) v -> p t v", p=P), Cst[:], reads=[b_st], sbuf=b_st)
                    ker.dma(sp, o_n[1, h:h + 1, :].rearrange("a (t p) -> p (a t)", p=P), nst[:], reads=[b_st], sbuf=b_st, allow_slow_non_contiguous=True)
                ker.barrier()
            if os.environ.get("KDEBUG"):
                b_dbg = ker.buf("dbg2", dma=True)
                ker.dma(sp, dt_tmp("dbg_mo", [KT, P, T], BF16).rearrange("k p t -> p k t"), moutT[:], reads=[b_mout], sbuf=b_dbg)
                ker.barrier()
            if STAGE >= 8 and 'g4' not in SKIP:
              with ExitStack() as ps:
                alloc_wslots(ps)
                ost = [sb(ps, f"ost{i}", [P, T]) for i in range(2)]
                b_ost = [ker.buf(f"ost{i}", dma="sw") for i in range(2)]
                blocks = [(c_w_out[:, b * 256:(b + 1) * 256], KT, 256) for b in range(16)]
                jobs = [(b, j * P, P, ("o", b * 2 + j)) for b in range(16) for j in range(2)]
                gemm(lambda tag, kt: (moutT[:, kt, :], b_mout), blocks, jobs, CH, make_accum_epi(ost, b_ost))
                ker.barrier()
            pC.__exit__(None, None, None)
            pL.__exit__(None, None, None)

        if STAGE >= 6:
            layer1()
        if STAGE >= 9 and 'ffn1' not in SKIP:
            ffn(1)

        with ExitStack() as ps:
            yst = [sb(ps, f"yst{i}", [P, 2, T]) for i in range(1)]
            b_yst = [ker.buf("yst0"), ker.buf("yst1")]
            ytk = [sb(ps, f"ytk{i}", [P, 9, 2 * P]) for i in range(2)]
            b_ytk = [ker.buf(f"ytk{i}", dma=True) for i in range(2)]
            cnt = [0]

            def fin_cb(kt, xs_t, b_xs_t, rstd, b_rstd):
                h = kt % 2
                ker.op(dve, lambda: nc.vector.scalar_tensor_tensor(out=yst[0][:, h, :], in0=xs_t[:], scalar=nrm[:, 4, kt:kt + 1],
                                                                  in1=rstd[:], op0=ALU.mult, op1=ALU.mult),
                       reads=[b_xs_t, b_rstd, b_nrm], writes=[b_yst[h]])
                s = (kt // 2) % 2
                for g0, g1 in ((0, 4), (4, 8), (8, 9)):
                    srcs = [((yst[0][:, h, TT[ti][0]:TT[ti][0] + TT[ti][1]], [b_yst[h]]), TT[ti][1], P) for ti in range(g0, g1)]

                    def wr(bk, bb, g0=g0, g1=g1, s=s, h=h):
                        if g0 < 8:
                            evac(ytk[s][:, g0:g1, h * P:(h + 1) * P], bk[:, :].rearrange("p (j n) -> p j n", n=P), [bb], [b_ytk[s]])
                        else:
                            evac(ytk[s][:TS, 8, h * P:(h + 1) * P], bk[:TS, 0:P], [bb], [b_ytk[s]])
                    transpose_group(srcs, wr)
                if h == 1:
                    col0 = (kt - 1) * P
                    ker.dma(sp, y_tok[0:TP, col0:col0 + 2 * P].rearrange("(tt p) n -> p tt n", p=P), ytk[s][:, 0:8, :],
                            reads=[b_ytk[s]], sbuf=b_ytk[s])
                    ker.dma(sp, y_tok[TP:T, col0:col0 + 2 * P], ytk[s][:TS, 8, :], reads=[b_ytk[s]], sbuf=b_ytk[s])

            rmsnorm_to(ps, None, None, 4, out_f32_cb=fin_cb)
            ker.barrier()
    _CACHE['declared'] = declared
    return nc


_CACHE = {}


def _get_program():
    if "nc" not in _CACHE:
        _CACHE["nc"] = build_program()
    return _CACHE["nc"]


def kernel(x_prompt, x_sample, cache_fox_k, cache_fox_v, cache_fox_logf, state_mlstm_c, state_mlstm_n,
           state_mlstm_m, norm_mix, norm_ffn, norm_final, ab_w_in, ab_w_out, gmlp_w_s, gmlp_b, fox_b_f,
           c_w_in, c_b_i, c_b_f, c_head_norm, c_w_out, ffn_w_gate, ffn_w_up, ffn_w_down):
    f = lambda a: np.ascontiguousarray(np.asarray(a, dtype=np.float32))
    x_prompt, x_sample = f(x_prompt), f(x_sample)
    nc = _get_program()
    nrm_all = np.stack([f(norm_mix)[0], f(norm_ffn)[0], f(norm_mix)[1], f(norm_ffn)[1], f(norm_final)], 0)
    norms = np.ascontiguousarray(nrm_all.reshape(5, KT, P).transpose(2, 0, 1))
    consts = np.zeros((P, 5, P), np.float32)
    consts[15, 4, :] = 1.0
    consts[:, 0, :] = np.eye(P)
    consts[:, 1, :] = np.tril(np.ones((P, P)))
    consts[:, 2, :] = np.triu(np.ones((P, P)))
    consts[127, 3, :] = 1.0
    kpos = np.zeros((P, 40), np.float32)
    for kt in range(32):
        kpos[:, kt] = kt * 128 + np.arange(P)
    kpos[:, 32] = 1024 + np.arange(P)
    kpos[16:, 32] = 1e9
    shared = {
        "norms": norms,
        "ab_w_in": f(ab_w_in)[0], "ab_w_out": f(ab_w_out)[0],
        "gmlp_ws": f(gmlp_w_s)[0],
        "gmlp_bb": np.ascontiguousarray(np.broadcast_to(f(gmlp_b)[0][None], (P, 8, P))),
        "fox_bf": f(fox_b_f)[0].reshape(16, 1),
        "c_w_in": f(c_w_in)[0],
        "c_bif": np.concatenate([f(c_b_i)[0], f(c_b_f)[0]]).reshape(16, 1),
        "c_hn": np.ascontiguousarray(f(c_head_norm)[0].reshape(32, P).T),
        "c_w_out": f(c_w_out)[0],
        "ffn_wg": f(ffn_w_gate), "ffn_wu": f(ffn_w_up), "ffn_wd": f(ffn_w_down),
        "consts": consts, "kpos": kpos,
        "esel": np.ascontiguousarray(np.broadcast_to(np.eye(8, dtype=np.float32)[:, :, None], (8, 8, P))).reshape(8, 8 * P),
    }
    in_maps = []
    for c in range(NCORES):
        b, p = c // 4, c % 4
        qpos = np.concatenate([p * TP + np.arange(TP), 1024 + np.arange(TS)]).astype(np.float32)
        pm = np.zeros((P, 8), np.float32)
        for r in range(4):
            pm[:, r] = 1.0 if r < p else 0.0
            pm[:, 4 + r] = 1.0 if r == p else 0.0
        m = dict(shared)
        m.update({
            "x_tok": np.ascontiguousarray(np.concatenate([x_prompt[b, p * TP:(p + 1) * TP], x_sample[c]], 0)),
            "cache_k": f(cache_fox_k)[0, c].reshape(1024, 2048),
            "cache_v": f(cache_fox_v)[0, c].reshape(1024, 2048),
            "cache_lf": f(cache_fox_logf)[0, c],
            "st_c": f(state_mlstm_c)[0, c],
            "st_n": f(state_mlstm_n)[0, c],
            "st_m_c": f(state_mlstm_m)[0, c].reshape(8, 1),
            "st_m_b": np.ascontiguousarray(np.broadcast_to(f(state_mlstm_m)[0, c][None], (P, H_C))),
            "qpos_b": np.ascontiguousarray(np.broadcast_to(qpos[None], (P, T))),
            "pmask_b": pm,
        })
        in_maps.append(m)
    decl = set(_CACHE['declared'])
    in_maps = [{k: v for k, v in m.items() if k in decl} for m in in_maps]
    res = run_bass_kernel_spmd(nc, in_maps, core_ids=list(range(NCORES)))
    R = res.results
    B, S = 2, 4096

    def prompt_rows(name, width):
        return np.stack([np.concatenate([R[b * 4 + p][name][:TP] for p in range(4)], 0) for b in range(B)], 0).reshape(B, S, width)

    def sample_rows(name, width):
        return np.stack([R[c][name][TP:T] for c in range(NCORES)], 0).reshape(NCORES, TS, width)

    yp = prompt_rows("y_tok", D)
    ys = sample_rows("y_tok", D)
    pk = prompt_rows("o_k", 2048).reshape(1, B, S, 16, 128)
    pv = prompt_rows("o_v", 2048).reshape(1, B, S, 16, 128)
    plf = prompt_rows("o_lf", 16).reshape(1, B, S, 16)
    pc = np.stack([R[3]["o_c"][0], R[7]["o_c"][0]], 0)[None]
    pn = np.stack([R[3]["o_n"][0], R[7]["o_n"][0]], 0)[None]
    pm_ = np.stack([R[3]["o_m"][0], R[7]["o_m"][0]], 0)[None]
    sk = sample_rows("o_k", 2048).reshape(1, NCORES, TS, 16, 128)
    sv = sample_rows("o_v", 2048).reshape(1, NCORES, TS, 16, 128)
    slf = sample_rows("o_lf", 16).reshape(1, NCORES, TS, 16)
    sgv = np.stack([R[c]["o_gv"] for c in range(NCORES)], 0)[None]
    sc = np.stack([R[c]["o_c"][1] for c in range(NCORES)], 0)[None]
    sn = np.stack([R[c]["o_n"][1] for c in range(NCORES)], 0)[None]
    sm = np.stack([R[c]["o_m"][1] for c in range(NCORES)], 0)[None]
    outs = (yp, ys, pk, pv, plf, pc, pn, pm_, sk, sv, slf, sgv, sc, sn, sm)
    return tuple(np.ascontiguousarray(o, dtype=np.float32) for o in outs)
```

```python
import os
import numpy as np
import ml_dtypes
from contextlib import ExitStack
import concourse.bass as bass
import concourse.mybir as mybir
from concourse.bass_utils import run_bass_kernel_spmd

F32 = mybir.dt.float32
BF16 = mybir.dt.bfloat16
AF = mybir.ActivationFunctionType
ALU = mybir.AluOpType

P = 128
D = 4096
KT = 32
TP = 1024
TS = 16
T = TP + TS
CH = [(0, 512), (512, 512), (1024, 16)]
TT = [(i * 128, 128) for i in range(8)] + [(1024, 16)]
D_A = 2048
D_B = 2048
H_B = 16
AB_IN = 10256
H_C = 8
DK = 256
DV = 512
C_IN = 12304
D_FF = 11008
EPS = 1e-6
NCORES = int(os.environ.get('KCORES', '8'))
GROUPS = [list(range(g * 4, g * 4 + 4)) for g in range(NCORES // 4)]

STAGE = int(os.environ.get('KSTAGE', '99'))
SKIP = set(os.environ.get('KSKIP', '').split(','))


class Buf:
    __slots__ = ("name", "w", "wp", "r", "dsem", "excl")

    def __init__(self, name, dsem=None):
        self.name = name
        self.excl = False
        self.w = {}
        self.wp = {}
        self.r = {}
        self.dsem = dsem


class Q:
    def __init__(self, ker, eng, name, is_pe=False):
        self.ker = ker
        self.eng = eng
        self.name = name
        self.is_pe = is_pe
        self.key = ker.new_sem("q_" + name)
        self.seen = {}

    def wait(self, k, v):
        if self.seen.get(k, 0) >= v:
            return
        assert v <= self.ker.issued[k], (self.name, k, v, self.ker.issued[k])
        self.eng.wait_ge(self.ker.sems[k], v)
        self.seen[k] = v


class Ker:
    def __init__(self, nc, es):
        self.nc = nc
        self.es = es
        self.sems = []
        self.issued = []
        self.pe = Q(self, nc.tensor, "pe", True)
        self.act = Q(self, nc.scalar, "act")
        self.dve = Q(self, nc.vector, "dve")
        self.pool = Q(self, nc.gpsimd, "pool")
        self.sp = Q(self, nc.sync, "sp")
        self.queues = [self.pe, self.act, self.dve, self.pool, self.sp]
        self.dma_pool = [self.new_sem(f"d{i}") for i in range(36)]
        self.dma_next = 0
        self.sw_pool = [self.new_sem(f"w{i}") for i in range(12)]
        self.sw_next = 0
        self.dma_last = {}
        self.uid = 0

    def new_sem(self, name):
        s = self.es.enter_context(self.nc.semaphore(name))
        self.sems.append(s)
        self.issued.append(0)
        return len(self.sems) - 1

    def buf(self, name, dma=False):
        ds = None
        if dma == "sw":
            ds = self.sw_pool[self.sw_next % len(self.sw_pool)]
            self.sw_next += 1
        elif dma:
            ds = self.dma_pool[self.dma_next % len(self.dma_pool)]
            self.dma_next += 1
        return Buf(name, ds)

    def _deps(self, q, reads, writes, pwrites=()):
        deps = {}

        def add(d):
            for k, v in d.items():
                if deps.get(k, 0) < v:
                    deps[k] = v
        for b in reads:
            add(b.w)
            add(b.wp)
            if b.excl:
                add({k: v for k, v in b.r.items() if k != q.key})
        for b in writes:
            add(b.w)
            add(b.wp)
            add(b.r)
        for b in pwrites:
            add(b.w)
            add(b.r)
        for k, v in deps.items():
            if q.is_pe and k == q.key:
                continue
            q.wait(k, v)

    def _record(self, ev, reads, writes, pwrites=()):
        k, v = ev
        for b in reads:
            if b.r.get(k, 0) < v:
                b.r[k] = v
        for b in writes:
            b.w = {k: v}
            b.wp = {}
            b.r = {}
        for b in pwrites:
            if b.wp.get(k, 0) < v:
                b.wp[k] = v

    def op(self, q, fn, reads=(), writes=(), pwrites=(), signal=True):
        self._deps(q, reads, writes, pwrites)
        ins = fn()
        if signal:
            self.issued[q.key] += 1
            ins.then_inc(self.sems[q.key], 1)
            ev = (q.key, self.issued[q.key])
        else:
            ev = (q.key, self.issued[q.key] + 1)
        self._record(ev, reads, writes, pwrites)
        return ins

    def dma(self, q, out, in_, reads=(), writes=(), pwrites=(), sbuf=None, **kw):
        k = sbuf.dsem
        self._deps(q, reads, writes, pwrites)
        if self.issued[k] > 0:
            q.wait(k, self.issued[k])
        ins = q.eng.dma_start(out=out, in_=in_, **kw)
        self.issued[k] += 16
        ins.then_inc(self.sems[k], 16)
        self._record((k, self.issued[k]), reads, writes, pwrites)
        return ins

    def barrier(self):
        for q in self.queues:
            for k in range(len(self.sems)):
                if self.issued[k] > 0:
                    q.wait(k, self.issued[k])


def build_program():
    nc = bass.Bass("TRN2", target_bir_lowering=False)
    declared = []

    def dt_in(name, shape, dt=F32):
        declared.append(name)
        return nc.dram_tensor(name, list(shape), dt, kind="ExternalInput").ap()
    dt_out = lambda name, shape, dt=F32: nc.dram_tensor(name, list(shape), dt, kind="ExternalOutput").ap()
    dt_tmp = lambda name, shape, dt=F32: nc.dram_tensor(name, list(shape), dt).ap()

    x_tok = dt_in("x_tok", [T, D])
    cache_k = dt_in("cache_k", [1024, 2048])
    cache_v = dt_in("cache_v", [1024, 2048])
    cache_lf = dt_in("cache_lf", [1024, 16])
    st_c = dt_in("st_c", [H_C, DK, DV])
    st_n = dt_in("st_n", [H_C, DK])
    st_m_b = dt_in("st_m_b", [P, H_C])
    norms = dt_in("norms", [P, 5, KT])
    ab_w_in_l = lambda: dt_in("ab_w_in", [D, AB_IN])
    ab_w_out_l = lambda: dt_in("ab_w_out", [D, D])
    gmlp_ws = dt_in("gmlp_ws", [8, P, P])
    gmlp_bb = dt_in("gmlp_bb", [P, 8, P])
    fox_bf = dt_in("fox_bf", [16, 1])
    c_w_in_l = lambda: dt_in("c_w_in", [D, C_IN])
    c_bif = dt_in("c_bif", [16, 1])
    c_hn = dt_in("c_hn", [P, 32])
    c_w_out_l = lambda: dt_in("c_w_out", [D, D])
    ffn_wg_l = lambda: dt_in("ffn_wg", [2, D, D_FF])
    ffn_wu_l = lambda: dt_in("ffn_wu", [2, D, D_FF])
    ffn_wd_l = lambda: dt_in("ffn_wd", [2, D_FF, D])
    consts = dt_in("consts", [P, 5, P])
    qpos_b = dt_in("qpos_b", [P, T])
    kpos = dt_in("kpos", [P, 40])
    pmask_b = dt_in("pmask_b", [P, 8])

    y_tok = dt_out("y_tok", [T, D])
    o_k = dt_out("o_k", [T, 2048])
    o_v = dt_out("o_v", [T, 2048])
    o_lf = dt_out("o_lf", [T, 16])
    o_gv = dt_out("o_gv", [TS, 2048])
    o_c = dt_out("o_c", [2, H_C, DK, DV])
    o_n = dt_out("o_n", [2, H_C, DK])
    o_m = dt_out("o_m", [2, H_C])

    xres = dt_tmp("xres", [KT, P, T])
    uT_d = dt_tmp("uT_d", [16, P, T], BF16)
    va_d = dt_tmp("va_d", [T, 2048], BF16)
    qT_d = dt_tmp("qT_d", [16, P, T], BF16)
    kT_in = [dt_tmp(f"kT_in{j}", [4 * P, TP], BF16) for j in range(4)]
    v_in = [dt_tmp(f"v_in{j}", [TP, 4 * P], BF16) for j in range(4)]
    kT_all = [dt_tmp(f"kT_all{j}", [4 * 4 * P, TP], BF16) for j in range(4)]
    v_all = [dt_tmp(f"v_all{j}", [4 * TP, 4 * P], BF16) for j in range(4)]
    kTs_d = dt_tmp("kTs_d", [16, P, TS], BF16)
    vs_d = dt_tmp("vs_d", [TS, 2048], BF16)
    c_in = dt_tmp("c_in", [16, TP])
    c_all = dt_tmp("c_all", [64, TP])
    ffn_wg, ffn_wu, ffn_wd = (ffn_wg_l(), ffn_wu_l(), ffn_wd_l()) if STAGE >= 5 else (None, None, None)

    es = ExitStack()
    with es:
        ker = Ker(nc, es)
        pe, act, dve, pool, sp = ker.pe, ker.act, ker.dve, ker.pool, ker.sp
        _uid = [0]

        def sb(st, name, shape, dt=F32):
            _uid[0] += 1
            return st.enter_context(nc.sbuf_tensor(f"{name}_{_uid[0]}", list(shape), dt))

        cst = sb(es, "cst", [P, 5, P])
        ident = cst[:, 0, :]
        ones_f = sb(es, "ones_f", [P, P])
        ones_b = sb(es, "ones_b", [P, P], BF16)
        nrm = sb(es, "nrm", [P, 5, KT])
        lfT = sb(es, "lfT", [16, T])
        b_cst = ker.buf("cst", dma=True)
        b_nrm = ker.buf("nrm", dma=True)
        b_ones = ker.buf("ones")
        b_lfT = ker.buf("lfT")
        banks = [es.enter_context(nc.psum_tensor(f"bank{i}", [P, 512], F32)) for i in range(8)]
        bbank = [ker.buf(f"bank{i}") for i in range(8)]
        for b_ in bbank:
            b_.excl = True

        ker.dma(sp, cst[:], consts, writes=[b_cst], sbuf=b_cst)
        ker.dma(sp, nrm[:], norms, writes=[b_nrm], sbuf=b_nrm)
        ker.op(dve, lambda: nc.vector.memset(ones_f[:], 1.0), writes=[b_ones])
        ker.op(dve, lambda: nc.vector.memset(ones_b[:], 1.0), writes=[b_ones])

        acc_ring = [0]

        def next_acc():
            i = acc_ring[0] % 6
            acc_ring[0] += 1
            return banks[i], bbank[i]

        tr_ring = [0]

        def tr_bank():
            i = tr_ring[0] % 2
            tr_ring[0] += 1
            return banks[6 + i], bbank[6 + i]

        def transpose_group(srcs, dst_writer):
            bk, bb = tr_bank()
            for j, (in_ap, m, k) in enumerate(srcs):
                reads_j = in_ap[1]
                ker.op(pe, lambda: nc.tensor.transpose(bk[:m, j * P:j * P + k], in_ap[0], ident[:k, :k]),
                       reads=list(reads_j) + [b_cst], writes=[bb] if j == 0 else (), pwrites=() if j == 0 else [bb])
            dst_writer(bk, bb)

        ev_alt = [0]

        def evac(out_ap, in_ap, reads, pwrites, eng=None):
            if eng is None:
                eng = act if ev_alt[0] % 2 == 0 else dve
                ev_alt[0] += 1
            if eng is act:
                return ker.op(act, lambda: nc.scalar.copy(out=out_ap, in_=in_ap), reads=reads, pwrites=pwrites)
            return ker.op(dve, lambda: nc.vector.tensor_copy(out=out_ap, in_=in_ap), reads=reads, pwrites=pwrites)

        xres_v = xres.rearrange("kt p t -> p kt t")
        b_xres = [ker.buf(f"xres{kt}") for kt in range(KT)]
        with ExitStack() as ps:
            xin = [sb(ps, f"xin{i}", [P, D]) for i in range(2)]
            b_xin = [ker.buf(f"xin{i}", dma=True) for i in range(2)]
            xst = [sb(ps, f"xst{i}", [P, KT, P]) for i in range(2)]
            b_xst = [ker.buf(f"xst{i}", dma=True) for i in range(2)]
            for ti, (t0, tn) in enumerate(TT):
                s = ti % 2
                ker.dma(sp, xin[s][:tn, :], x_tok[t0:t0 + tn, :], writes=[b_xin[s]], sbuf=b_xin[s])
                for g4 in range(KT // 4):
                    srcs = [((xin[s][:tn, (g4 * 4 + j) * P:(g4 * 4 + j + 1) * P], [b_xin[s]]), P, tn) for j in range(4)]

                    def wr(bk, bb, g4=g4, s=s, tn=tn):
                        evac(xst[s][:, g4 * 4:g4 * 4 + 4, :tn], bk[:, :].rearrange("p (j n) -> p j n", n=P)[:, :, :tn], [bb], [b_xst[s]])
                    transpose_group(srcs, wr)
                ker.dma(sp, xres_v[:, :, t0:t0 + tn], xst[s][:, :, :tn], reads=[b_xst[s]], writes=b_xres, sbuf=b_xst[s])
            ker.barrier()

        def rmsnorm_to(ps, dst, b_dst, gi, out_f32_cb=None):
            NXS = 6
            xs = [sb(ps, f"xs{i}", [P, T]) for i in range(NXS)]
            b_xs = [ker.buf(f"xs{i}", dma=True) for i in range(NXS)]
            sq = [sb(ps, f"sq{i}", [P, T]) for i in range(2)]
            b_sq = [ker.buf(f"sq{i}") for i in range(2)]
            rstd = sb(ps, "rstd", [P, T])
            b_rstd = ker.buf("rstd")
            accs = [next_acc() for _ in range(3)]
            for kt in range(KT):
                s = kt % NXS
                ker.dma(sp, xs[s][:], xres_v[:, kt, :], reads=[b_xres[kt]], writes=[b_xs[s]], sbuf=b_xs[s])
                q2 = kt % 2
                ker.op(act, lambda: nc.scalar.activation(out=sq[q2][:], in_=xs[s][:], func=AF.Square),
                       reads=[b_xs[s]], writes=[b_sq[q2]])
                for ci, (c0, cn) in enumerate(CH):
                    bk, bb = accs[ci]
                    ker.op(pe, lambda: nc.tensor.matmul(bk[:, :cn], ones_f[:], sq[q2][:, c0:c0 + cn],
                                                        start=(kt == 0), stop=(kt == KT - 1)),
                           reads=[b_sq[q2], b_ones], writes=[bb], signal=(kt == KT - 1 or True))
            for ci, (c0, cn) in enumerate(CH):
                bk, bb = accs[ci]
                ker.op(act, lambda: nc.scalar.activation(out=rstd[:, c0:c0 + cn], in_=bk[:, :cn], func=AF.Sqrt,
                                                         scale=1.0 / D, bias=eps_t[:, 0:1]),
                       reads=[bb, b_ones], pwrites=[b_rstd])
            ker.op(dve, lambda: nc.vector.reciprocal(out=rstd[:], in_=rstd[:]), reads=[b_rstd], writes=[b_rstd])
            for kt in range(KT):
                s = kt % NXS
                ker.dma(sp, xs[s][:], xres_v[:, kt, :], reads=[b_xres[kt]], writes=[b_xs[s]], sbuf=b_xs[s])
                if out_f32_cb is None:
                    ker.op(dve, lambda: nc.vector.scalar_tensor_tensor(out=dst[:, kt, :], in0=xs[s][:], scalar=nrm[:, gi, kt:kt + 1],
                                                                      in1=rstd[:], op0=ALU.mult, op1=ALU.mult),
                           reads=[b_xs[s], b_rstd, b_nrm], pwrites=[b_dst])
                else:
                    out_f32_cb(kt, xs[s], b_xs[s], rstd, b_rstd)

        eps_t = sb(es, "eps_t", [P, 1])
        ker.op(dve, lambda: nc.vector.memset(eps_t[:], EPS), writes=[b_ones])

        WS_N = 4
        WSL = {}

        def alloc_wslots(ps):
            WSL["w"] = [sb(ps, f"wslot{i}", [P, 8192], BF16) for i in range(WS_N)]
            WSL["b"] = [ker.buf(f"wslot{i}", dma="sw") for i in range(WS_N)]

        def make_accum_epi(ost, b_ost):
            st = {"i": 0}

            def epi(tag, ci, c0, cn, m, bk, bb):
                o = tag[1]
                if ci == 0:
                    st["i"] += 1
                s = st["i"] % 2
                evac(ost[s][:, c0:c0 + cn], bk[:, :cn], [bb], [b_ost[s]])
                if ci == len(CH) - 1:
                    ker.dma(pool, xres_v[:, o, :], ost[s][:], reads=[b_ost[s]], writes=[b_xres[o]], sbuf=b_ost[s], accum_op=ALU.add)
            return epi

        def gemm(A_of, blocks, jobs, chunks, epilogue):
            wslots, b_wslots = WSL["w"], WSL["b"]
            nblk = len(blocks)
            loaded = [0]
            ring = [0]
            slot_of = {}

            def load_next():
                bi = loaded[0]
                if bi >= nblk:
                    return
                Wap, ktn, ncols = blocks[bi]
                s = ring[0] % WS_N
                ring[0] += 1
                slot_of[bi] = s
                dstv = wslots[s][:, 0:ktn * ncols].rearrange("p (kt n) -> p kt n", n=ncols)
                ker.dma(pool, dstv, Wap.rearrange("(kt p) n -> p kt n", p=P), writes=[b_wslots[s]], sbuf=b_wslots[s])
                loaded[0] += 1

            last_use = {}
            for ji, (bi, co, m, tag) in enumerate(jobs):
                last_use[bi] = ji
            for _ in range(min(WS_N - 1, nblk)):
                load_next()
            for ji, (bi, co, m, tag) in enumerate(jobs):
                while bi >= loaded[0]:
                    load_next()
                Wap, ktn, ncols = blocks[bi]
                s = slot_of[bi]
                wv = wslots[s][:, 0:ktn * ncols].rearrange("p (kt n) -> p kt n", n=ncols)
                for ci, (c0, cn) in enumerate(chunks):
                    bk, bb = next_acc()
                    for kt in range(ktn):
                        a_ap, a_b = A_of(tag, kt)
                        ker.op(pe, lambda: nc.tensor.matmul(bk[:m, :cn], wv[:, kt, co:co + m], a_ap[:, c0:c0 + cn],
                                                            start=(kt == 0), stop=(kt == ktn - 1)),
                               reads=[b_wslots[s], a_b], writes=[bb], signal=(kt == ktn - 1))
                    epilogue(tag, ci, c0, cn, m, bk, bb)
                if last_use[bi] == ji:
                    load_next()

        with ExitStack() as pA:
            A = sb(pA, "A", [P, KT, T], BF16)
            b_A = ker.buf("A")
            with ExitStack() as ps:
                rmsnorm_to(ps, A, b_A, 0)
                ker.barrier()
            if STAGE >= 1 and 'l0' not in SKIP:
                with ExitStack() as ps:
                    alloc_wslots(ps)
                    stg = [sb(ps, f"stg{i}", [P, T]) for i in range(2)]
                    b_stg = [ker.buf(f"stg{i}") for i in range(2)]
                    stb = [sb(ps, f"stb{i}", [P, T], BF16) for i in range(2)]
                    b_stb = [ker.buf(f"stb{i}", dma=True) for i in range(2)]
                    tko = [sb(ps, f"tko{i}", [P, 9, P]) for i in range(2)]
                    b_tko = [ker.buf(f"tko{i}", dma=True) for i in range(2)]
                    tkb = [sb(ps, f"tkb{i}", [P, 9, P], BF16) for i in range(2)]
                    b_tkb = [ker.buf(f"tkb{i}", dma=True) for i in range(2)]
                    nbf = sb(ps, "nbf", [16, 1])
                    b_nbf = ker.buf("nbf", dma=True)
                    lft = sb(ps, "lft", [P, 9, 16])
                    b_lft = ker.buf("lft", dma=True)
                    ker.dma(sp, nbf[:], fox_bf, writes=[b_nbf], sbuf=b_nbf)
                    ker.op(dve, lambda: nc.vector.tensor_scalar(out=nbf[:], in0=nbf[:], scalar1=-1.0, scalar2=None, op0=ALU.mult),
                           reads=[b_nbf], writes=[b_nbf])
                    ctr = [0]
                    cur = {}

                    def store_tokmajor(dst_dram, col0, src, b_src):
                        ker.dma(sp, dst_dram[0:TP, col0:col0 + P].rearrange("(tt p) n -> p tt n", p=P), src[:, 0:8, :],
                                reads=[b_src], sbuf=b_src)
                        ker.dma(sp, dst_dram[TP:T, col0:col0 + P], src[:TS, 8, :], reads=[b_src], sbuf=b_src)

                    def epi(tag, ci, c0, cn, m, bk, bb):
                        kind, idx = tag
                        if ci == 0:
                            cur["s"] = ctr[0] % 2
                            ctr[0] += 1
                        s = cur["s"]
                        if kind in ("u", "q"):
                            evac(stb[s][:, c0:c0 + cn], bk[:, :cn], [bb], [b_stb[s]])
                            if ci == 2:
                                dstd = uT_d if kind == "u" else qT_d
                                ker.dma(sp, dstd[idx], stb[s][:], reads=[b_stb[s]], sbuf=b_stb[s])
                            return
                        if kind == "f":
                            ker.op(act, lambda: nc.scalar.activation(out=lfT[:, c0:c0 + cn], in_=bk[:16, :cn], func=AF.Exp,
                                                                     scale=-1.0, bias=nbf[:, 0:1]),
                                   reads=[bb, b_nbf], pwrites=[b_lfT])
                            ker.op(act, lambda: nc.scalar.activation(out=lfT[:, c0:c0 + cn], in_=lfT[:, c0:c0 + cn], func=AF.Ln,
                                                                     scale=1.0, bias=ones_f[:16, 0:1]),
                                   reads=[b_lfT, b_ones], pwrites=[b_lfT])
                            ker.op(dve, lambda: nc.vector.tensor_scalar(out=lfT[:, c0:c0 + cn], in0=lfT[:, c0:c0 + cn], scalar1=-1.0,
                                                                        scalar2=None, op0=ALU.mult),
                                   reads=[b_lfT], pwrites=[b_lfT])
                            if ci == 2:
                                bk2, bb2 = tr_bank()
                                for ti, (t0, tn) in enumerate(TT):
                                    ker.op(pe, lambda: nc.tensor.transpose(bk2[:tn, ti * 16:(ti + 1) * 16], lfT[:, t0:t0 + tn], ident[:16, :16]),
                                           reads=[b_lfT, b_cst], writes=[bb2] if ti == 0 else (), pwrites=() if ti == 0 else [bb2])
                                evac(lft[:, 0:8, :], bk2[:, 0:128].rearrange("p (j n) -> p j n", n=16), [bb2], [b_lft])
                                evac(lft[:TS, 8, :], bk2[:TS, 128:144], [bb2], [b_lft])
                                ker.dma(sp, o_lf[0:TP, :].rearrange("(tt p) n -> p tt n", p=P), lft[:, 0:8, :], reads=[b_lft], sbuf=b_lft)
                                ker.dma(sp, o_lf[TP:T, :], lft[:TS, 8, :], reads=[b_lft], sbuf=b_lft)
                            return
                        evac(stg[s][:, c0:c0 + cn], bk[:, :cn], [bb], [b_stg[s]])
                        if kind == "k":
                            ker.op(dve, lambda: nc.vector.tensor_copy(out=stb[s][:, c0:c0 + cn], in_=stg[s][:, c0:c0 + cn]),
                                   reads=[b_stg[s]], pwrites=[b_stb[s]])
                        if ci != 2:
                            return
                        if kind == "k":
                            ker.dma(sp, kT_in[idx // 4][(idx % 4) * P:(idx % 4 + 1) * P, :], stb[s][:, 0:TP], reads=[b_stb[s]], sbuf=b_stb[s])
                            ker.dma(sp, kTs_d[idx], stb[s][:, TP:T], reads=[b_stb[s]], sbuf=b_stb[s])
                        for g0, g1 in ((0, 4), (4, 8), (8, 9)):
                            srcs = [((stg[s][:, TT[ti][0]:TT[ti][0] + TT[ti][1]], [b_stg[s]]), TT[ti][1], P) for ti in range(g0, g1)]

                            def wr(bk, bb, g0=g0, g1=g1, s=s, kind=kind):
                                if g0 < 8:
                                    src = bk[:, :].rearrange("p (j n) -> p j n", n=P)
                                    if kind in ("k", "v"):
                                        ker.op(act, lambda: nc.scalar.copy(out=tko[s][:, g0:g1, :], in_=src), reads=[bb], pwrites=[b_tko[s]])
                                    if kind in ("va", "v"):
                                        ker.op(dve, lambda: nc.vector.tensor_copy(out=tkb[s][:, g0:g1, :], in_=src), reads=[bb], pwrites=[b_tkb[s]])
                                else:
                                    ker.op(act, lambda: nc.scalar.copy(out=tko[s][:TS, 8, :], in_=bk[:TS, 0:P]), reads=[bb], pwrites=[b_tko[s]])
                                    if kind in ("va", "v"):
                                        ker.op(dve, lambda: nc.vector.tensor_copy(out=tkb[s][:TS, 8, :], in_=bk[:TS, 0:P]), reads=[bb], pwrites=[b_tkb[s]])
                            transpose_group(srcs, wr)
                        col0 = idx * P
                        if kind == "k":
                            store_tokmajor(o_k, col0, tko[s], b_tko[s])
                        elif kind == "v":
                            store_tokmajor(o_v, col0, tko[s], b_tko[s])
                            ker.dma(sp, v_in[idx // 4][:, (idx % 4) * P:(idx % 4 + 1) * P].rearrange("(tt p) n -> p tt n", p=P), tkb[s][:, 0:8, :],
                                    reads=[b_tkb[s]], sbuf=b_tkb[s])
                            ker.dma(sp, vs_d[:, col0:col0 + P], tkb[s][:TS, 8, :], reads=[b_tkb[s]], sbuf=b_tkb[s])
                        else:
                            ker.dma(sp, o_gv[:, col0:col0 + P], tko[s][:TS, 8, :], reads=[b_tko[s]], sbuf=b_tko[s])
                            ker.dma(sp, va_d[0:TP, col0:col0 + P].rearrange("(tt p) n -> p tt n", p=P), tkb[s][:, 0:8, :],
                                    reads=[b_tkb[s]], sbuf=b_tkb[s])
                            ker.dma(sp, va_d[TP:T, col0:col0 + P], tkb[s][:TS, 8, :], reads=[b_tkb[s]], sbuf=b_tkb[s])

                    ab_w_in = ab_w_in_l()
                    blocks, jobs = [], []
                    kinds = ["u"] * 16 + ["va"] * 16 + ["q"] * 16 + ["k"] * 16 + ["v"] * 16
                    for b in range(40):
                        blocks.append((ab_w_in[:, b * 256:(b + 1) * 256], KT, 256))
                        for j in range(2):
                            oi = b * 2 + j
                            jobs.append((b, j * 128, 128, (kinds[oi], oi % 16)))
                    blocks.append((ab_w_in[:, 10128:10256], KT, 128))
                    jobs.append((40, 112, 16, ("f", 0)))
                    if os.environ.get("KG1"):
                        keep = os.environ["KG1"].split(",")
                        jobs = [j for j in jobs if j[3][0] in keep and j[3][1] < int(os.environ.get("KG1N", "16"))]
                    gemm(lambda tag, kt: (A[:, kt, :], b_A), blocks, jobs, CH, epi)
                    ker.barrier()

        coll_sems = [ker.new_sem(f"cc{i}") for i in range(14)]
        coll_i = [0]

        def all_gather(src, dst, reads, writes, pwrites=()):
            k = coll_sems[coll_i[0]]
            coll_i[0] += 1
            ker._deps(pool, reads, writes, pwrites)
            ins = nc.gpsimd.collective_compute("AllGather", ALU.bypass, replica_groups=GROUPS,
                                               ins=[src.opt()], outs=[dst.opt()])
            ins.then_inc(ker.sems[k], 1)
            ker.issued[k] += 1
            ker._record((k, 1), reads, writes, pwrites)

        b_gath = {n: ker.buf(n) for n in ("kT_all", "v_all", "c_all", "sum_all")}
        if STAGE >= 2 and 'l0' not in SKIP:
          with ExitStack() as pB:
            aoT = sb(pB, "aoT", [P, KT, T], BF16)
            b_aoT = ker.buf("aoT")
            for j in range(4):
                all_gather(kT_in[j], kT_all[j], [], [b_gath["kT_all"]] if j == 0 else [], [] if j == 0 else [b_gath["kT_all"]])
                all_gather(v_in[j], v_all[j], [], [b_gath["v_all"]] if j == 0 else [], [] if j == 0 else [b_gath["v_all"]])
            with ExitStack() as ps:
                wsT = sb(ps, "wsT", [P, 8, P], BF16)
                b_wsT = ker.buf("wsT")
                wraw = sb(ps, "wraw", [P, 8, P])
                b_wraw = ker.buf("wraw", dma=True)
                bbt = sb(ps, "bbt", [P, 8, P])
                b_bbt = ker.buf("bbt", dma=True)
                vat = [sb(ps, f"vat{i}", [P, 2048], BF16) for i in range(2)]
                b_vat = [ker.buf(f"vat{i}", dma=True) for i in range(2)]
                ut = [sb(ps, f"ut{i}", [P, 16, P], BF16) for i in range(2)]
                b_ut = [ker.buf(f"ut{i}", dma=True) for i in range(2)]
                gtmp = [sb(ps, f"gtmp{i}", [P, P]) for i in range(2)]
                b_gtmp = [ker.buf(f"gtmp{i}") for i in range(2)]
                ker.dma(sp, wraw[:], gmlp_ws.rearrange("g r s -> r g s"), writes=[b_wraw], sbuf=b_wraw)
                ker.dma(sp, bbt[:], gmlp_bb, writes=[b_bbt], sbuf=b_bbt)
                for g in range(8):
                    ker.op(dve, lambda: nc.vector.tensor_tensor(out=wraw[:, g, :], in0=wraw[:, g, :], in1=cst[:, 1, :], op=ALU.mult),
                           reads=[b_wraw, b_cst], writes=[b_wraw])
                for gg in range(2):
                    srcs = [((wraw[:, gg * 4 + j, :], [b_wraw]), P, P) for j in range(4)]

                    def wr(bk, bb, gg=gg):
                        evac(wsT[:, gg * 4:gg * 4 + 4, :], bk[:, :].rearrange("p (j n) -> p j n", n=P), [bb], [b_wsT])
                    transpose_group(srcs, wr)
                uT_v = uT_d.rearrange("f p t -> p f t")
                for ti, (t0, tn) in enumerate(TT):
                    s2 = ti % 2
                    ker.dma(sp, vat[s2][:tn, :], va_d[t0:t0 + tn, :], writes=[b_vat[s2]], sbuf=b_vat[s2])
                    ker.dma(sp, ut[s2][:, :, :tn], uT_v[:, :, t0:t0 + tn], writes=[b_ut[s2]], sbuf=b_ut[s2])
                    for ft in range(16):
                        g = ft // 2
                        bk, bb = next_acc()
                        ker.op(pe, lambda: nc.tensor.matmul(bk[:, :tn], vat[s2][:tn, ft * P:(ft + 1) * P], wsT[:tn, g, :tn], start=True, stop=True),
                               reads=[b_vat[s2], b_wsT], writes=[bb])
                        s3 = ft % 2
                        ker.op(dve, lambda: nc.vector.tensor_tensor(out=gtmp[s3][:, :tn], in0=bk[:, :tn], in1=bbt[:, g, :tn], op=ALU.add),
                               reads=[bb, b_bbt], writes=[b_gtmp[s3]])
                        ker.op(dve, lambda: nc.vector.tensor_tensor(out=aoT[:, ft, t0:t0 + tn], in0=gtmp[s3][:, :tn], in1=ut[s2][:, ft, :tn], op=ALU.mult),
                               reads=[b_gtmp[s3], b_ut[s2]], pwrites=[b_aoT])
                ker.barrier()
            if STAGE >= 3:
             with ExitStack() as psT:
              biasT = sb(psT, "biasT", [P, 2, 32, 16])
              biasS = sb(psT, "biasS", [P, 9, 16])
              qpb = sb(psT, "qpb", [P, T])
              kps = sb(psT, "kps", [P, 40])
              ps = ExitStack()
              ps.__enter__()
              if True:
                one16 = sb(ps, "one16", [16, T])
                b_one16 = ker.buf("one16")
                cs = sb(ps, "cs", [16, TP])
                b_cs = ker.buf("cs", dma=True)
                lfs = sb(ps, "lfs", [16, T])
                b_lfs = ker.buf("lfs")
                css = sb(ps, "css", [16, T])
                b_css = ker.buf("css")
                clf = sb(ps, "clf", [P, 8, 16])
                b_clf = ker.buf("clf", dma=True)
                cg = sb(ps, "cg", [16, 4, TP])
                b_cg = ker.buf("cg", dma=True)
                offs = sb(ps, "offs", [16, 4])
                b_offs = ker.buf("offs")
                cql = sb(ps, "cql", [16, TP])
                b_cql = ker.buf("cql")
                pmk = sb(ps, "pmk", [P, 8])
                b_pmk = ker.buf("pmk", dma=True)
                ckT = sb(ps, "ckT", [P, 33, 16])
                b_ckT = ker.buf("ckT")
                ckTs = sb(ps, "ckTs", [P, 9, 16])
                b_ckTs = ker.buf("ckTs")
                cqe = sb(ps, "cqe", [P, 3, 16])
                b_cqe = ker.buf("cqe")
                cref = sb(ps, "cref", [P, 3, 16])
                b_cref = ker.buf("cref")
                b_biasT = ker.buf("biasT")
                b_biasS = ker.buf("biasS")
                b_qpb = ker.buf("qpb", dma=True)
                b_kps = ker.buf("kps", dma=True)
                ker.dma(sp, pmk[:], pmask_b, writes=[b_pmk], sbuf=b_pmk)
                ker.dma(sp, qpb[:], qpos_b, writes=[b_qpb], sbuf=b_qpb)
                ker.dma(sp, kps[:], kpos, writes=[b_kps], sbuf=b_kps)
                ker.dma(sp, clf[:], cache_lf.rearrange("(kt p) h -> p kt h", p=P), writes=[b_clf], sbuf=b_clf)
                ker.op(dve, lambda: nc.vector.memset(one16[:], 1.0), writes=[b_one16])
                ker.op(dve, lambda: nc.vector.tensor_tensor_scan(out=cs[:], data0=one16[:, 0:TP], data1=lfT[:, 0:TP], initial=0.0,
                                                               op0=ALU.mult, op1=ALU.add),
                       reads=[b_one16, b_lfT], writes=[b_cs])
                b_c_in = ker.buf("c_in")
                ker.dma(sp, c_in, cs[:], reads=[b_cs], writes=[b_c_in], sbuf=b_cs)
                all_gather(c_in, c_all, [b_c_in], [b_gath["c_all"]])
                bk2, bb2 = tr_bank()
                for kt in range(8):
                    ker.op(pe, lambda: nc.tensor.transpose(bk2[:16, kt * P:(kt + 1) * P] if kt < 4 else bk2[:16, (kt - 4) * P:(kt - 3) * P],
                                                           clf[:, kt, :], ident[:, :]),
                           reads=[b_clf, b_cst], writes=[bb2] if kt % 4 == 0 else (), pwrites=() if kt % 4 == 0 else [bb2])
                    if kt % 4 == 3:
                        evac(lfs[:, (kt - 3) * P:(kt + 1) * P], bk2[:16, :], [bb2], [b_lfs])
                        if kt == 3:
                            bk2, bb2 = tr_bank()
                ker.op(dve, lambda: nc.vector.tensor_copy(out=lfs[:, TP:T], in_=lfT[:, TP:T]), reads=[b_lfT], pwrites=[b_lfs])
                ker.op(dve, lambda: nc.vector.tensor_tensor_scan(out=css[:], data0=one16[:], data1=lfs[:], initial=0.0,
                                                               op0=ALU.mult, op1=ALU.add),
                       reads=[b_one16, b_lfs], writes=[b_css])
                ker.dma(sp, cg[:], c_all.rearrange("(r h) t -> h r t", h=16), reads=[b_gath["c_all"]], writes=[b_cg], sbuf=b_cg)
                ker.op(dve, lambda: nc.vector.tensor_copy(out=offs[:, 1:2], in_=cg[:, 0, TP - 1:TP]), reads=[b_cg], writes=[b_offs])
                for r in (2, 3):
                    ker.op(dve, lambda: nc.vector.tensor_tensor(out=offs[:, r:r + 1], in0=offs[:, r - 1:r], in1=cg[:, r - 1, TP - 1:TP], op=ALU.add),
                           reads=[b_cg, b_offs], writes=[b_offs])
                for r in (1, 2, 3):
                    ker.op(dve, lambda: nc.vector.tensor_scalar(out=cg[:, r, :], in0=cg[:, r, :], scalar1=offs[:, r:r + 1], scalar2=None, op0=ALU.add),
                           reads=[b_cg, b_offs], writes=[b_cg])
                ker.op(dve, lambda: nc.vector.tensor_scalar(out=cql[:], in0=cg[:, 0, :], scalar1=pmk[:16, 4:5], scalar2=None, op0=ALU.mult),
                       reads=[b_cg, b_pmk], writes=[b_cql])
                for r in (1, 2, 3):
                    ker.op(dve, lambda: nc.vector.scalar_tensor_tensor(out=cql[:], in0=cg[:, r, :], scalar=pmk[:16, 4 + r:5 + r], in1=cql[:],
                                                                      op0=ALU.mult, op1=ALU.add),
                           reads=[b_cg, b_pmk, b_cql], writes=[b_cql])
                bk2, bb2 = tr_bank()
                for kt in range(32):
                    ker.op(pe, lambda: nc.tensor.transpose(bk2[:, kt * 16:(kt + 1) * 16], cg[:, kt // 8, (kt % 8) * P:(kt % 8 + 1) * P], ident[:16, :16]),
                           reads=[b_cg, b_cst], writes=[bb2] if kt == 0 else (), pwrites=() if kt == 0 else [bb2])
                evac(ckT[:, 0:32, :], bk2[:, :].rearrange("p (j n) -> p j n", n=16), [bb2], [b_ckT])
                bk2, bb2 = tr_bank()
                ker.op(dve, lambda: nc.vector.memset(ckTs[:], 0.0), writes=[b_ckTs])
                ker.op(dve, lambda: nc.vector.memset(cqe[:], 0.0), writes=[b_cqe])
                for kt in range(9):
                    tn = 128 if kt < 8 else 16
                    ker.op(pe, lambda: nc.tensor.transpose(bk2[:tn, kt * 16:(kt + 1) * 16], css[:, kt * P:kt * P + tn], ident[:16, :16]),
                           reads=[b_css, b_cst], writes=[bb2] if kt == 0 else (), pwrites=() if kt == 0 else [bb2])
                for j, tl in enumerate((3, 7)):
                    ker.op(pe, lambda: nc.tensor.transpose(bk2[:, (9 + j) * 16:(10 + j) * 16], cql[:, tl * P:(tl + 1) * P], ident[:16, :16]),
                           reads=[b_cql, b_cst], pwrites=[bb2])
                evac(ckTs[:, 0:8, :], bk2[:, 0:128].rearrange("p (j n) -> p j n", n=16), [bb2], [b_ckTs])
                evac(ckTs[:16, 8, :], bk2[:16, 128:144], [bb2], [b_ckTs])
                evac(cqe[:, 0:2, :], bk2[:, 144:176].rearrange("p (j n) -> p j n", n=16), [bb2], [b_cqe])
                ker.op(dve, lambda: nc.vector.tensor_copy(out=cqe[:16, 2, :], in_=ckTs[:16, 8, :]), reads=[b_ckTs], pwrites=[b_cqe])
                bk3, bb3 = next_acc()
                for j in range(3):
                    selm = cst[:, 3, :] if j < 2 else cst[:, 4, :]
                    ker.op(pe, lambda: nc.tensor.matmul(bk3[:, j * 16:(j + 1) * 16], selm, cqe[:, j, :], start=True, stop=True),
                           reads=[b_cqe, b_cst], writes=[bb3] if j == 0 else (), pwrites=() if j == 0 else [bb3])
                evac(cref[:, :, :], bk3[:, 0:48].rearrange("p (j n) -> p j n", n=16), [bb3], [b_cref])
                for qb in range(2):
                    for kt in range(32):
                        ker.op(dve, lambda: nc.vector.tensor_tensor(out=biasT[:, qb, kt, :], in0=cref[:, qb, :], in1=ckT[:, kt, :], op=ALU.subtract),
                               reads=[b_cref, b_ckT], pwrites=[b_biasT])
                ker.op(dve, lambda: nc.vector.tensor_scalar(out=biasT[:], in0=biasT[:], scalar1=0.0, scalar2=None, op0=ALU.min),
                       reads=[b_biasT], writes=[b_biasT])
                for kt in range(9):
                    ker.op(dve, lambda: nc.vector.tensor_tensor(out=biasS[:, kt, :], in0=cref[:, 2, :], in1=ckTs[:, kt, :], op=ALU.subtract),
                           reads=[b_cref, b_ckTs], pwrites=[b_biasS])
                ker.op(dve, lambda: nc.vector.tensor_scalar(out=biasS[:], in0=biasS[:], scalar1=0.0, scalar2=None, op0=ALU.min),
                       reads=[b_biasS], writes=[b_biasS])

                ker.barrier()
                ps.__exit__(None, None, None)
                ps = psT
                SCALE = 128.0 ** -0.5
                pf = [sb(ps, f"pf{i}", [P, 512]) for i in range(2)]
                b_pf = [ker.buf(f"pf{i}") for i in range(2)]
                pm = [sb(ps, f"pm{i}", [P, 512], BF16) for i in range(2)]
                b_pm = [ker.buf(f"pm{i}") for i in range(2)]
                rl = sb(ps, "rl", [P, 512])
                b_rl = ker.buf("rl")

                def attention(qh_ap, b_qh, q0, nq, keys, qp0, out_ap, olb):
                    bO, bbO, bL, bbL = olb
                    nk = len(keys)
                    pend = None

                    def tail(it):
                        kt, i = it
                        _, v_ap, bufs, _, _ = keys[kt]
                        ker.op(pe, lambda: nc.tensor.matmul(bO[:, :nq], v_ap, pm[i][:, :nq], start=(kt == 0), stop=(kt == nk - 1)),
                               reads=[b_pm[i]] + bufs, writes=[bbO], signal=(kt == nk - 1))
                        ker.op(pe, lambda: nc.tensor.matmul(bL[:, :nq], ones_b[:, :], pm[i][:, :nq], start=(kt == 0), stop=(kt == nk - 1)),
                               reads=[b_pm[i], b_ones], writes=[bbL], signal=True)
                    for kt in range(nk):
                        kT_ap, v_ap, bufs, kp_ap, bias_ap = keys[kt]
                        i = kt % 2
                        bS, bbS = banks[i], bbank[i]
                        ker.op(pe, lambda: nc.tensor.matmul(bS[:, :nq], kT_ap, qh_ap[:, q0:q0 + nq], start=True, stop=True),
                               reads=[b_qh] + bufs, writes=[bbS])
                        if pend is not None:
                            tail(pend)
                        ker.op(act, lambda: nc.scalar.activation(out=pf[i][:, :nq], in_=bS[:, :nq], func=AF.Exp, scale=SCALE, bias=bias_ap),
                               reads=[bbS, b_biasT, b_biasS], writes=[b_pf[i]])
                        ker.op(dve, lambda: nc.vector.scalar_tensor_tensor(out=pm[i][:, :nq], in0=qpb[:, qp0:qp0 + nq], scalar=kp_ap, in1=pf[i][:, :nq],
                                                                          op0=ALU.is_ge, op1=ALU.mult),
                               reads=[b_pf[i], b_qpb, b_kps], writes=[b_pm[i]])
                        pend = (kt, i)
                    tail(pend)
                    ker.op(dve, lambda: nc.vector.reciprocal(out=rl[:, :nq], in_=bL[:, :nq]), reads=[bbL], writes=[b_rl])
                    ker.op(dve, lambda: nc.vector.tensor_tensor(out=out_ap, in0=bO[:, :nq], in1=rl[:, :nq], op=ALU.mult),
                           reads=[bbO, b_rl], pwrites=[b_aoT])

                olbs = [(banks[2], bbank[2], banks[3], bbank[3]), (banks[4], bbank[4], banks[5], bbank[5])]
                olb_i = [0]
                ps = ExitStack()
                ps.__enter__()
                kTh = [sb(ps, f"kTh{i}", [P, 4, TP], BF16) for i in range(2)]
                b_kTh = [ker.buf(f"kTh{i}", dma=True) for i in range(2)]
                vh = [sb(ps, f"vh{i}", [P, 32, P], BF16) for i in range(2)]
                b_vh = [ker.buf(f"vh{i}", dma=True) for i in range(2)]
                qh = [sb(ps, f"qh{i}", [P, T], BF16) for i in range(2)]
                b_qh = [ker.buf(f"qh{i}", dma=True) for i in range(2)]
                kT_all_v = [t_.rearrange("(r h d) t -> h d r t", h=4, d=P) for t_ in kT_all]
                v_all_v = [t_.rearrange("(kt p) n -> p kt n", p=P) for t_ in v_all]

                def load_head(h):
                    s2 = h % 2
                    ker.dma(sp, kTh[s2][:], kT_all_v[h // 4][h % 4], reads=[b_gath["kT_all"]], writes=[b_kTh[s2]], sbuf=b_kTh[s2])
                    ker.dma(sp, vh[s2][:], v_all_v[h // 4][:, :, (h % 4) * P:(h % 4 + 1) * P], reads=[b_gath["v_all"]], writes=[b_vh[s2]], sbuf=b_vh[s2])
                    ker.dma(sp, qh[s2][:], qT_d[h], writes=[b_qh[s2]], sbuf=b_qh[s2])
                load_head(0)
                for h in range(16):
                    s2 = h % 2
                    if h + 1 < 16:
                        load_head(h + 1)
                    for qb in range(2):
                        keys = [(kTh[s2][:, kt // 8, (kt % 8) * P:(kt % 8 + 1) * P], vh[s2][:, kt, :], [b_kTh[s2], b_vh[s2]],
                                 kps[:, kt:kt + 1], biasT[:, qb, kt, h:h + 1]) for kt in range(32)]
                        attention(qh[s2], b_qh[s2], qb * 512, 512, keys, qb * 512, aoT[:, 16 + h, qb * 512:(qb + 1) * 512], olbs[olb_i[0] % 2])
                        olb_i[0] += 1
                ker.barrier()
                ps.__exit__(None, None, None)
                psS = ExitStack()
                psS.__enter__()
                kTs_all = sb(psS, "kTs_all", [P, 16, 1152], BF16)
                b_kTs = ker.buf("kTs_all", dma=True)
                vS_all = sb(psS, "vS_all", [P, 9, 2048], BF16)
                b_vS = ker.buf("vS_all", dma="sw")
                ckin = [sb(psS, f"ckin{i}", [P, 2048]) for i in range(2)]
                b_ckin = [ker.buf(f"ckin{i}", dma=True) for i in range(2)]
                qs = sb(psS, "qs", [P, 16, TS], BF16)
                b_qs = ker.buf("qs", dma=True)
                ker.op(dve, lambda: nc.vector.memset(kTs_all[:, :, TP:1152], 0.0), writes=[b_kTs])
                ker.op(dve, lambda: nc.vector.memset(vS_all[:, 8, :], 0.0), writes=[b_vS])
                ker.dma(sp, kTs_all[:, :, TP:T], kTs_d.rearrange("h d t -> d h t"), pwrites=[b_kTs], sbuf=b_kTs)
                ker.dma(pool, vS_all[:, 0:8, :], cache_v.rearrange("(kt p) n -> p kt n", p=P), pwrites=[b_vS], sbuf=b_vS)
                ker.dma(sp, vS_all[:TS, 8, :], vs_d, pwrites=[b_vS], sbuf=b_qs)
                ker.dma(sp, qs[:], qT_d.rearrange("h d t -> d h t")[:, :, TP:T], writes=[b_qs], sbuf=b_qs)
                for kt in range(8):
                    s2 = kt % 2
                    ker.dma(sp, ckin[s2][:], cache_k[kt * P:(kt + 1) * P, :], writes=[b_ckin[s2]], sbuf=b_ckin[s2])
                    for g4 in range(4):
                        srcs = [((ckin[s2][:, (g4 * 4 + j) * P:(g4 * 4 + j + 1) * P], [b_ckin[s2]]), P, P) for j in range(4)]

                        def wr(bk, bb, g4=g4, kt=kt):
                            evac(kTs_all[:, g4 * 4:g4 * 4 + 4, kt * P:(kt + 1) * P], bk[:, :].rearrange("p (j n) -> p j n", n=P), [bb], [b_kTs])
                        transpose_group(srcs, wr)
                for h in range(16):
                    keys = [(kTs_all[:, h, kt * P:(kt + 1) * P], vS_all[:, kt, h * P:(h + 1) * P], [b_kTs, b_vS],
                             kps[:, kt:kt + 1] if kt < 8 else kps[:, 32:33], biasS[:, kt, h:h + 1]) for kt in range(9)]
                    attention(qs[:, h, :], b_qs, 0, TS, keys, TP, aoT[:, 16 + h, TP:T], olbs[olb_i[0] % 2])
                    olb_i[0] += 1
                ker.barrier()
                psS.__exit__(None, None, None)
                if os.environ.get("KDEBUG"):
                    b_dbg = ker.buf("dbg", dma=True)
                    ker.dma(sp, dt_tmp("dbg_ao", [KT, P, T], BF16).rearrange("k p t -> p k t"), aoT[:], reads=[b_aoT], sbuf=b_dbg)
                    ker.barrier()

            if STAGE >= 4:
              with ExitStack() as ps:
                alloc_wslots(ps)
                ost = [sb(ps, f"ost{i}", [P, T]) for i in range(2)]
                b_ost = [ker.buf(f"ost{i}", dma="sw") for i in range(2)]
                ab_w_out = ab_w_out_l()
                blocks = [(ab_w_out[:, b * 256:(b + 1) * 256], KT, 256) for b in range(16)]
                jobs = [(b, j * P, P, ("o", b * 2 + j)) for b in range(16) for j in range(2)]
                accum_epilogue = make_accum_epi(ost, b_ost)
                gemm(lambda tag, kt: (aoT[:, kt, :], b_aoT), blocks, jobs, CH, accum_epilogue)
                ker.barrier()

        def ffn(layer):
            with ExitStack() as pA:
                A = sb(pA, "Af", [P, KT, T], BF16)
                b_A = ker.buf("Af")
                with ExitStack() as ps:
                    rmsnorm_to(ps, A, b_A, 1 + 2 * layer)
                    ker.barrier()
                with ExitStack() as ps:
                    alloc_wslots(ps)
                    hid = sb(ps, "hid", [P, 8, T], BF16)
                    b_hid = [ker.buf(f"hid{j}") for j in range(8)]
                    sg = [sb(ps, f"sg{i}", [P, T]) for i in range(2)]
                    b_sg = [ker.buf(f"sg{i}") for i in range(2)]
                    ost = [sb(ps, f"ost{i}", [P, T]) for i in range(2)]
                    b_ost = [ker.buf(f"ost{i}", dma="sw") for i in range(2)]
                    acc_epi = make_accum_epi(ost, b_ost)
                    wg, wu, wd = ffn_wg[layer], ffn_wu[layer], ffn_wd[layer]
                    NJ = D_FF // P
                    blocks, jobs = [], []
                    for j0 in range(0, NJ, 8):
                        nj = min(8, NJ - j0)
                        for jj in range(0, nj, 2):
                            c0 = (j0 + jj) * P
                            nt = min(2, nj - jj)
                            bg = len(blocks)
                            blocks.append((wg[:, c0:c0 + nt * P], KT, nt * P))
                            blocks.append((wu[:, c0:c0 + nt * P], KT, nt * P))
                            for t2 in range(nt):
                                jobs.append((bg, t2 * P, P, ("g", jj + t2)))
                                jobs.append((bg + 1, t2 * P, P, ("u", jj + t2)))
                        for b4 in range(4):
                            bd = len(blocks)
                            blocks.append((wd[j0 * P:(j0 + nj) * P, b4 * 1024:(b4 + 1) * 1024], nj, 1024))
                            for o8 in range(8):
                                jobs.append((bd, o8 * P, P, ("d", b4 * 8 + o8)))
                    cur = {"i": 0}

                    def A_of(tag, kt):
                        if tag[0] == "d":
                            return hid[:, kt, :], b_hid[kt]
                        return A[:, kt, :], b_A

                    def epi(tag, ci, c0, cn, m, bk, bb):
                        kind, idx = tag
                        if kind == "d":
                            return acc_epi(tag, ci, c0, cn, m, bk, bb)
                        if kind == "g":
                            if ci == 0:
                                cur["i"] += 1
                            s = cur["i"] % 2
                            ker.op(act, lambda: nc.scalar.activation(out=sg[s][:, c0:c0 + cn], in_=bk[:, :cn], func=AF.Silu),
                                   reads=[bb], pwrites=[b_sg[s]] if ci > 0 else (), writes=[b_sg[s]] if ci == 0 else ())
                        else:
                            s = cur["i"] % 2
                            ker.op(dve, lambda: nc.vector.tensor_tensor(out=hid[:, idx, c0:c0 + cn], in0=bk[:, :cn], in1=sg[s][:, c0:c0 + cn], op=ALU.mult),
                                   reads=[bb, b_sg[s]], pwrites=[b_hid[idx]])
                    gemm(A_of, blocks, jobs, CH, epi)
                    ker.barrier()

        if STAGE >= 5 and 'ffn0' not in SKIP:
            ffn(0)

        def layer1():
            mq_d = dt_tmp("mq_d", [16, P, T], BF16)
            mkT_d = dt_tmp("mkT_d", [16, P, T], BF16)
            mk_d = dt_tmp("mk_d", [T, 2048], BF16)
            mv_d = dt_tmp("mv_d", [T, 4096], BF16)
            mo_d = dt_tmp("mo_d", [32, P, T])
            sumC_in = [dt_tmp(f"sumC_in{j}", [2 * DK, DV]) for j in range(4)]
            sumC_all = [dt_tmp(f"sumC_all{j}", [4 * 2 * DK, DV]) for j in range(4)]
            sumS_in = dt_tmp("sumS_in", [5, 512])
            sumS_all = dt_tmp("sumS_all", [4 * 5, 512])
            c_w_in = c_w_in_l()
            c_w_out = c_w_out_l()
            esel_d = dt_in("esel", [8, 8 * P])
            st_m_c = dt_in("st_m_c", [8, 1])
            pL = ExitStack()
            pL.__enter__()
            igT = sb(pL, "igT", [8, T])
            lfc = sb(pL, "lfc", [8, T])
            b_igT = ker.buf("igT")
            b_lfc = ker.buf("lfc")
            with ExitStack() as pA:
                A = sb(pA, "A1", [P, KT, T], BF16)
                b_A = ker.buf("A1")
                with ExitStack() as ps:
                    rmsnorm_to(ps, A, b_A, 2)
                    ker.barrier()
                with ExitStack() as ps:
                    alloc_wslots(ps)
                    stg = [sb(ps, f"stg{i}", [P, T]) for i in range(2)]
                    b_stg = [ker.buf(f"stg{i}", dma=True) for i in range(2)]
                    stb = [sb(ps, f"stb{i}", [P, T], BF16) for i in range(2)]
                    b_stb = [ker.buf(f"stb{i}", dma=True) for i in range(2)]
                    tkb = [sb(ps, f"tkb{i}", [P, 9, P], BF16) for i in range(2)]
                    b_tkb = [ker.buf(f"tkb{i}", dma=True) for i in range(2)]
                    bi_t = sb(ps, "bi_t", [8, 1])
                    nbf_t = sb(ps, "nbf_t", [8, 1])
                    b_bif = ker.buf("bif", dma=True)
                    ker.dma(sp, bi_t[:], c_bif[0:8, :], pwrites=[b_bif], sbuf=b_bif)
                    ker.dma(sp, nbf_t[:], c_bif[8:16, :], pwrites=[b_bif], sbuf=b_bif)
                    ker.op(dve, lambda: nc.vector.tensor_scalar(out=nbf_t[:], in0=nbf_t[:], scalar1=-1.0, scalar2=None, op0=ALU.mult),
                           reads=[b_bif], writes=[b_bif])
                    ctr = [0]
                    cur = {}

                    def epi(tag, ci, c0, cn, m, bk, bb):
                        kind, idx = tag
                        if ci == 0:
                            cur["s"] = ctr[0] % 2
                            ctr[0] += 1
                        s = cur["s"]
                        if kind == "q":
                            evac(stb[s][:, c0:c0 + cn], bk[:, :cn], [bb], [b_stb[s]])
                            if ci == 2:
                                ker.dma(sp, mq_d[idx], stb[s][:], reads=[b_stb[s]], sbuf=b_stb[s])
                            return
                        if kind == "o":
                            ker.op(act, lambda: nc.scalar.activation(out=stg[s][:, c0:c0 + cn], in_=bk[:, :cn], func=AF.Sigmoid),
                                   reads=[bb], pwrites=[b_stg[s]])
                            if ci == 2:
                                ker.dma(sp, mo_d[idx], stg[s][:], reads=[b_stg[s]], sbuf=b_stg[s])
                            return
                        if kind == "i":
                            ker.op(act, lambda: nc.scalar.activation(out=igT[:, c0:c0 + cn], in_=bk[:8, :cn], func=AF.Identity,
                                                                     scale=1.0, bias=bi_t[:, 0:1]),
                                   reads=[bb, b_bif], pwrites=[b_igT])
                            return
                        if kind == "f":
                            ker.op(act, lambda: nc.scalar.activation(out=lfc[:, c0:c0 + cn], in_=bk[:8, :cn], func=AF.Exp, scale=-1.0, bias=nbf_t[:, 0:1]),
                                   reads=[bb, b_bif], pwrites=[b_lfc])
                            ker.op(act, lambda: nc.scalar.activation(out=lfc[:, c0:c0 + cn], in_=lfc[:, c0:c0 + cn], func=AF.Ln, scale=1.0, bias=ones_f[:8, 0:1]),
                                   reads=[b_lfc, b_ones], pwrites=[b_lfc])
                            ker.op(dve, lambda: nc.vector.tensor_scalar(out=lfc[:, c0:c0 + cn], in0=lfc[:, c0:c0 + cn], scalar1=-1.0, scalar2=None, op0=ALU.mult),
                                   reads=[b_lfc], pwrites=[b_lfc])
                            return
                        if kind == "k":
                            ker.op(act, lambda: nc.scalar.mul(out=stg[s][:, c0:c0 + cn], in_=bk[:, :cn], mul=DK ** -0.5), reads=[bb], pwrites=[b_stg[s]])
                            ker.op(dve, lambda: nc.vector.tensor_copy(out=stb[s][:, c0:c0 + cn], in_=stg[s][:, c0:c0 + cn]),
                                   reads=[b_stg[s]], pwrites=[b_stb[s]])
                        else:
                            evac(stg[s][:, c0:c0 + cn], bk[:, :cn], [bb], [b_stg[s]])
                        if ci != 2:
                            return
                        if kind == "k":
                            ker.dma(sp, mkT_d[idx], stb[s][:], reads=[b_stb[s]], sbuf=b_stb[s])
                        for g0, g1 in ((0, 4), (4, 8), (8, 9)):
                            srcs = [((stg[s][:, TT[ti][0]:TT[ti][0] + TT[ti][1]], [b_stg[s]]), TT[ti][1], P) for ti in range(g0, g1)]

                            def wr(bk2, bb2, g0=g0, g1=g1, s=s):
                                if g0 < 8:
                                    evac(tkb[s][:, g0:g1, :], bk2[:, :].rearrange("p (j n) -> p j n", n=P), [bb2], [b_tkb[s]])
                                else:
                                    evac(tkb[s][:TS, 8, :], bk2[:TS, 0:P], [bb2], [b_tkb[s]])
                            transpose_group(srcs, wr)
                        dstd = mk_d if kind == "k" else mv_d
                        col0 = idx * P
                        ker.dma(sp, dstd[0:TP, col0:col0 + P].rearrange("(tt p) n -> p tt n", p=P), tkb[s][:, 0:8, :],
                                reads=[b_tkb[s]], sbuf=b_tkb[s])
                        ker.dma(sp, dstd[TP:T, col0:col0 + P], tkb[s][:TS, 8, :], reads=[b_tkb[s]], sbuf=b_tkb[s])

                    blocks, jobs = [], []
                    kinds = ["q"] * 16 + ["k"] * 16 + ["v"] * 32 + ["o"] * 32
                    base = {"q": 0, "k": 16, "v": 32, "o": 64}
                    for b in range(48):
                        blocks.append((c_w_in[:, b * 256:(b + 1) * 256], KT, 256))
                        for j in range(2):
                            oi = b * 2 + j
                            jobs.append((b, j * P, P, (kinds[oi], oi - base[kinds[oi]])))
                    blocks.append((c_w_in[:, 12176:12304], KT, 128))
                    jobs.append((48, 112, 8, ("i", 0)))
                    jobs.append((48, 120, 8, ("f", 0)))
                    gemm(lambda tag, kt: (A[:, kt, :], b_A), blocks, jobs, CH, epi)
                    ker.barrier()
            if STAGE < 7:
                pL.__exit__(None, None, None)
                return
            pC = ExitStack()
            pC.__enter__()
            moutT = sb(pC, "moutT", [P, KT, T], BF16)
            b_mout = ker.buf("moutT")
            negM = sb(pC, "negM", [8, T])
            negm = sb(pC, "negm", [8, T])
            wint = sb(pC, "wint", [8, T])
            b_rows = ker.buf("rows")
            a_tm = sb(pC, "a_tm", [P, 9, 8])
            wS_tm = sb(pC, "wS_tm", [P, 8, 8])
            b_tm = ker.buf("tm")
            esel = sb(pC, "esel", [8, 8, P])
            b_esel = ker.buf("esel", dma=True)
            hn = sb(pC, "hn", [P, 32])
            b_hn = ker.buf("hn", dma=True)
            pmk = sb(pC, "pmk1", [P, 8])
            b_pmk = ker.buf("pmk1", dma=True)
            stm = sb(pC, "stm", [P, 8])
            b_stm = ker.buf("stm", dma=True)
            msc = sb(pC, "msc", [8, 8])
            b_msc = ker.buf("msc", dma=True)
            minb = sb(pC, "minb", [P, 8])
            wrr = sb(pC, "wrr", [P, 4, 8])
            b_comb = ker.buf("comb")
            ker.dma(sp, esel[:], esel_d.rearrange("a (h m) -> a h m", m=P), writes=[b_esel], sbuf=b_esel)
            ker.dma(sp, hn[:], c_hn, writes=[b_hn], sbuf=b_hn)
            ker.dma(sp, pmk[:], pmask_b, writes=[b_pmk], sbuf=b_pmk)
            ker.dma(sp, stm[:], st_m_b, writes=[b_stm], sbuf=b_stm)
            ker.dma(sp, msc[:, 4:5], st_m_c, writes=[b_msc], sbuf=b_msc)
            pR = ExitStack()
            pR.__enter__()
            Bc = sb(pR, "Bc", [8, T])
            M0 = sb(pR, "M0", [8, T])
            b_BM = ker.buf("BM")
            with ExitStack() as ps:
                one8 = sb(ps, "one8", [8, T])
                av = sb(ps, "av", [8, T])
                wS = sb(ps, "wS", [8, TP])
                b_t = ker.buf("gtmp")
                ker.op(dve, lambda: nc.vector.memset(one8[:], 1.0), writes=[b_t])
                for (c0, cn) in ((0, TP), (TP, TS)):
                    ker.op(dve, lambda: nc.vector.tensor_tensor_scan(out=Bc[:, c0:c0 + cn], data0=one8[:, c0:c0 + cn], data1=lfc[:, c0:c0 + cn],
                                                                   initial=0.0, op0=ALU.mult, op1=ALU.add),
                           reads=[b_t, b_lfc], writes=[b_BM])
                ker.op(dve, lambda: nc.vector.tensor_tensor(out=av[:], in0=igT[:], in1=Bc[:], op=ALU.subtract), reads=[b_igT, b_BM], writes=[b_t])
                for (c0, cn) in ((0, TP), (TP, TS)):
                    ker.op(dve, lambda: nc.vector.tensor_tensor_scan(out=M0[:, c0:c0 + cn], data0=one8[:, c0:c0 + cn], data1=av[:, c0:c0 + cn],
                                                                   initial=-1e30, op0=ALU.mult, op1=ALU.max),
                           reads=[b_t], writes=[b_BM])
                ker.op(dve, lambda: nc.vector.tensor_copy(out=msc[:, 0:1], in_=Bc[:, TP - 1:TP]), reads=[b_BM], writes=[b_msc])
                ker.op(dve, lambda: nc.vector.tensor_copy(out=msc[:, 1:2], in_=M0[:, TP - 1:TP]), reads=[b_BM], writes=[b_msc])
                ker.op(dve, lambda: nc.vector.tensor_scalar(out=msc[:, 2:3], in0=M0[:, TP - 1:TP], scalar1=-1.0, scalar2=None, op0=ALU.mult),
                       reads=[b_BM], writes=[b_msc])
                ker.op(act, lambda: nc.scalar.activation(out=wS[:], in_=av[:, 0:TP], func=AF.Exp, scale=1.0, bias=msc[:, 2:3]),
                       reads=[b_t, b_msc], writes=[b_t])
                bk2, bb2 = tr_bank()
                for ti, (t0, tn) in enumerate(TT):
                    ker.op(pe, lambda: nc.tensor.transpose(bk2[:tn, ti * 8:(ti + 1) * 8], av[:, t0:t0 + tn], ident[:8, :8]),
                           reads=[b_t, b_cst], writes=[bb2] if ti == 0 else (), pwrites=() if ti == 0 else [bb2])
                ker.op(dve, lambda: nc.vector.memset(a_tm[:], 0.0), writes=[b_tm])
                evac(a_tm[:, 0:8, :], bk2[:, 0:64].rearrange("p (j n) -> p j n", n=8), [bb2], [b_tm])
                evac(a_tm[:TS, 8, :], bk2[:TS, 64:72], [bb2], [b_tm])
                bk2, bb2 = tr_bank()
                for ti in range(8):
                    ker.op(pe, lambda: nc.tensor.transpose(bk2[:, ti * 8:(ti + 1) * 8], wS[:, ti * P:(ti + 1) * P], ident[:8, :8]),
                           reads=[b_t, b_cst], writes=[bb2] if ti == 0 else (), pwrites=() if ti == 0 else [bb2])
                evac(wS_tm[:, :, :], bk2[:, 0:64].rearrange("p (j n) -> p j n", n=8), [bb2], [b_tm])
                b_sum_in = ker.buf("sum_in")
                zrow = sb(ps, "zrow", [1, 512])
                b_zrow = ker.buf("zrow", dma=True)
                ker.op(dve, lambda: nc.vector.memset(zrow[:], 0.0), writes=[b_zrow])
                ker.dma(sp, sumS_in[4:5, :], zrow[:], reads=[b_zrow], writes=[b_sum_in], sbuf=b_zrow)
                ker.dma(sp, sumS_in[4:5, 0:8].rearrange("a h -> h a"), msc[:, 0:1], reads=[b_msc], pwrites=[b_sum_in], sbuf=b_msc)
                ker.dma(sp, sumS_in[4:5, 8:16].rearrange("a h -> h a"), msc[:, 1:2], reads=[b_msc], pwrites=[b_sum_in], sbuf=b_msc)
                ker.barrier()
            with ExitStack() as ps:
                k1 = [sb(ps, f"k1_{i}", [P, 8, DK], BF16) for i in range(2)]
                v1 = [sb(ps, f"v1_{i}", [P, 8, DV], BF16) for i in range(2)]
                b_kv1 = [ker.buf(f"kv1_{i}", dma=True) for i in range(2)]
                kw1 = sb(ps, "kw1", [P, 8, DK], BF16)
                b_kw1 = ker.buf("kw1")
                cl = [sb(ps, f"cl{i}", [P, 2, DV]) for i in range(2)]
                b_cl = [ker.buf(f"cl{i}", dma=True) for i in range(2)]
                nl = [sb(ps, f"nl{i}", [P, 2]) for i in range(2)]
                b_nl = [ker.buf(f"nl{i}", dma=True) for i in range(2)]
                n_flat = sumS_in[0:4, :].rearrange("a (b p) -> p (a b)", p=P)
                for h in range(H_C):
                    s2 = h % 2
                    ker.dma(sp, k1[s2][:], mk_d[0:TP, h * DK:(h + 1) * DK].rearrange("(tt p) n -> p tt n", p=P), pwrites=[b_kv1[s2]], sbuf=b_kv1[s2])
                    ker.dma(sp, v1[s2][:], mv_d[0:TP, h * DV:(h + 1) * DV].rearrange("(tt p) n -> p tt n", p=P), pwrites=[b_kv1[s2]], sbuf=b_kv1[s2])
                    for tt in range(8):
                        ker.op(dve, lambda: nc.vector.tensor_scalar(out=kw1[:, tt, :], in0=k1[s2][:, tt, :], scalar1=wS_tm[:, tt, h:h + 1], scalar2=None, op0=ALU.mult),
                               reads=[b_kv1[s2], b_tm], writes=[b_kw1] if tt == 0 else (), pwrites=() if tt == 0 else [b_kw1])
                    for dkt in range(2):
                        bk, bb = next_acc()
                        for tt in range(8):
                            ker.op(pe, lambda: nc.tensor.matmul(bk[:, :DV], kw1[:, tt, dkt * P:(dkt + 1) * P], v1[s2][:, tt, :], start=(tt == 0), stop=(tt == 7)),
                                   reads=[b_kw1, b_kv1[s2]], writes=[bb], signal=(tt == 7))
                        evac(cl[s2][:, dkt, :], bk[:, :DV], [bb], [b_cl[s2]])
                    bk, bb = next_acc()
                    for dkt in range(2):
                        for tt in range(8):
                            ker.op(pe, lambda: nc.tensor.matmul(bk[:, dkt:dkt + 1], kw1[:, tt, dkt * P:(dkt + 1) * P], ones_b[:, 0:1], start=(tt == 0), stop=(tt == 7)),
                                   reads=[b_kw1, b_ones], writes=[bb] if (dkt == 0 and tt == 0) else (), pwrites=() if (dkt == 0 and tt == 0) else [bb],
                                   signal=(tt == 7))
                    evac(nl[s2][:, :], bk[:, 0:2], [bb], [b_nl[s2]])
                    ker.dma(sp, sumC_in[h // 2][(h % 2) * DK:(h % 2 + 1) * DK, :].rearrange("(t p) v -> p t v", p=P), cl[s2][:], reads=[b_cl[s2]], pwrites=[b_sum_in], sbuf=b_cl[s2])
                    ker.dma(sp, n_flat[:, h * 2:h * 2 + 2], nl[s2][:], reads=[b_nl[s2]], pwrites=[b_sum_in], sbuf=b_nl[s2], allow_slow_non_contiguous=True)
                ker.barrier()
            for j in range(4):
                all_gather(sumC_in[j], sumC_all[j], [b_sum_in], [b_gath["sum_all"]] if j == 0 else [], [] if j == 0 else [b_gath["sum_all"]])
            all_gather(sumS_in, sumS_all, [b_sum_in], [], [b_gath["sum_all"]])
            with ExitStack() as ps:
                scrow = sb(ps, "scrow", [P, 4, 16])
                b_scrow = ker.buf("scrow", dma=True)
                scb = sb(ps, "scb", [P, 4, 16])
                boffs = sb(ps, "boffs", [P, 5, 8])
                Lp = sb(ps, "Lp", [P, 4, 8])
                pen = sb(ps, "pen", [P, 4])
                mrel = sb(ps, "mrel", [P, 8])
                tmp8 = sb(ps, "tmp8", [P, 8])
                b_c = ker.buf("combtmp")
                ker.op(dve, lambda: nc.vector.memset(scrow[:], 0.0), writes=[b_scrow])
                ker.dma(sp, scrow[0:1, :, :], sumS_all.rearrange("(r a) v -> a r v", a=5)[4:5, :, 0:16], reads=[b_gath["sum_all"]], pwrites=[b_scrow], sbuf=b_scrow)
                bk, bb = next_acc()
                ker.op(pe, lambda: nc.tensor.matmul(bk[:, 0:64], ones_f[:, :], scrow[:, :, :].rearrange("p r v -> p (r v)"), start=True, stop=True),
                       reads=[b_scrow, b_ones], writes=[bb])
                ker.op(dve, lambda: nc.vector.tensor_copy(out=scb[:], in_=bk[:, 0:64].rearrange("p (r v) -> p r v", v=16)), reads=[bb], writes=[b_c])
                D_ = lambda f, **kw: ker.op(dve, f, reads=[b_c, b_pmk], writes=[b_c])
                D_(lambda: nc.vector.memset(boffs[:], 0.0))
                for r in range(3):
                    D_(lambda: nc.vector.tensor_tensor(out=boffs[:, r + 1, :], in0=boffs[:, r, :], in1=scb[:, r, 0:8], op=ALU.add))
                for r in range(4):
                    D_(lambda: nc.vector.scalar_tensor_tensor(out=boffs[:, 4, :], in0=scb[:, r, 0:8], scalar=pmk[:, r:r + 1], in1=boffs[:, 4, :],
                                                             op0=ALU.mult, op1=ALU.add))
                D_(lambda: nc.vector.tensor_scalar(out=pen[:], in0=pmk[:, 0:4], scalar1=-1.0, scalar2=1e30, op0=ALU.add, op1=ALU.mult))
                D_(lambda: nc.vector.memset(mrel[:], 0.0))
                for r in range(4):
                    D_(lambda: nc.vector.tensor_tensor(out=Lp[:, r, :], in0=scb[:, r, 8:16], in1=boffs[:, r, :], op=ALU.subtract))
                    D_(lambda: nc.vector.tensor_scalar(out=tmp8[:], in0=Lp[:, r, :], scalar1=pen[:, r:r + 1], scalar2=None, op0=ALU.add))
                    D_(lambda: nc.vector.tensor_tensor(out=mrel[:], in0=mrel[:], in1=tmp8[:], op=ALU.max))
                ker.op(dve, lambda: nc.vector.tensor_tensor(out=minb[:], in0=boffs[:, 4, :], in1=mrel[:], op=ALU.add), reads=[b_c], writes=[b_comb])
                for r in range(4):
                    D_(lambda: nc.vector.tensor_tensor(out=tmp8[:], in0=Lp[:, r, :], in1=mrel[:], op=ALU.subtract))
                    D_(lambda: nc.vector.tensor_scalar(out=tmp8[:], in0=tmp8[:], scalar1=0.0, scalar2=None, op0=ALU.min))
                    ker.op(act, lambda: nc.scalar.activation(out=tmp8[:], in_=tmp8[:], func=AF.Exp), reads=[b_c], writes=[b_c])
                    ker.op(dve, lambda: nc.vector.tensor_scalar(out=wrr[:, r, :], in0=tmp8[:], scalar1=pmk[:, r:r + 1], scalar2=None, op0=ALU.mult),
                           reads=[b_c, b_pmk], pwrites=[b_comb])
                dg = sb(ps, "dg", [8, 8])
                b_dg = ker.buf("dg")
                ker.op(dve, lambda: nc.vector.tensor_tensor(out=dg[:, :], in0=minb[:8, 0:8], in1=ident[:8, :8], op=ALU.mult),
                       reads=[b_comb, b_cst], writes=[b_dg])
                bk2, bb2 = next_acc()
                ker.op(pe, lambda: nc.tensor.matmul(bk2[:8, 0:1], dg[:, :], ones_f[:8, 0:1], start=True, stop=True), reads=[b_dg, b_ones], writes=[bb2])
                ker.op(dve, lambda: nc.vector.tensor_copy(out=msc[:, 3:4], in_=bk2[:8, 0:1]), reads=[bb2], writes=[b_msc])
                ker.barrier()
            for (c0, cn, mc) in ((0, TP, 3), (TP, TS, 4)):
                ker.op(dve, lambda: nc.vector.tensor_scalar(out=negM[:, c0:c0 + cn], in0=M0[:, c0:c0 + cn], scalar1=msc[:, mc:mc + 1], scalar2=-1.0,
                                                            op0=ALU.max, op1=ALU.mult),
                       reads=[b_BM, b_msc], pwrites=[b_rows])
                ker.op(dve, lambda: nc.vector.tensor_tensor(out=negm[:, c0:c0 + cn], in0=negM[:, c0:c0 + cn], in1=Bc[:, c0:c0 + cn], op=ALU.subtract),
                       reads=[b_BM, b_rows], pwrites=[b_rows])
                ker.op(act, lambda: nc.scalar.activation(out=wint[:, c0:c0 + cn], in_=negM[:, c0:c0 + cn], func=AF.Exp, scale=1.0, bias=msc[:, mc:mc + 1]),
                       reads=[b_rows, b_msc], pwrites=[b_rows])
                ker.op(dve, lambda: nc.vector.tensor_scalar(out=msc[:, mc + 2:mc + 3], in0=negm[:, c0 + cn - 1:c0 + cn], scalar1=-1.0, scalar2=None, op0=ALU.mult),
                       reads=[b_rows], writes=[b_msc])
            ker.dma(sp, o_m[0:1, :].rearrange("a h -> h a"), msc[:, 5:6], reads=[b_msc], sbuf=b_msc)
            ker.dma(sp, o_m[1:2, :].rearrange("a h -> h a"), msc[:, 6:7], reads=[b_msc], sbuf=b_msc)
            ker.barrier()
            pR.__exit__(None, None, None)
            with ExitStack() as ps:
                qT2 = [sb(ps, f"qT2_{i}", [P, 2, T], BF16) for i in range(2)]
                kT2 = [sb(ps, f"kT2_{i}", [P, 2, T], BF16) for i in range(2)]
                ktm = [sb(ps, f"ktm{i}", [P, 9, DK], BF16) for i in range(2)]
                vtm = [sb(ps, f"vtm{i}", [P, 9, DV], BF16) for i in range(2)]
                b_in2 = [ker.buf(f"in2_{i}", dma=True) for i in range(2)]
                so = sb(ps, "so", [P, 4, T])
                b_so = ker.buf("so", dma=True)
                Cst = sb(ps, "Cst", [P, 2, DV])
                Cb = sb(ps, "Cb", [P, 2, DV], BF16)
                nst = sb(ps, "nst", [P, 2])
                nb = sb(ps, "nb", [P, 2, P], BF16)
                Mprev = sb(ps, "Mprev", [P, 1])
                b_st = ker.buf("state", dma=True)
                clr = [sb(ps, f"clr{i}", [P, 2, DV]) for i in range(2)]
                b_clr = [ker.buf(f"clr{i}", dma=True) for i in range(2)]
                nlr = sb(ps, "nlr", [P, 4, 2])
                b_nlr = ker.buf("nlr", dma=True)
                bcs = sb(ps, "bcs", [P, 3, P])
                b_bcs = ker.buf("bcs")
                b_wib = ker.buf("wib")
                e1 = sb(ps, "e1", [P, P])
                Dm = sb(ps, "Dm", [P, P])
                Wt = sb(ps, "Wt", [P, P], BF16)
                qw = sb(ps, "qw", [P, 2, P], BF16)
                exm = sb(ps, "exm", [P, P])
                rd = sb(ps, "rd", [P, P])
                hT = sb(ps, "hT", [P, 4, P])
                hsq = sb(ps, "hsq", [P, 4, P])
                rs = sb(ps, "rs", [P, P])
                t1 = sb(ps, "t1", [P, P])
                wsc = sb(ps, "wsc", [P, 1])
                wcc = sb(ps, "wcc", [P, 1])
                kwt = sb(ps, "kwt", [P, DK], BF16)
                b_x = {n: ker.buf(n) for n in ("e1", "Dm", "Wt", "qw", "exm", "rd", "hT", "hsq", "rs", "t1", "wsc", "wcc", "kwt")}

                def load_head(h):
                    s2 = h % 2
                    ker.dma(sp, qT2[s2][:], mq_d[2 * h:2 * h + 2].rearrange("k p t -> p k t"), pwrites=[b_in2[s2]], sbuf=b_in2[s2])
                    ker.dma(sp, kT2[s2][:], mkT_d[2 * h:2 * h + 2].rearrange("k p t -> p k t"), pwrites=[b_in2[s2]], sbuf=b_in2[s2])
                    ker.dma(sp, ktm[s2][:, 0:8, :], mk_d[0:TP, h * DK:(h + 1) * DK].rearrange("(tt p) n -> p tt n", p=P), pwrites=[b_in2[s2]], sbuf=b_in2[s2])
                    ker.dma(sp, ktm[s2][:TS, 8, :], mk_d[TP:T, h * DK:(h + 1) * DK], pwrites=[b_in2[s2]], sbuf=b_in2[s2])
                    ker.dma(sp, vtm[s2][:, 0:8, :], mv_d[0:TP, h * DV:(h + 1) * DV].rearrange("(tt p) n -> p tt n", p=P), pwrites=[b_in2[s2]], sbuf=b_in2[s2])
                    ker.dma(sp, vtm[s2][:TS, 8, :], mv_d[TP:T, h * DV:(h + 1) * DV], pwrites=[b_in2[s2]], sbuf=b_in2[s2])

                def refresh_state_copies():
                    ker.op(act, lambda: nc.scalar.copy(out=Cb[:], in_=Cst[:]), reads=[b_st], pwrites=[b_st])
                    for dkt in range(2):
                        ker.op(dve, lambda: nc.vector.tensor_scalar(out=nb[:, dkt, :], in0=ones_f[:, :], scalar1=nst[:, dkt:dkt + 1], scalar2=None, op0=ALU.mult),
                               reads=[b_st, b_ones], pwrites=[b_st])

                def tile_step(h, s2, ti, t0, L):
                    bX, bbX = next_acc()
                    for j, row in enumerate((negM, negm)):
                        ker.op(pe, lambda: nc.tensor.matmul(bX[:, j * P:j * P + L], esel[:, h, :], row[:, t0:t0 + L], start=True, stop=True),
                               reads=[b_rows, b_esel], writes=[bbX] if j == 0 else (), pwrites=() if j == 0 else [bbX])
                    ker.op(act, lambda: nc.scalar.copy(out=bcs[:, 0:2, :L], in_=bX[:, 0:2 * P].rearrange("p (j n) -> p j n", n=P)[:, :, :L]),
                           reads=[bbX], writes=[b_bcs])
                    ker.op(dve, lambda: nc.vector.tensor_scalar(out=e1[:L, :L], in0=bcs[:L, 0, :L], scalar1=a_tm[:L, ti, h:h + 1], scalar2=0.0,
                                                                op0=ALU.add, op1=ALU.min),
                           reads=[b_bcs, b_tm], writes=[b_x["e1"]])
                    ker.op(act, lambda: nc.scalar.activation(out=e1[:L, :L], in_=e1[:L, :L], func=AF.Exp), reads=[b_x["e1"]], writes=[b_x["e1"]])
                    ker.op(dve, lambda: nc.vector.tensor_tensor(out=Dm[:L, :L], in0=e1[:L, :L], in1=cst[:L, 2, :L], op=ALU.mult),
                           reads=[b_x["e1"], b_cst], writes=[b_x["Dm"]])
                    bS, bbS = next_acc()
                    for dkt in range(2):
                        ker.op(pe, lambda: nc.tensor.matmul(bS[:L, :L], kT2[s2][:, dkt, t0:t0 + L], qT2[s2][:, dkt, t0:t0 + L], start=(dkt == 0), stop=(dkt == 1)),
                               reads=[b_in2[s2]], writes=[bbS], signal=(dkt == 1))
                    ker.op(dve, lambda: nc.vector.tensor_tensor(out=Wt[:L, :L], in0=bS[:L, :L], in1=Dm[:L, :L], op=ALU.mult),
                           reads=[bbS, b_x["Dm"]], writes=[b_x["Wt"]])
                    ker.op(act, lambda: nc.scalar.activation(out=bcs[:, 2, :L], in_=bcs[:, 0, :L], func=AF.Exp, scale=1.0, bias=Mprev[:, 0:1]),
                           reads=[b_bcs, b_st], writes=[b_wib])
                    for dkt in range(2):
                        ker.op(dve, lambda: nc.vector.tensor_tensor(out=qw[:, dkt, :L], in0=qT2[s2][:, dkt, t0:t0 + L], in1=bcs[:, 2, :L], op=ALU.mult),
                               reads=[b_in2[s2], b_wib], writes=[b_x["qw"]] if dkt == 0 else (), pwrites=() if dkt == 0 else [b_x["qw"]])
                    bN, bbN = next_acc()
                    first = True
                    for dvt in range(4):
                        reg = bN[:, dvt * P:dvt * P + L]
                        ker.op(pe, lambda: nc.tensor.matmul(reg, vtm[s2][:L, ti, dvt * P:(dvt + 1) * P], Wt[:L, :L], start=True, stop=False),
                               reads=[b_in2[s2], b_x["Wt"]], writes=[bbN] if first else (), pwrites=() if first else [bbN], signal=False)
                        first = False
                        for dkt in range(2):
                            ker.op(pe, lambda: nc.tensor.matmul(reg, Cb[:, dkt, dvt * P:(dvt + 1) * P], qw[:, dkt, :L], start=False, stop=(dkt == 1)),
                                   reads=[b_st, b_x["qw"]], pwrites=[bbN], signal=(dkt == 1 and dvt == 3))
                    bD, bbD = next_acc()
                    ker.op(pe, lambda: nc.tensor.matmul(bD[:, :L], ones_b[:L, :], Wt[:L, :L], start=True, stop=False),
                           reads=[b_ones, b_x["Wt"]], writes=[bbD], signal=False)
                    for dkt in range(2):
                        ker.op(pe, lambda: nc.tensor.matmul(bD[:, :L], nb[:, dkt, :], qw[:, dkt, :L], start=False, stop=(dkt == 1)),
                               reads=[b_st, b_x["qw"]], pwrites=[bbD], signal=(dkt == 1))
                    ker.op(act, lambda: nc.scalar.activation(out=exm[:, :L], in_=bcs[:, 1, :L], func=AF.Exp), reads=[b_bcs], writes=[b_x["exm"]])
                    ker.op(act, lambda: nc.scalar.activation(out=rd[:, :L], in_=bD[:, :L], func=AF.Abs),
                           reads=[bbD], writes=[b_x["rd"]])
                    ker.op(dve, lambda: nc.vector.tensor_tensor(out=rd[:, :L], in0=rd[:, :L], in1=exm[:, :L], op=ALU.max),
                           reads=[b_x["rd"], b_x["exm"]], writes=[b_x["rd"]])
                    ker.op(dve, lambda: nc.vector.reciprocal(out=rd[:, :L], in_=rd[:, :L]), reads=[b_x["rd"]], writes=[b_x["rd"]])
                    for dvt in range(4):
                        ker.op(dve, lambda: nc.vector.tensor_tensor(out=hT[:, dvt, :L], in0=bN[:, dvt * P:dvt * P + L], in1=rd[:, :L], op=ALU.mult),
                               reads=[bbN, b_x["rd"]], writes=[b_x["hT"]] if dvt == 0 else (), pwrites=() if dvt == 0 else [b_x["hT"]])
                    ker.op(dve, lambda: nc.vector.tensor_tensor(out=hsq[:, :, :L], in0=hT[:, :, :L], in1=hT[:, :, :L], op=ALU.mult),
                           reads=[b_x["hT"]], writes=[b_x["hsq"]])
                    bQ, bbQ = next_acc()
                    for dvt in range(4):
                        ker.op(pe, lambda: nc.tensor.matmul(bQ[:, :L], ones_f[:, :], hsq[:, dvt, :L], start=(dvt == 0), stop=(dvt == 3)),
                               reads=[b_x["hsq"], b_ones], writes=[bbQ], signal=(dvt == 3))
                    ker.op(act, lambda: nc.scalar.activation(out=rs[:, :L], in_=bQ[:, :L], func=AF.Ln, scale=1.0 / DV, bias=eps_t[:, 0:1]),
                           reads=[bbQ, b_ones], writes=[b_x["rs"]])
                    ker.op(act, lambda: nc.scalar.activation(out=rs[:, :L], in_=rs[:, :L], func=AF.Exp, scale=-0.5), reads=[b_x["rs"]], writes=[b_x["rs"]])
                    for dvt in range(4):
                        ker.op(dve, lambda: nc.vector.scalar_tensor_tensor(out=t1[:, :L], in0=hT[:, dvt, :L], scalar=hn[:, h * 4 + dvt:h * 4 + dvt + 1], in1=rs[:, :L],
                                                                          op0=ALU.mult, op1=ALU.mult),
                               reads=[b_x["hT"], b_x["rs"], b_hn], writes=[b_x["t1"]])
                        ker.op(dve, lambda: nc.vector.tensor_tensor(out=moutT[:, h * 4 + dvt, t0:t0 + L], in0=t1[:, :L], in1=so[:, dvt, t0:t0 + L], op=ALU.mult),
                               reads=[b_x["t1"], b_so], pwrites=[b_mout])
                    ker.op(act, lambda: nc.scalar.activation(out=wsc[:L, :], in_=a_tm[:L, ti, h:h + 1], func=AF.Exp, scale=1.0, bias=bcs[:L, 0, L - 1:L]),
                           reads=[b_tm, b_bcs], writes=[b_x["wsc"]])
                    ker.op(act, lambda: nc.scalar.activation(out=wcc[:, :], in_=bcs[:, 0, L - 1:L], func=AF.Exp, scale=1.0, bias=Mprev[:, 0:1]),
                           reads=[b_bcs, b_st], writes=[b_x["wcc"]])
                    ker.op(dve, lambda: nc.vector.tensor_scalar(out=kwt[:L, :], in0=ktm[s2][:L, ti, :], scalar1=wsc[:L, 0:1], scalar2=None, op0=ALU.mult),
                           reads=[b_in2[s2], b_x["wsc"]], writes=[b_x["kwt"]])
                    for dkt in range(2):
                        bC, bbC = next_acc()
                        ker.op(pe, lambda: nc.tensor.matmul(bC[:, :DV], kwt[:L, dkt * P:(dkt + 1) * P], vtm[s2][:L, ti, :], start=True, stop=True),
                               reads=[b_x["kwt"], b_in2[s2]], writes=[bbC])
                        ker.op(dve, lambda: nc.vector.scalar_tensor_tensor(out=Cst[:, dkt, :], in0=Cst[:, dkt, :], scalar=wcc[:, 0:1], in1=bC[:, :DV],
                                                                          op0=ALU.mult, op1=ALU.add),
                               reads=[bbC, b_x["wcc"], b_st], writes=[b_st])
                    bn_, bbn_ = next_acc()
                    for dkt in range(2):
                        ker.op(pe, lambda: nc.tensor.matmul(bn_[:, dkt:dkt + 1], kwt[:L, dkt * P:(dkt + 1) * P], ones_b[:L, 0:1], start=True, stop=True),
                               reads=[b_x["kwt"], b_ones], writes=[bbn_] if dkt == 0 else (), pwrites=() if dkt == 0 else [bbn_])
                    ker.op(dve, lambda: nc.vector.scalar_tensor_tensor(out=nst[:, :], in0=nst[:, :], scalar=wcc[:, 0:1], in1=bn_[:, 0:2], op0=ALU.mult, op1=ALU.add),
                           reads=[bbn_, b_x["wcc"], b_st], writes=[b_st])
                    ker.op(dve, lambda: nc.vector.tensor_scalar(out=Mprev[:, :], in0=bcs[:, 0, L - 1:L], scalar1=-1.0, scalar2=None, op0=ALU.mult),
                           reads=[b_bcs, b_x["wcc"]], writes=[b_st])
                    refresh_state_copies()

                load_head(0)
                for h in range(H_C):
                    s2 = h % 2
                    if h + 1 < H_C:
                        load_head(h + 1)
                    ker.dma(sp, so[:], mo_d[4 * h:4 * h + 4].rearrange("k p t -> p k t"), writes=[b_so], sbuf=b_so)
                    for r in range(4):
                        s3 = r % 2
                        n_r = sumS_all[r * 5:r * 5 + 4, :].rearrange("a (b p) -> p (a b)", p=P)
                        ker.dma(sp, nlr[:, r, :], n_r[:, 2 * h:2 * h + 2], reads=[b_gath["sum_all"]], writes=[b_nlr] if r == 0 else (),
                                pwrites=() if r == 0 else [b_nlr], sbuf=b_nlr, allow_slow_non_contiguous=True)
                        ker.dma(sp, clr[s3][:], sumC_all[h // 2][r * 2 * DK + (h % 2) * DK:r * 2 * DK + (h % 2 + 1) * DK, :].rearrange("(t p) v -> p t v", p=P), reads=[b_gath["sum_all"]],
                                writes=[b_clr[s3]], sbuf=b_clr[s3])
                        if r == 0:
                            ker.op(dve, lambda: nc.vector.tensor_scalar(out=Cst[:], in0=clr[s3][:], scalar1=wrr[:, r, h:h + 1], scalar2=None, op0=ALU.mult),
                                   reads=[b_clr[s3], b_comb], writes=[b_st])
                            ker.op(dve, lambda: nc.vector.tensor_scalar(out=nst[:], in0=nlr[:, r, :], scalar1=wrr[:, r, h:h + 1], scalar2=None, op0=ALU.mult),
                                   reads=[b_nlr, b_comb, b_st], writes=[b_st])
                        else:
                            ker.op(dve, lambda: nc.vector.scalar_tensor_tensor(out=Cst[:], in0=clr[s3][:], scalar=wrr[:, r, h:h + 1], in1=Cst[:],
                                                                              op0=ALU.mult, op1=ALU.add),
                                   reads=[b_clr[s3], b_comb, b_st], writes=[b_st])
                            ker.op(dve, lambda: nc.vector.scalar_tensor_tensor(out=nst[:], in0=nlr[:, r, :], scalar=wrr[:, r, h:h + 1], in1=nst[:],
                                                                              op0=ALU.mult, op1=ALU.add),
                                   reads=[b_nlr, b_comb, b_st], writes=[b_st])
                    ker.op(dve, lambda: nc.vector.tensor_copy(out=Mprev[:, :], in_=minb[:, h:h + 1]), reads=[b_comb, b_st], writes=[b_st])
                    refresh_state_copies()
                    for ti in range(8):
                        tile_step(h, s2, ti, ti * P, P)
                    ker.dma(sp, o_c[0, h].rearrange("(t p) v -> p t v", p=P), Cst[:], reads=[b_st], sbuf=b_st)
                    ker.dma(sp, o_n[0, h:h + 1, :].rearrange("a (t p) -> p (a t)", p=P), nst[:], reads=[b_st], sbuf=b_st, allow_slow_non_contiguous=True)
                    ker.dma(sp, Cst[:], st_c[h].rearrange("(t p) v -> p t v", p=P), writes=[b_st], sbuf=b_st)
                    ker.dma(sp, nst[:], st_n[h:h + 1, :].rearrange("a (t p) -> p (a t)", p=P), reads=[b_st], writes=[b_st], sbuf=b_st, allow_slow_non_contiguous=True)
                    ker.op(dve, lambda: nc.vector.tensor_copy(out=Mprev[:, :], in_=stm[:, h:h + 1]), reads=[b_stm, b_st], writes=[b_st])
                    refresh_state_copies()
                    tile_step(h, s2, 8, TP, TS)
                    ker.dma(sp, o_c[1, h].rearrange("(t p) v -> p t v", p=P), Cst[:], reads=[b_st], sbuf=b_st)
                    ker.dma(sp, o_n[1, h:h + 1, :].rearrange("a (t p) -> p (a t)", p=P), nst[:], reads=[b_st], sbuf=b_st, allow_slow_non_contiguous=True)
                ker.barrier()
            if os.environ.get("KDEBUG"):
                b_dbg = ker.buf("dbg2", dma=True)
                ker.dma(sp, dt_tmp("dbg_mo", [KT, P, T], BF16).rearrange("k p t -> p k t"), moutT[:], reads=[b_mout], sbuf=b_dbg)
                ker.barrier()
            if STAGE >= 8 and 'g4' not in SKIP:
              with ExitStack() as ps:
                alloc_wslots(ps)
                ost = [sb(ps, f"ost{i}", [P, T]) for i in range(2)]
                b_ost = [ker.buf(f"ost{i}", dma="sw") for i in range(2)]
                blocks = [(c_w_out[:, b * 256:(b + 1) * 256], KT, 256) for b in range(16)]
                jobs = [(b, j * P, P, ("o", b * 2 + j)) for b in range(16) for j in range(2)]
                gemm(lambda tag, kt: (moutT[:, kt, :], b_mout), blocks, jobs, CH, make_accum_epi(ost, b_ost))
                ker.barrier()
            pC.__exit__(None, None, None)
            pL.__exit__(None, None, None)

        if STAGE >= 6:
            layer1()
        if STAGE >= 9 and 'ffn1' not in SKIP:
            ffn(1)

        with ExitStack() as ps:
            yst = [sb(ps, f"yst{i}", [P, 2, T]) for i in range(1)]
            b_yst = [ker.buf("yst0"), ker.buf("yst1")]
            ytk = [sb(ps, f"ytk{i}", [P, 9, 2 * P]) for i in range(2)]
            b_ytk = [ker.buf(f"ytk{i}", dma=True) for i in range(2)]
            cnt = [0]

            def fin_cb(kt, xs_t, b_xs_t, rstd, b_rstd):
                h = kt % 2
                ker.op(dve, lambda: nc.vector.scalar_tensor_tensor(out=yst[0][:, h, :], in0=xs_t[:], scalar=nrm[:, 4, kt:kt + 1],
                                                                  in1=rstd[:], op0=ALU.mult, op1=ALU.mult),
                       reads=[b_xs_t, b_rstd, b_nrm], writes=[b_yst[h]])
                s = (kt // 2) % 2
                for g0, g1 in ((0, 4), (4, 8), (8, 9)):
                    srcs = [((yst[0][:, h, TT[ti][0]:TT[ti][0] + TT[ti][1]], [b_yst[h]]), TT[ti][1], P) for ti in range(g0, g1)]

                    def wr(bk, bb, g0=g0, g1=g1, s=s, h=h):
                        if g0 < 8:
                            evac(ytk[s][:, g0:g1, h * P:(h + 1) * P], bk[:, :].rearrange("p (j n) -> p j n", n=P), [bb], [b_ytk[s]])
                        else:
                            evac(ytk[s][:TS, 8, h * P:(h + 1) * P], bk[:TS, 0:P], [bb], [b_ytk[s]])
                    transpose_group(srcs, wr)
                if h == 1:
                    col0 = (kt - 1) * P
                    ker.dma(sp, y_tok[0:TP, col0:col0 + 2 * P].rearrange("(tt p) n -> p tt n", p=P), ytk[s][:, 0:8, :],
                            reads=[b_ytk[s]], sbuf=b_ytk[s])
                    ker.dma(sp, y_tok[TP:T, col0:col0 + 2 * P], ytk[s][:TS, 8, :], reads=[b_ytk[s]], sbuf=b_ytk[s])

            rmsnorm_to(ps, None, None, 4, out_f32_cb=fin_cb)
            ker.barrier()
    _CACHE['declared'] = declared
    return nc


_CACHE = {}


def _get_program():
    if "nc" not in _CACHE:
        _CACHE["nc"] = build_program()
    return _CACHE["nc"]


def kernel(x_prompt, x_sample, cache_fox_k, cache_fox_v, cache_fox_logf, state_mlstm_c, state_mlstm_n,
           state_mlstm_m, norm_mix, norm_ffn, norm_final, ab_w_in, ab_w_out, gmlp_w_s, gmlp_b, fox_b_f,
           c_w_in, c_b_i, c_b_f, c_head_norm, c_w_out, ffn_w_gate, ffn_w_up, ffn_w_down):
    f = lambda a: np.ascontiguousarray(np.asarray(a, dtype=np.float32))
    x_prompt, x_sample = f(x_prompt), f(x_sample)
    nc = _get_program()
    nrm_all = np.stack([f(norm_mix)[0], f(norm_ffn)[0], f(norm_mix)[1], f(norm_ffn)[1], f(norm_final)], 0)
    norms = np.ascontiguousarray(nrm_all.reshape(5, KT, P).transpose(2, 0, 1))
    consts = np.zeros((P, 5, P), np.float32)
    consts[15, 4, :] = 1.0
    consts[:, 0, :] = np.eye(P)
    consts[:, 1, :] = np.tril(np.ones((P, P)))
    consts[:, 2, :] = np.triu(np.ones((P, P)))
    consts[127, 3, :] = 1.0
    kpos = np.zeros((P, 40), np.float32)
    for kt in range(32):
        kpos[:, kt] = kt * 128 + np.arange(P)
    kpos[:, 32] = 1024 + np.arange(P)
    kpos[16:, 32] = 1e9
    shared = {
        "norms": norms,
        "ab_w_in": f(ab_w_in)[0], "ab_w_out": f(ab_w_out)[0],
        "gmlp_ws": f(gmlp_w_s)[0],
        "gmlp_bb": np.ascontiguousarray(np.broadcast_to(f(gmlp_b)[0][None], (P, 8, P))),
        "fox_bf": f(fox_b_f)[0].reshape(16, 1),
        "c_w_in": f(c_w_in)[0],
        "c_bif": np.concatenate([f(c_b_i)[0], f(c_b_f)[0]]).reshape(16, 1),
        "c_hn": np.ascontiguousarray(f(c_head_norm)[0].reshape(32, P).T),
        "c_w_out": f(c_w_out)[0],
        "ffn_wg": f(ffn_w_gate), "ffn_wu": f(ffn_w_up), "ffn_wd": f(ffn_w_down),
        "consts": consts, "kpos": kpos,
        "esel": np.ascontiguousarray(np.broadcast_to(np.eye(8, dtype=np.float32)[:, :, None], (8, 8, P))).reshape(8, 8 * P),
    }
    in_maps = []
    for c in range(NCORES):
        b, p = c // 4, c % 4
        qpos = np.concatenate([p * TP + np.arange(TP), 1024 + np.arange(TS)]).astype(np.float32)
        pm = np.zeros((P, 8), np.float32)
        for r in range(4):
            pm[:, r] = 1.0 if r < p else 0.0
            pm[:, 4 + r] = 1.0 if r == p else 0.0
        m = dict(shared)
        m.update({
            "x_tok": np.ascontiguousarray(np.concatenate([x_prompt[b, p * TP:(p + 1) * TP], x_sample[c]], 0)),
            "cache_k": f(cache_fox_k)[0, c].reshape(1024, 2048),
            "cache_v": f(cache_fox_v)[0, c].reshape(1024, 2048),
            "cache_lf": f(cache_fox_logf)[0, c],
            "st_c": f(state_mlstm_c)[0, c],
            "st_n": f(state_mlstm_n)[0, c],
            "st_m_c": f(state_mlstm_m)[0, c].reshape(8, 1),
            "st_m_b": np.ascontiguousarray(np.broadcast_to(f(state_mlstm_m)[0, c][None], (P, H_C))),
            "qpos_b": np.ascontiguousarray(np.broadcast_to(qpos[None], (P, T))),
            "pmask_b": pm,
        })
        in_maps.append(m)
    decl = set(_CACHE['declared'])
    in_maps = [{k: v for k, v in m.items() if k in decl} for m in in_maps]
    res = run_bass_kernel_spmd(nc, in_maps, core_ids=list(range(NCORES)))
    R = res.results
    B, S = 2, 4096

    def prompt_rows(name, width):
        return np.stack([np.concatenate([R[b * 4 + p][name][:TP] for p in range(4)], 0) for b in range(B)], 0).reshape(B, S, width)

    def sample_rows(name, width):
        return np.stack([R[c][name][TP:T] for c in range(NCORES)], 0).reshape(NCORES, TS, width)

    yp = prompt_rows("y_tok", D)
    ys = sample_rows("y_tok", D)
    pk = prompt_rows("o_k", 2048).reshape(1, B, S, 16, 128)
    pv = prompt_rows("o_v", 2048).reshape(1, B, S, 16, 128)
    plf = prompt_rows("o_lf", 16).reshape(1, B, S, 16)
    pc = np.stack([R[3]["o_c"][0], R[7]["o_c"][0]], 0)[None]
    pn = np.stack([R[3]["o_n"][0], R[7]["o_n"][0]], 0)[None]
    pm_ = np.stack([R[3]["o_m"][0], R[7]["o_m"][0]], 0)[None]
    sk = sample_rows("o_k", 2048).reshape(1, NCORES, TS, 16, 128)
    sv = sample_rows("o_v", 2048).reshape(1, NCORES, TS, 16, 128)
    slf = sample_rows("o_lf", 16).reshape(1, NCORES, TS, 16)
    sgv = np.stack([R[c]["o_gv"] for c in range(NCORES)], 0)[None]
    sc = np.stack([R[c]["o_c"][1] for c in range(NCORES)], 0)[None]
    sn = np.stack([R[c]["o_n"][1] for c in range(NCORES)], 0)[None]
    sm = np.stack([R[c]["o_m"][1] for c in range(NCORES)], 0)[None]
    outs = (yp, ys, pk, pv, plf, pc, pn, pm_, sk, sv, slf, sgv, sc, sn, sm)
    return tuple(np.ascontiguousarray(o, dtype=np.float32) for o in outs)
```

```python
import os
import numpy as np
import ml_dtypes
from contextlib import ExitStack
import concourse.bass as bass
import concourse.mybir as mybir
from concourse.bass_utils import run_bass_kernel_spmd

F32 = mybir.dt.float32
BF16 = mybir.dt.bfloat16
AF = mybir.ActivationFunctionType
ALU = mybir.AluOpType

P = 128
D = 4096
KT = 32
TP = 1024
TS = 16
T = TP + TS
CH = [(0, 512), (512, 512), (1024, 16)]
TT = [(i * 128, 128) for i in range(8)] + [(1024, 16)]
D_A = 2048
D_B = 2048
H_B = 16
AB_IN = 10256
H_C = 8
DK = 256
DV = 512
C_IN = 12304
D_FF = 11008
EPS = 1e-6
NCORES = int(os.environ.get('KCORES', '8'))
GROUPS = [list(range(g * 4, g * 4 + 4)) for g in range(NCORES // 4)]

STAGE = int(os.environ.get('KSTAGE', '99'))
SKIP = set(os.environ.get('KSKIP', '').split(','))


class Buf:
    __slots__ = ("name", "w", "wp", "r", "dsem", "excl")

    def __init__(self, name, dsem=None):
        self.name = name
        self.excl = False
        self.w = {}
        self.wp = {}
        self.r = {}
        self.dsem = dsem


class Q:
    def __init__(self, ker, eng, name, is_pe=False):
        self.ker = ker
        self.eng = eng
        self.name = name
        self.is_pe = is_pe
        self.key = ker.new_sem("q_" + name)
        self.seen = {}

    def wait(self, k, v):
        if self.seen.get(k, 0) >= v:
            return
        assert v <= self.ker.issued[k], (self.name, k, v, self.ker.issued[k])
        self.eng.wait_ge(self.ker.sems[k], v)
        self.seen[k] = v


class Ker:
    def __init__(self, nc, es):
        self.nc = nc
        self.es = es
        self.sems = []
        self.issued = []
        self.pe = Q(self, nc.tensor, "pe", True)
        self.act = Q(self, nc.scalar, "act")
        self.dve = Q(self, nc.vector, "dve")
        self.pool = Q(self, nc.gpsimd, "pool")
        self.sp = Q(self, nc.sync, "sp")
        self.queues = [self.pe, self.act, self.dve, self.pool, self.sp]
        self.dma_pool = [self.new_sem(f"d{i}") for i in range(36)]
        self.dma_next = 0
        self.sw_pool = [self.new_sem(f"w{i}") for i in range(12)]
        self.sw_next = 0
        self.dma_last = {}
        self.uid = 0

    def new_sem(self, name):
        s = self.es.enter_context(self.nc.semaphore(name))
        self.sems.append(s)
        self.issued.append(0)
        return len(self.sems) - 1

    def buf(self, name, dma=False):
        ds = None
        if dma == "sw":
            ds = self.sw_pool[self.sw_next % len(self.sw_pool)]
            self.sw_next += 1
        elif dma:
            ds = self.dma_pool[self.dma_next % len(self.dma_pool)]
            self.dma_next += 1
        return Buf(name, ds)

    def _deps(self, q, reads, writes, pwrites=()):
        deps = {}

        def add(d):
            for k, v in d.items():
                if deps.get(k, 0) < v:
                    deps[k] = v
        for b in reads:
            add(b.w)
            add(b.wp)
            if b.excl:
                add({k: v for k, v in b.r.items() if k != q.key})
        for b in writes:
            add(b.w)
            add(b.wp)
            add(b.r)
        for b in pwrites:
            add(b.w)
            add(b.r)
        for k, v in deps.items():
            if q.is_pe and k == q.key:
                continue
            q.wait(k, v)

    def _record(self, ev, reads, writes, pwrites=()):
        k, v = ev
        for b in reads:
            if b.r.get(k, 0) < v:
                b.r[k] = v
        for b in writes:
            b.w = {k: v}
            b.wp = {}
            b.r = {}
        for b in pwrites:
            if b.wp.get(k, 0) < v:
                b.wp[k] = v

    def op(self, q, fn, reads=(), writes=(), pwrites=(), signal=True):
        self._deps(q, reads, writes, pwrites)
        ins = fn()
        if signal:
            self.issued[q.key] += 1
            ins.then_inc(self.sems[q.key], 1)
            ev = (q.key, self.issued[q.key])
        else:
            ev = (q.key, self.issued[q.key] + 1)
        self._record(ev, reads, writes, pwrites)
        return ins

    def dma(self, q, out, in_, reads=(), writes=(), pwrites=(), sbuf=None, **kw):
        k = sbuf.dsem
        self._deps(q, reads, writes, pwrites)
        if self.issued[k] > 0:
            q.wait(k, self.issued[k])
        ins = q.eng.dma_start(out=out, in_=in_, **kw)
        self.issued[k] += 16
        ins.then_inc(self.sems[k], 16)
        self._record((k, self.issued[k]), reads, writes, pwrites)
        return ins

    def barrier(self):
        for q in self.queues:
            for k in range(len(self.sems)):
                if self.issued[k] > 0:
                    q.wait(k, self.issued[k])


def build_program():
    nc = bass.Bass("TRN2", target_bir_lowering=False)
    declared = []

    def dt_in(name, shape, dt=F32):
        declared.append(name)
        return nc.dram_tensor(name, list(shape), dt, kind="ExternalInput").ap()
    dt_out = lambda name, shape, dt=F32: nc.dram_tensor(name, list(shape), dt, kind="ExternalOutput").ap()
    dt_tmp = lambda name, shape, dt=F32: nc.dram_tensor(name, list(shape), dt).ap()

    x_tok = dt_in("x_tok", [T, D])
    cache_k = dt_in("cache_k", [1024, 2048])
    cache_v = dt_in("cache_v", [1024, 2048])
    cache_lf = dt_in("cache_lf", [1024, 16])
    st_c = dt_in("st_c", [H_C, DK, DV])
    st_n = dt_in("st_n", [H_C, DK])
    st_m_b = dt_in("st_m_b", [P, H_C])
    norms = dt_in("norms", [P, 5, KT])
    ab_w_in_l = lambda: dt_in("ab_w_in", [D, AB_IN])
    ab_w_out_l = lambda: dt_in("ab_w_out", [D, D])
    gmlp_ws = dt_in("gmlp_ws", [8, P, P])
    gmlp_bb = dt_in("gmlp_bb", [P, 8, P])
    fox_bf = dt_in("fox_bf", [16, 1])
    c_w_in_l = lambda: dt_in("c_w_in", [D, C_IN])
    c_bif = dt_in("c_bif", [16, 1])
    c_hn = dt_in("c_hn", [P, 32])
    c_w_out_l = lambda: dt_in("c_w_out", [D, D])
    ffn_wg_l = lambda: dt_in("ffn_wg", [2, D, D_FF])
    ffn_wu_l = lambda: dt_in("ffn_wu", [2, D, D_FF])
    ffn_wd_l = lambda: dt_in("ffn_wd", [2, D_FF, D])
    consts = dt_in("consts", [P, 5, P])
    qpos_b = dt_in("qpos_b", [P, T])
    kpos = dt_in("kpos", [P, 40])
    pmask_b = dt_in("pmask_b", [P, 8])

    y_tok = dt_out("y_tok", [T, D])
    o_k = dt_out("o_k", [T, 2048])
    o_v = dt_out("o_v", [T, 2048])
    o_lf = dt_out("o_lf", [T, 16])
    o_gv = dt_out("o_gv", [TS, 2048])
    o_c = dt_out("o_c", [2, H_C, DK, DV])
    o_n = dt_out("o_n", [2, H_C, DK])
    o_m = dt_out("o_m", [2, H_C])

    xres = dt_tmp("xres", [KT, P, T])
    uT_d = dt_tmp("uT_d", [16, P, T], BF16)
    va_d = dt_tmp("va_d", [T, 2048], BF16)
    qT_d = dt_tmp("qT_d", [16, P, T], BF16)
    kT_in = [dt_tmp(f"kT_in{j}", [4 * P, TP], BF16) for j in range(4)]
    v_in = [dt_tmp(f"v_in{j}", [TP, 4 * P], BF16) for j in range(4)]
    kT_all = [dt_tmp(f"kT_all{j}", [4 * 4 * P, TP], BF16) for j in range(4)]
    v_all = [dt_tmp(f"v_all{j}", [4 * TP, 4 * P], BF16) for j in range(4)]
    kTs_d = dt_tmp("kTs_d", [16, P, TS], BF16)
    vs_d = dt_tmp("vs_d", [TS, 2048], BF16)
    c_in = dt_tmp("c_in", [16, TP])
    c_all = dt_tmp("c_all", [64, TP])
    ffn_wg, ffn_wu, ffn_wd = (ffn_wg_l(), ffn_wu_l(), ffn_wd_l()) if STAGE >= 5 else (None, None, None)

    es = ExitStack()
    with es:
        ker = Ker(nc, es)
        pe, act, dve, pool, sp = ker.pe, ker.act, ker.dve, ker.pool, ker.sp
        _uid = [0]

        def sb(st, name, shape, dt=F32):
            _uid[0] += 1
            return st.enter_context(nc.sbuf_tensor(f"{name}_{_uid[0]}", list(shape), dt))

        cst = sb(es, "cst", [P, 5, P])
        ident = cst[:, 0, :]
        ones_f = sb(es, "ones_f", [P, P])
        ones_b = sb(es, "ones_b", [P, P], BF16)
        nrm = sb(es, "nrm", [P, 5, KT])
        lfT = sb(es, "lfT", [16, T])
        b_cst = ker.buf("cst", dma=True)
        b_nrm = ker.buf("nrm", dma=True)
        b_ones = ker.buf("ones")
        b_lfT = ker.buf("lfT")
        banks = [es.enter_context(nc.psum_tensor(f"bank{i}", [P, 512], F32)) for i in range(8)]
        bbank = [ker.buf(f"bank{i}") for i in range(8)]
        for b_ in bbank:
            b_.excl = True

        ker.dma(sp, cst[:], consts, writes=[b_cst], sbuf=b_cst)
        ker.dma(sp, nrm[:], norms, writes=[b_nrm], sbuf=b_nrm)
        ker.op(dve, lambda: nc.vector.memset(ones_f[:], 1.0), writes=[b_ones])
        ker.op(dve, lambda: nc.vector.memset(ones_b[:], 1.0), writes=[b_ones])

        acc_ring = [0]

        def next_acc():
            i = acc_ring[0] % 6
            acc_ring[0] += 1
            return banks[i], bbank[i]

        tr_ring = [0]

        def tr_bank():
            i = tr_ring[0] % 2
            tr_ring[0] += 1
            return banks[6 + i], bbank[6 + i]

        def transpose_group(srcs, dst_writer):
            bk, bb = tr_bank()
            for j, (in_ap, m, k) in enumerate(srcs):
                reads_j = in_ap[1]
                ker.op(pe, lambda: nc.tensor.transpose(bk[:m, j * P:j * P + k], in_ap[0], ident[:k, :k]),
                       reads=list(reads_j) + [b_cst], writes=[bb] if j == 0 else (), pwrites=() if j == 0 else [bb])
            dst_writer(bk, bb)

        ev_alt = [0]

        def evac(out_ap, in_ap, reads, pwrites, eng=None):
            if eng is None:
                eng = act if ev_alt[0] % 2 == 0 else dve
                ev_alt[0] += 1
            if eng is act:
                return ker.op(act, lambda: nc.scalar.copy(out=out_ap, in_=in_ap), reads=reads, pwrites=pwrites)
            return ker.op(dve, lambda: nc.vector.tensor_copy(out=out_ap, in_=in_ap), reads=reads, pwrites=pwrites)

        xres_v = xres.rearrange("kt p t -> p kt t")
        b_xres = [ker.buf(f"xres{kt}") for kt in range(KT)]
        with ExitStack() as ps:
            xin = [sb(ps, f"xin{i}", [P, D]) for i in range(2)]
            b_xin = [ker.buf(f"xin{i}", dma=True) for i in range(2)]
            xst = [sb(ps, f"xst{i}", [P, KT, P]) for i in range(2)]
            b_xst = [ker.buf(f"xst{i}", dma=True) for i in range(2)]
            for ti, (t0, tn) in enumerate(TT):
                s = ti % 2
                ker.dma(sp, xin[s][:tn, :], x_tok[t0:t0 + tn, :], writes=[b_xin[s]], sbuf=b_xin[s])
                for g4 in range(KT // 4):
                    srcs = [((xin[s][:tn, (g4 * 4 + j) * P:(g4 * 4 + j + 1) * P], [b_xin[s]]), P, tn) for j in range(4)]

                    def wr(bk, bb, g4=g4, s=s, tn=tn):
                        evac(xst[s][:, g4 * 4:g4 * 4 + 4, :tn], bk[:, :].rearrange("p (j n) -> p j n", n=P)[:, :, :tn], [bb], [b_xst[s]])
                    transpose_group(srcs, wr)
                ker.dma(sp, xres_v[:, :, t0:t0 + tn], xst[s][:, :, :tn], reads=[b_xst[s]], writes=b_xres, sbuf=b_xst[s])
            ker.barrier()

        def rmsnorm_to(ps, dst, b_dst, gi, out_f32_cb=None):
            NXS = 6
            xs = [sb(ps, f"xs{i}", [P, T]) for i in range(NXS)]
            b_xs = [ker.buf(f"xs{i}", dma=True) for i in range(NXS)]
            sq = [sb(ps, f"sq{i}", [P, T]) for i in range(2)]
            b_sq = [ker.buf(f"sq{i}") for i in range(2)]
            rstd = sb(ps, "rstd", [P, T])
            b_rstd = ker.buf("rstd")
            accs = [next_acc() for _ in range(3)]
            for kt in range(KT):
                s = kt % NXS
                ker.dma(sp, xs[s][:], xres_v[:, kt, :], reads=[b_xres[kt]], writes=[b_xs[s]], sbuf=b_xs[s])
                q2 = kt % 2
                ker.op(act, lambda: nc.scalar.activation(out=sq[q2][:], in_=xs[s][:], func=AF.Square),
                       reads=[b_xs[s]], writes=[b_sq[q2]])
                for ci, (c0, cn) in enumerate(CH):
                    bk, bb = accs[ci]
                    ker.op(pe, lambda: nc.tensor.matmul(bk[:, :cn], ones_f[:], sq[q2][:, c0:c0 + cn],
                                                        start=(kt == 0), stop=(kt == KT - 1)),
                           reads=[b_sq[q2], b_ones], writes=[bb], signal=(kt == KT - 1 or True))
            for ci, (c0, cn) in enumerate(CH):
                bk, bb = accs[ci]
                ker.op(act, lambda: nc.scalar.activation(out=rstd[:, c0:c0 + cn], in_=bk[:, :cn], func=AF.Sqrt,
                                                         scale=1.0 / D, bias=eps_t[:, 0:1]),
                       reads=[bb, b_ones], pwrites=[b_rstd])
            ker.op(dve, lambda: nc.vector.reciprocal(out=rstd[:], in_=rstd[:]), reads=[b_rstd], writes=[b_rstd])
            for kt in range(KT):
                s = kt % NXS
                ker.dma(sp, xs[s][:], xres_v[:, kt, :], reads=[b_xres[kt]], writes=[b_xs[s]], sbuf=b_xs[s])
                if out_f32_cb is None:
                    ker.op(dve, lambda: nc.vector.scalar_tensor_tensor(out=dst[:, kt, :], in0=xs[s][:], scalar=nrm[:, gi, kt:kt + 1],
                                                                      in1=rstd[:], op0=ALU.mult, op1=ALU.mult),
                           reads=[b_xs[s], b_rstd, b_nrm], pwrites=[b_dst])
                else:
                    out_f32_cb(kt, xs[s], b_xs[s], rstd, b_rstd)

        eps_t = sb(es, "eps_t", [P, 1])
        ker.op(dve, lambda: nc.vector.memset(eps_t[:], EPS), writes=[b_ones])

        WS_N = 5
        WSL = {}

        def alloc_wslots(ps):
            WSL["w"] = [sb(ps, f"wslot{i}", [P, 8192], BF16) for i in range(WS_N)]
            WSL["b"] = [ker.buf(f"wslot{i}", dma="sw") for i in range(WS_N)]

        def make_accum_epi(ost, b_ost):
            st = {"i": 0}

            def epi(tag, ci, c0, cn, m, bk, bb):
                o = tag[1]
                if ci == 0:
                    st["i"] += 1
                s = st["i"] % 2
                evac(ost[s][:, c0:c0 + cn], bk[:, :cn], [bb], [b_ost[s]])
                if ci == len(CH) - 1:
                    ker.dma(pool, xres_v[:, o, :], ost[s][:], reads=[b_ost[s]], writes=[b_xres[o]], sbuf=b_ost[s], accum_op=ALU.add)
            return epi

        def gemm(A_of, blocks, jobs, chunks, epilogue):
            wslots, b_wslots = WSL["w"], WSL["b"]
            nblk = len(blocks)
            loaded = [0]
            ring = [0]
            slot_of = {}

            def load_next():
                bi = loaded[0]
                if bi >= nblk:
                    return
                Wap, ktn, ncols = blocks[bi]
                s = ring[0] % WS_N
                ring[0] += 1
                slot_of[bi] = s
                dstv = wslots[s][:, 0:ktn * ncols].rearrange("p (kt n) -> p kt n", n=ncols)
                ker.dma(pool, dstv, Wap.rearrange("(kt p) n -> p kt n", p=P), writes=[b_wslots[s]], sbuf=b_wslots[s])
                loaded[0] += 1

            last_use = {}
            for ji, (bi, co, m, tag) in enumerate(jobs):
                last_use[bi] = ji
            for _ in range(min(WS_N - 1, nblk)):
                load_next()
            for ji, (bi, co, m, tag) in enumerate(jobs):
                while bi >= loaded[0]:
                    load_next()
                Wap, ktn, ncols = blocks[bi]
                s = slot_of[bi]
                wv = wslots[s][:, 0:ktn * ncols].rearrange("p (kt n) -> p kt n", n=ncols)
                for ci, (c0, cn) in enumerate(chunks):
                    bk, bb = next_acc()
                    for kt in range(ktn):
                        a_ap, a_b = A_of(tag, kt)
                        ker.op(pe, lambda: nc.tensor.matmul(bk[:m, :cn], wv[:, kt, co:co + m], a_ap[:, c0:c0 + cn],
                                                            start=(kt == 0), stop=(kt == ktn - 1)),
                               reads=[b_wslots[s], a_b], writes=[bb], signal=(kt == ktn - 1))
                    epilogue(tag, ci, c0, cn, m, bk, bb)
                if last_use[bi] == ji:
                    load_next()

        with ExitStack() as pA:
            A = sb(pA, "A", [P, KT, T], BF16)
            b_A = ker.buf("A")
            with ExitStack() as ps:
                rmsnorm_to(ps, A, b_A, 0)
                ker.barrier()
            if STAGE >= 1 and 'l0' not in SKIP:
                with ExitStack() as ps:
                    alloc_wslots(ps)
                    stg = [sb(ps, f"stg{i}", [P, T]) for i in range(2)]
                    b_stg = [ker.buf(f"stg{i}") for i in range(2)]
                    stb = [sb(ps, f"stb{i}", [P, T], BF16) for i in range(2)]
                    b_stb = [ker.buf(f"stb{i}", dma=True) for i in range(2)]
                    tko = [sb(ps, f"tko{i}", [P, 9, P]) for i in range(2)]
                    b_tko = [ker.buf(f"tko{i}", dma=True) for i in range(2)]
                    tkb = [sb(ps, f"tkb{i}", [P, 9, P], BF16) for i in range(2)]
                    b_tkb = [ker.buf(f"tkb{i}", dma=True) for i in range(2)]
                    nbf = sb(ps, "nbf", [16, 1])
                    b_nbf = ker.buf("nbf", dma=True)
                    lft = sb(ps, "lft", [P, 9, 16])
                    b_lft = ker.buf("lft", dma=True)
                    ker.dma(sp, nbf[:], fox_bf, writes=[b_nbf], sbuf=b_nbf)
                    ker.op(dve, lambda: nc.vector.tensor_scalar(out=nbf[:], in0=nbf[:], scalar1=-1.0, scalar2=None, op0=ALU.mult),
                           reads=[b_nbf], writes=[b_nbf])
                    ctr = [0]
                    cur = {}

                    def store_tokmajor(dst_dram, col0, src, b_src):
                        ker.dma(sp, dst_dram[0:TP, col0:col0 + P].rearrange("(tt p) n -> p tt n", p=P), src[:, 0:8, :],
                                reads=[b_src], sbuf=b_src)
                        ker.dma(sp, dst_dram[TP:T, col0:col0 + P], src[:TS, 8, :], reads=[b_src], sbuf=b_src)

                    def epi(tag, ci, c0, cn, m, bk, bb):
                        kind, idx = tag
                        if ci == 0:
                            cur["s"] = ctr[0] % 2
                            ctr[0] += 1
                        s = cur["s"]
                        if kind in ("u", "q"):
                            evac(stb[s][:, c0:c0 + cn], bk[:, :cn], [bb], [b_stb[s]])
                            if ci == 2:
                                dstd = uT_d if kind == "u" else qT_d
                                ker.dma(sp, dstd[idx], stb[s][:], reads=[b_stb[s]], sbuf=b_stb[s])
                            return
                        if kind == "f":
                            ker.op(act, lambda: nc.scalar.activation(out=lfT[:, c0:c0 + cn], in_=bk[:16, :cn], func=AF.Exp,
                                                                     scale=-1.0, bias=nbf[:, 0:1]),
                                   reads=[bb, b_nbf], pwrites=[b_lfT])
                            ker.op(act, lambda: nc.scalar.activation(out=lfT[:, c0:c0 + cn], in_=lfT[:, c0:c0 + cn], func=AF.Ln,
                                                                     scale=1.0, bias=ones_f[:16, 0:1]),
                                   reads=[b_lfT, b_ones], pwrites=[b_lfT])
                            ker.op(dve, lambda: nc.vector.tensor_scalar(out=lfT[:, c0:c0 + cn], in0=lfT[:, c0:c0 + cn], scalar1=-1.0,
                                                                        scalar2=None, op0=ALU.mult),
                                   reads=[b_lfT], pwrites=[b_lfT])
                            if ci == 2:
                                bk2, bb2 = tr_bank()
                                for ti, (t0, tn) in enumerate(TT):
                                    ker.op(pe, lambda: nc.tensor.transpose(bk2[:tn, ti * 16:(ti + 1) * 16], lfT[:, t0:t0 + tn], ident[:16, :16]),
                                           reads=[b_lfT, b_cst], writes=[bb2] if ti == 0 else (), pwrites=() if ti == 0 else [bb2])
                                evac(lft[:, 0:8, :], bk2[:, 0:128].rearrange("p (j n) -> p j n", n=16), [bb2], [b_lft])
                                evac(lft[:TS, 8, :], bk2[:TS, 128:144], [bb2], [b_lft])
                                ker.dma(sp, o_lf[0:TP, :].rearrange("(tt p) n -> p tt n", p=P), lft[:, 0:8, :], reads=[b_lft], sbuf=b_lft)
                                ker.dma(sp, o_lf[TP:T, :], lft[:TS, 8, :], reads=[b_lft], sbuf=b_lft)
                            return
                        evac(stg[s][:, c0:c0 + cn], bk[:, :cn], [bb], [b_stg[s]])
                        if kind == "k":
                            ker.op(dve, lambda: nc.vector.tensor_copy(out=stb[s][:, c0:c0 + cn], in_=stg[s][:, c0:c0 + cn]),
                                   reads=[b_stg[s]], pwrites=[b_stb[s]])
                        if ci != 2:
                            return
                        if kind == "k":
                            ker.dma(sp, kT_in[idx // 4][(idx % 4) * P:(idx % 4 + 1) * P, :], stb[s][:, 0:TP], reads=[b_stb[s]], sbuf=b_stb[s])
                            ker.dma(sp, kTs_d[idx], stb[s][:, TP:T], reads=[b_stb[s]], sbuf=b_stb[s])
                        for g0, g1 in ((0, 4), (4, 8), (8, 9)):
                            srcs = [((stg[s][:, TT[ti][0]:TT[ti][0] + TT[ti][1]], [b_stg[s]]), TT[ti][1], P) for ti in range(g0, g1)]

                            def wr(bk, bb, g0=g0, g1=g1, s=s, kind=kind):
                                if g0 < 8:
                                    src = bk[:, :].rearrange("p (j n) -> p j n", n=P)
                                    if kind in ("k", "v"):
                                        ker.op(act, lambda: nc.scalar.copy(out=tko[s][:, g0:g1, :], in_=src), reads=[bb], pwrites=[b_tko[s]])
                                    if kind in ("va", "v"):
                                        ker.op(dve, lambda: nc.vector.tensor_copy(out=tkb[s][:, g0:g1, :], in_=src), reads=[bb], pwrites=[b_tkb[s]])
                                else:
                                    ker.op(act, lambda: nc.scalar.copy(out=tko[s][:TS, 8, :], in_=bk[:TS, 0:P]), reads=[bb], pwrites=[b_tko[s]])
                                    if kind in ("va", "v"):
                                        ker.op(dve, lambda: nc.vector.tensor_copy(out=tkb[s][:TS, 8, :], in_=bk[:TS, 0:P]), reads=[bb], pwrites=[b_tkb[s]])
                            transpose_group(srcs, wr)
                        col0 = idx * P
                        if kind == "k":
                            store_tokmajor(o_k, col0, tko[s], b_tko[s])
                        elif kind == "v":
                            store_tokmajor(o_v, col0, tko[s], b_tko[s])
                            ker.dma(sp, v_in[idx // 4][:, (idx % 4) * P:(idx % 4 + 1) * P].rearrange("(tt p) n -> p tt n", p=P), tkb[s][:, 0:8, :],
                                    reads=[b_tkb[s]], sbuf=b_tkb[s])
                            ker.dma(sp, vs_d[:, col0:col0 + P], tkb[s][:TS, 8, :], reads=[b_tkb[s]], sbuf=b_tkb[s])
                        else:
                            ker.dma(sp, o_gv[:, col0:col0 + P], tko[s][:TS, 8, :], reads=[b_tko[s]], sbuf=b_tko[s])
                            ker.dma(sp, va_d[0:TP, col0:col0 + P].rearrange("(tt p) n -> p tt n", p=P), tkb[s][:, 0:8, :],
                                    reads=[b_tkb[s]], sbuf=b_tkb[s])
                            ker.dma(sp, va_d[TP:T, col0:col0 + P], tkb[s][:TS, 8, :], reads=[b_tkb[s]], sbuf=b_tkb[s])

                    ab_w_in = ab_w_in_l()
                    blocks, jobs = [], []
                    kinds = ["u"] * 16 + ["va"] * 16 + ["q"] * 16 + ["k"] * 16 + ["v"] * 16
                    for b in range(40):
                        blocks.append((ab_w_in[:, b * 256:(b + 1) * 256], KT, 256))
                        for j in range(2):
                            oi = b * 2 + j
                            jobs.append((b, j * 128, 128, (kinds[oi], oi % 16)))
                    blocks.append((ab_w_in[:, 10128:10256], KT, 128))
                    jobs.append((40, 112, 16, ("f", 0)))
                    if os.environ.get("KG1"):
                        keep = os.environ["KG1"].split(",")
                        jobs = [j for j in jobs if j[3][0] in keep and j[3][1] < int(os.environ.get("KG1N", "16"))]
                    gemm(lambda tag, kt: (A[:, kt, :], b_A), blocks, jobs, CH, epi)
                    ker.barrier()

        coll_sems = [ker.new_sem(f"cc{i}") for i in range(14)]
        coll_i = [0]

        def all_gather(src, dst, reads, writes, pwrites=()):
            k = coll_sems[coll_i[0]]
            coll_i[0] += 1
            ker._deps(pool, reads, writes, pwrites)
            ins = nc.gpsimd.collective_compute("AllGather", ALU.bypass, replica_groups=GROUPS,
                                               ins=[src.opt()], outs=[dst.opt()])
            ins.then_inc(ker.sems[k], 1)
            ker.issued[k] += 1
            ker._record((k, 1), reads, writes, pwrites)

        b_gath = {n: ker.buf(n) for n in ("kT_all", "v_all", "c_all", "sum_all")}
        if STAGE >= 2 and 'l0' not in SKIP:
          with ExitStack() as pB:
            aoT = sb(pB, "aoT", [P, KT, T], BF16)
            b_aoT = ker.buf("aoT")
            for j in range(4):
                all_gather(kT_in[j], kT_all[j], [], [b_gath["kT_all"]] if j == 0 else [], [] if j == 0 else [b_gath["kT_all"]])
                all_gather(v_in[j], v_all[j], [], [b_gath["v_all"]] if j == 0 else [], [] if j == 0 else [b_gath["v_all"]])
            with ExitStack() as ps:
                wsT = sb(ps, "wsT", [P, 8, P], BF16)
                b_wsT = ker.buf("wsT")
                wraw = sb(ps, "wraw", [P, 8, P])
                b_wraw = ker.buf("wraw", dma=True)
                bbt = sb(ps, "bbt", [P, 8, P])
                b_bbt = ker.buf("bbt", dma=True)
                vat = [sb(ps, f"vat{i}", [P, 2048], BF16) for i in range(2)]
                b_vat = [ker.buf(f"vat{i}", dma=True) for i in range(2)]
                ut = [sb(ps, f"ut{i}", [P, 16, P], BF16) for i in range(2)]
                b_ut = [ker.buf(f"ut{i}", dma=True) for i in range(2)]
                gtmp = [sb(ps, f"gtmp{i}", [P, P]) for i in range(2)]
                b_gtmp = [ker.buf(f"gtmp{i}") for i in range(2)]
                ker.dma(sp, wraw[:], gmlp_ws.rearrange("g r s -> r g s"), writes=[b_wraw], sbuf=b_wraw)
                ker.dma(sp, bbt[:], gmlp_bb, writes=[b_bbt], sbuf=b_bbt)
                for g in range(8):
                    ker.op(dve, lambda: nc.vector.tensor_tensor(out=wraw[:, g, :], in0=wraw[:, g, :], in1=cst[:, 1, :], op=ALU.mult),
                           reads=[b_wraw, b_cst], writes=[b_wraw])
                for gg in range(2):
                    srcs = [((wraw[:, gg * 4 + j, :], [b_wraw]), P, P) for j in range(4)]

                    def wr(bk, bb, gg=gg):
                        evac(wsT[:, gg * 4:gg * 4 + 4, :], bk[:, :].rearrange("p (j n) -> p j n", n=P), [bb], [b_wsT])
                    transpose_group(srcs, wr)
                uT_v = uT_d.rearrange("f p t -> p f t")
                for ti, (t0, tn) in enumerate(TT):
                    s2 = ti % 2
                    ker.dma(sp, vat[s2][:tn, :], va_d[t0:t0 + tn, :], writes=[b_vat[s2]], sbuf=b_vat[s2])
                    ker.dma(sp, ut[s2][:, :, :tn], uT_v[:, :, t0:t0 + tn], writes=[b_ut[s2]], sbuf=b_ut[s2])
                    for ft in range(16):
                        g = ft // 2
                        bk, bb = next_acc()
                        ker.op(pe, lambda: nc.tensor.matmul(bk[:, :tn], vat[s2][:tn, ft * P:(ft + 1) * P], wsT[:tn, g, :tn], start=True, stop=True),
                               reads=[b_vat[s2], b_wsT], writes=[bb])
                        s3 = ft % 2
                        ker.op(dve, lambda: nc.vector.tensor_tensor(out=gtmp[s3][:, :tn], in0=bk[:, :tn], in1=bbt[:, g, :tn], op=ALU.add),
                               reads=[bb, b_bbt], writes=[b_gtmp[s3]])
                        ker.op(dve, lambda: nc.vector.tensor_tensor(out=aoT[:, ft, t0:t0 + tn], in0=gtmp[s3][:, :tn], in1=ut[s2][:, ft, :tn], op=ALU.mult),
                               reads=[b_gtmp[s3], b_ut[s2]], pwrites=[b_aoT])
                ker.barrier()
            if STAGE >= 3:
             with ExitStack() as psT:
              biasT = sb(psT, "biasT", [P, 2, 32, 16])
              biasS = sb(psT, "biasS", [P, 9, 16])
              qpb = sb(psT, "qpb", [P, T])
              kps = sb(psT, "kps", [P, 40])
              ps = ExitStack()
              ps.__enter__()
              if True:
                one16 = sb(ps, "one16", [16, T])
                b_one16 = ker.buf("one16")
                cs = sb(ps, "cs", [16, TP])
                b_cs = ker.buf("cs", dma=True)
                lfs = sb(ps, "lfs", [16, T])
                b_lfs = ker.buf("lfs")
                css = sb(ps, "css", [16, T])
                b_css = ker.buf("css")
                clf = sb(ps, "clf", [P, 8, 16])
                b_clf = ker.buf("clf", dma=True)
                cg = sb(ps, "cg", [16, 4, TP])
                b_cg = ker.buf("cg", dma=True)
                offs = sb(ps, "offs", [16, 4])
                b_offs = ker.buf("offs")
                cql = sb(ps, "cql", [16, TP])
                b_cql = ker.buf("cql")
                pmk = sb(ps, "pmk", [P, 8])
                b_pmk = ker.buf("pmk", dma=True)
                ckT = sb(ps, "ckT", [P, 33, 16])
                b_ckT = ker.buf("ckT")
                ckTs = sb(ps, "ckTs", [P, 9, 16])
                b_ckTs = ker.buf("ckTs")
                cqe = sb(ps, "cqe", [P, 3, 16])
                b_cqe = ker.buf("cqe")
                cref = sb(ps, "cref", [P, 3, 16])
                b_cref = ker.buf("cref")
                b_biasT = ker.buf("biasT")
                b_biasS = ker.buf("biasS")
                b_qpb = ker.buf("qpb", dma=True)
                b_kps = ker.buf("kps", dma=True)
                ker.dma(sp, pmk[:], pmask_b, writes=[b_pmk], sbuf=b_pmk)
                ker.dma(sp, qpb[:], qpos_b, writes=[b_qpb], sbuf=b_qpb)
                ker.dma(sp, kps[:], kpos, writes=[b_kps], sbuf=b_kps)
                ker.dma(sp, clf[:], cache_lf.rearrange("(kt p) h -> p kt h", p=P), writes=[b_clf], sbuf=b_clf)
                ker.op(dve, lambda: nc.vector.memset(one16[:], 1.0), writes=[b_one16])
                ker.op(dve, lambda: nc.vector.tensor_tensor_scan(out=cs[:], data0=one16[:, 0:TP], data1=lfT[:, 0:TP], initial=0.0,
                                                               op0=ALU.mult, op1=ALU.add),
                       reads=[b_one16, b_lfT], writes=[b_cs])
                b_c_in = ker.buf("c_in")
                ker.dma(sp, c_in, cs[:], reads=[b_cs], writes=[b_c_in], sbuf=b_cs)
                all_gather(c_in, c_all, [b_c_in], [b_gath["c_all"]])
                bk2, bb2 = tr_bank()
                for kt in range(8):
                    ker.op(pe, lambda: nc.tensor.transpose(bk2[:16, kt * P:(kt + 1) * P] if kt < 4 else bk2[:16, (kt - 4) * P:(kt - 3) * P],
                                                           clf[:, kt, :], ident[:, :]),
                           reads=[b_clf, b_cst], writes=[bb2] if kt % 4 == 0 else (), pwrites=() if kt % 4 == 0 else [bb2])
                    if kt % 4 == 3:
                        evac(lfs[:, (kt - 3) * P:(kt + 1) * P], bk2[:16, :], [bb2], [b_lfs])
                        if kt == 3:
                            bk2, bb2 = tr_bank()
                ker.op(dve, lambda: nc.vector.tensor_copy(out=lfs[:, TP:T], in_=lfT[:, TP:T]), reads=[b_lfT], pwrites=[b_lfs])
                ker.op(dve, lambda: nc.vector.tensor_tensor_scan(out=css[:], data0=one16[:], data1=lfs[:], initial=0.0,
                                                               op0=ALU.mult, op1=ALU.add),
                       reads=[b_one16, b_lfs], writes=[b_css])
                ker.dma(sp, cg[:], c_all.rearrange("(r h) t -> h r t", h=16), reads=[b_gath["c_all"]], writes=[b_cg], sbuf=b_cg)
                ker.op(dve, lambda: nc.vector.tensor_copy(out=offs[:, 1:2], in_=cg[:, 0, TP - 1:TP]), reads=[b_cg], writes=[b_offs])
                for r in (2, 3):
                    ker.op(dve, lambda: nc.vector.tensor_tensor(out=offs[:, r:r + 1], in0=offs[:, r - 1:r], in1=cg[:, r - 1, TP - 1:TP], op=ALU.add),
                           reads=[b_cg, b_offs], writes=[b_offs])
                for r in (1, 2, 3):
                    ker.op(dve, lambda: nc.vector.tensor_scalar(out=cg[:, r, :], in0=cg[:, r, :], scalar1=offs[:, r:r + 1], scalar2=None, op0=ALU.add),
                           reads=[b_cg, b_offs], writes=[b_cg])
                ker.op(dve, lambda: nc.vector.tensor_scalar(out=cql[:], in0=cg[:, 0, :], scalar1=pmk[:16, 4:5], scalar2=None, op0=ALU.mult),
                       reads=[b_cg, b_pmk], writes=[b_cql])
                for r in (1, 2, 3):
                    ker.op(dve, lambda: nc.vector.scalar_tensor_tensor(out=cql[:], in0=cg[:, r, :], scalar=pmk[:16, 4 + r:5 + r], in1=cql[:],
                                                                      op0=ALU.mult, op1=ALU.add),
                           reads=[b_cg, b_pmk, b_cql], writes=[b_cql])
                bk2, bb2 = tr_bank()
                for kt in range(32):
                    ker.op(pe, lambda: nc.tensor.transpose(bk2[:, kt * 16:(kt + 1) * 16], cg[:, kt // 8, (kt % 8) * P:(kt % 8 + 1) * P], ident[:16, :16]),
                           reads=[b_cg, b_cst], writes=[bb2] if kt == 0 else (), pwrites=() if kt == 0 else [bb2])
                evac(ckT[:, 0:32, :], bk2[:, :].rearrange("p (j n) -> p j n", n=16), [bb2], [b_ckT])
                bk2, bb2 = tr_bank()
                ker.op(dve, lambda: nc.vector.memset(ckTs[:], 0.0), writes=[b_ckTs])
                ker.op(dve, lambda: nc.vector.memset(cqe[:], 0.0), writes=[b_cqe])
                for kt in range(9):
                    tn = 128 if kt < 8 else 16
                    ker.op(pe, lambda: nc.tensor.transpose(bk2[:tn, kt * 16:(kt + 1) * 16], css[:, kt * P:kt * P + tn], ident[:16, :16]),
                           reads=[b_css, b_cst], writes=[bb2] if kt == 0 else (), pwrites=() if kt == 0 else [bb2])
                for j, tl in enumerate((3, 7)):
                    ker.op(pe, lambda: nc.tensor.transpose(bk2[:, (9 + j) * 16:(10 + j) * 16], cql[:, tl * P:(tl + 1) * P], ident[:16, :16]),
                           reads=[b_cql, b_cst], pwrites=[bb2])
                evac(ckTs[:, 0:8, :], bk2[:, 0:128].rearrange("p (j n) -> p j n", n=16), [bb2], [b_ckTs])
                evac(ckTs[:16, 8, :], bk2[:16, 128:144], [bb2], [b_ckTs])
                evac(cqe[:, 0:2, :], bk2[:, 144:176].rearrange("p (j n) -> p j n", n=16), [bb2], [b_cqe])
                ker.op(dve, lambda: nc.vector.tensor_copy(out=cqe[:16, 2, :], in_=ckTs[:16, 8, :]), reads=[b_ckTs], pwrites=[b_cqe])
                bk3, bb3 = next_acc()
                for j in range(3):
                    selm = cst[:, 3, :] if j < 2 else cst[:, 4, :]
                    ker.op(pe, lambda: nc.tensor.matmul(bk3[:, j * 16:(j + 1) * 16], selm, cqe[:, j, :], start=True, stop=True),
                           reads=[b_cqe, b_cst], writes=[bb3] if j == 0 else (), pwrites=() if j == 0 else [bb3])
                evac(cref[:, :, :], bk3[:, 0:48].rearrange("p (j n) -> p j n", n=16), [bb3], [b_cref])
                for qb in range(2):
                    for kt in range(32):
                        ker.op(dve, lambda: nc.vector.tensor_tensor(out=biasT[:, qb, kt, :], in0=cref[:, qb, :], in1=ckT[:, kt, :], op=ALU.subtract),
                               reads=[b_cref, b_ckT], pwrites=[b_biasT])
                ker.op(dve, lambda: nc.vector.tensor_scalar(out=biasT[:], in0=biasT[:], scalar1=0.0, scalar2=None, op0=ALU.min),
                       reads=[b_biasT], writes=[b_biasT])
                for kt in range(9):
                    ker.op(dve, lambda: nc.vector.tensor_tensor(out=biasS[:, kt, :], in0=cref[:, 2, :], in1=ckTs[:, kt, :], op=ALU.subtract),
                           reads=[b_cref, b_ckTs], pwrites=[b_biasS])
                ker.op(dve, lambda: nc.vector.tensor_scalar(out=biasS[:], in0=biasS[:], scalar1=0.0, scalar2=None, op0=ALU.min),
                       reads=[b_biasS], writes=[b_biasS])

                ker.barrier()
                ps.__exit__(None, None, None)
                ps = psT
                SCALE = 128.0 ** -0.5
                pf = [sb(ps, f"pf{i}", [P, 512]) for i in range(2)]
                b_pf = [ker.buf(f"pf{i}") for i in range(2)]
                pm = [sb(ps, f"pm{i}", [P, 512], BF16) for i in range(2)]
                b_pm = [ker.buf(f"pm{i}") for i in range(2)]
                rl = sb(ps, "rl", [P, 512])
                b_rl = ker.buf("rl")

                def attention(qh_ap, b_qh, q0, nq, keys, qp0, out_ap, olb):
                    bO, bbO, bL, bbL = olb
                    nk = len(keys)
                    pend = None

                    def tail(it):
                        kt, i = it
                        _, v_ap, bufs, _, _ = keys[kt]
                        ker.op(pe, lambda: nc.tensor.matmul(bO[:, :nq], v_ap, pm[i][:, :nq], start=(kt == 0), stop=(kt == nk - 1)),
                               reads=[b_pm[i]] + bufs, writes=[bbO], signal=(kt == nk - 1))
                        ker.op(pe, lambda: nc.tensor.matmul(bL[:, :nq], ones_b[:, :], pm[i][:, :nq], start=(kt == 0), stop=(kt == nk - 1)),
                               reads=[b_pm[i], b_ones], writes=[bbL], signal=True)
                    for kt in range(nk):
                        kT_ap, v_ap, bufs, kp_ap, bias_ap = keys[kt]
                        i = kt % 2
                        bS, bbS = banks[i], bbank[i]
                        ker.op(pe, lambda: nc.tensor.matmul(bS[:, :nq], kT_ap, qh_ap[:, q0:q0 + nq], start=True, stop=True),
                               reads=[b_qh] + bufs, writes=[bbS])
                        if pend is not None:
                            tail(pend)
                        ker.op(act, lambda: nc.scalar.activation(out=pf[i][:, :nq], in_=bS[:, :nq], func=AF.Exp, scale=SCALE, bias=bias_ap),
                               reads=[bbS, b_biasT, b_biasS], writes=[b_pf[i]])
                        ker.op(dve, lambda: nc.vector.scalar_tensor_tensor(out=pm[i][:, :nq], in0=qpb[:, qp0:qp0 + nq], scalar=kp_ap, in1=pf[i][:, :nq],
                                                                          op0=ALU.is_ge, op1=ALU.mult),
                               reads=[b_pf[i], b_qpb, b_kps], writes=[b_pm[i]])
                        pend = (kt, i)
                    tail(pend)
                    ker.op(dve, lambda: nc.vector.reciprocal(out=rl[:, :nq], in_=bL[:, :nq]), reads=[bbL], writes=[b_rl])
                    ker.op(dve, lambda: nc.vector.tensor_tensor(out=out_ap, in0=bO[:, :nq], in1=rl[:, :nq], op=ALU.mult),
                           reads=[bbO, b_rl], pwrites=[b_aoT])

                olbs = [(banks[2], bbank[2], banks[3], bbank[3]), (banks[4], bbank[4], banks[5], bbank[5])]
                olb_i = [0]
                ps = ExitStack()
                ps.__enter__()
                kTh = [sb(ps, f"kTh{i}", [P, 4, TP], BF16) for i in range(2)]
                b_kTh = [ker.buf(f"kTh{i}", dma=True) for i in range(2)]
                vh = [sb(ps, f"vh{i}", [P, 32, P], BF16) for i in range(2)]
                b_vh = [ker.buf(f"vh{i}", dma=True) for i in range(2)]
                qh = [sb(ps, f"qh{i}", [P, T], BF16) for i in range(2)]
                b_qh = [ker.buf(f"qh{i}", dma=True) for i in range(2)]
                kT_all_v = [t_.rearrange("(r h d) t -> h d r t", h=4, d=P) for t_ in kT_all]
                v_all_v = [t_.rearrange("(kt p) n -> p kt n", p=P) for t_ in v_all]

                def load_head(h):
                    s2 = h % 2
                    ker.dma(sp, kTh[s2][:], kT_all_v[h // 4][h % 4], reads=[b_gath["kT_all"]], writes=[b_kTh[s2]], sbuf=b_kTh[s2])
                    ker.dma(sp, vh[s2][:], v_all_v[h // 4][:, :, (h % 4) * P:(h % 4 + 1) * P], reads=[b_gath["v_all"]], writes=[b_vh[s2]], sbuf=b_vh[s2])
                    ker.dma(sp, qh[s2][:], qT_d[h], writes=[b_qh[s2]], sbuf=b_qh[s2])
                load_head(0)
                for h in range(16):
                    s2 = h % 2
                    if h + 1 < 16:
                        load_head(h + 1)
                    for qb in range(2):
                        keys = [(kTh[s2][:, kt // 8, (kt % 8) * P:(kt % 8 + 1) * P], vh[s2][:, kt, :], [b_kTh[s2], b_vh[s2]],
                                 kps[:, kt:kt + 1], biasT[:, qb, kt, h:h + 1]) for kt in range(32)]
                        attention(qh[s2], b_qh[s2], qb * 512, 512, keys, qb * 512, aoT[:, 16 + h, qb * 512:(qb + 1) * 512], olbs[olb_i[0] % 2])
                        olb_i[0] += 1
                ker.barrier()
                ps.__exit__(None, None, None)
                psS = ExitStack()
                psS.__enter__()
                kTs_all = sb(psS, "kTs_all", [P, 16, 1152], BF16)
                b_kTs = ker.buf("kTs_all", dma=True)
                vS_all = sb(psS, "vS_all", [P, 9, 2048], BF16)
                b_vS = ker.buf("vS_all", dma="sw")
                ckin = [sb(psS, f"ckin{i}", [P, 2048]) for i in range(2)]
                b_ckin = [ker.buf(f"ckin{i}", dma=True) for i in range(2)]
                qs = sb(psS, "qs", [P, 16, TS], BF16)
                b_qs = ker.buf("qs", dma=True)
                ker.op(dve, lambda: nc.vector.memset(kTs_all[:, :, TP:1152], 0.0), writes=[b_kTs])
                ker.op(dve, lambda: nc.vector.memset(vS_all[:, 8, :], 0.0), writes=[b_vS])
                ker.dma(sp, kTs_all[:, :, TP:T], kTs_d.rearrange("h d t -> d h t"), pwrites=[b_kTs], sbuf=b_kTs)
                ker.dma(pool, vS_all[:, 0:8, :], cache_v.rearrange("(kt p) n -> p kt n", p=P), pwrites=[b_vS], sbuf=b_vS)
                ker.dma(sp, vS_all[:TS, 8, :], vs_d, pwrites=[b_vS], sbuf=b_qs)
                ker.dma(sp, qs[:], qT_d.rearrange("h d t -> d h t")[:, :, TP:T], writes=[b_qs], sbuf=b_qs)
                for kt in range(8):
                    s2 = kt % 2
                    ker.dma(sp, ckin[s2][:], cache_k[kt * P:(kt + 1) * P, :], writes=[b_ckin[s2]], sbuf=b_ckin[s2])
                    for g4 in range(4):
                        srcs = [((ckin[s2][:, (g4 * 4 + j) * P:(g4 * 4 + j + 1) * P], [b_ckin[s2]]), P, P) for j in range(4)]

                        def wr(bk, bb, g4=g4, kt=kt):
                            evac(kTs_all[:, g4 * 4:g4 * 4 + 4, kt * P:(kt + 1) * P], bk[:, :].rearrange("p (j n) -> p j n", n=P), [bb], [b_kTs])
                        transpose_group(srcs, wr)
                for h in range(16):
                    keys = [(kTs_all[:, h, kt * P:(kt + 1) * P], vS_all[:, kt, h * P:(h + 1) * P], [b_kTs, b_vS],
                             kps[:, kt:kt + 1] if kt < 8 else kps[:, 32:33], biasS[:, kt, h:h + 1]) for kt in range(9)]
                    attention(qs[:, h, :], b_qs, 0, TS, keys, TP, aoT[:, 16 + h, TP:T], olbs[olb_i[0] % 2])
                    olb_i[0] += 1
                ker.barrier()
                psS.__exit__(None, None, None)
                if os.environ.get("KDEBUG"):
                    b_dbg = ker.buf("dbg", dma=True)
                    ker.dma(sp, dt_tmp("dbg_ao", [KT, P, T], BF16).rearrange("k p t -> p k t"), aoT[:], reads=[b_aoT], sbuf=b_dbg)
                    ker.barrier()

            if STAGE >= 4:
              with ExitStack() as ps:
                alloc_wslots(ps)
                ost = [sb(ps, f"ost{i}", [P, T]) for i in range(2)]
                b_ost = [ker.buf(f"ost{i}", dma="sw") for i in range(2)]
                ab_w_out = ab_w_out_l()
                blocks = [(ab_w_out[:, b * 256:(b + 1) * 256], KT, 256) for b in range(16)]
                jobs = [(b, j * P, P, ("o", b * 2 + j)) for b in range(16) for j in range(2)]
                accum_epilogue = make_accum_epi(ost, b_ost)
                gemm(lambda tag, kt: (aoT[:, kt, :], b_aoT), blocks, jobs, CH, accum_epilogue)
                ker.barrier()

        def ffn(layer):
            with ExitStack() as pA:
                A = sb(pA, "Af", [P, KT, T], BF16)
                b_A = ker.buf("Af")
                with ExitStack() as ps:
                    rmsnorm_to(ps, A, b_A, 1 + 2 * layer)
                    ker.barrier()
                with ExitStack() as ps:
                    alloc_wslots(ps)
                    hid = sb(ps, "hid", [P, 8, T], BF16)
                    b_hid = [ker.buf(f"hid{j}") for j in range(8)]
                    sg = [sb(ps, f"sg{i}", [P, T]) for i in range(2)]
                    b_sg = [ker.buf(f"sg{i}") for i in range(2)]
                    ost = [sb(ps, f"ost{i}", [P, T]) for i in range(2)]
                    b_ost = [ker.buf(f"ost{i}", dma="sw") for i in range(2)]
                    acc_epi = make_accum_epi(ost, b_ost)
                    wg, wu, wd = ffn_wg[layer], ffn_wu[layer], ffn_wd[layer]
                    NJ = D_FF // P
                    blocks, jobs = [], []
                    for j0 in range(0, NJ, 8):
                        nj = min(8, NJ - j0)
                        for jj in range(0, nj, 2):
                            c0 = (j0 + jj) * P
                            nt = min(2, nj - jj)
                            bg = len(blocks)
                            blocks.append((wg[:, c0:c0 + nt * P], KT, nt * P))
                            blocks.append((wu[:, c0:c0 + nt * P], KT, nt * P))
                            for t2 in range(nt):
                                jobs.append((bg, t2 * P, P, ("g", jj + t2)))
                                jobs.append((bg + 1, t2 * P, P, ("u", jj + t2)))
                        for b4 in range(4):
                            bd = len(blocks)
                            blocks.append((wd[j0 * P:(j0 + nj) * P, b4 * 1024:(b4 + 1) * 1024], nj, 1024))
                            for o8 in range(8):
                                jobs.append((bd, o8 * P, P, ("d", b4 * 8 + o8)))
                    cur = {"i": 0}

                    def A_of(tag, kt):
                        if tag[0] == "d":
                            return hid[:, kt, :], b_hid[kt]
                        return A[:, kt, :], b_A

                    def epi(tag, ci, c0, cn, m, bk, bb):
                        kind, idx = tag
                        if kind == "d":
                            return acc_epi(tag, ci, c0, cn, m, bk, bb)
                        if kind == "g":
                            if ci == 0:
                                cur["i"] += 1
                            s = cur["i"] % 2
                            ker.op(act, lambda: nc.scalar.activation(out=sg[s][:, c0:c0 + cn], in_=bk[:, :cn], func=AF.Silu),
                                   reads=[bb], pwrites=[b_sg[s]] if ci > 0 else (), writes=[b_sg[s]] if ci == 0 else ())
                        else:
                            s = cur["i"] % 2
                            ker.op(dve, lambda: nc.vector.tensor_tensor(out=hid[:, idx, c0:c0 + cn], in0=bk[:, :cn], in1=sg[s][:, c0:c0 + cn], op=ALU.mult),
                                   reads=[bb, b_sg[s]], pwrites=[b_hid[idx]])
                    gemm(A_of, blocks, jobs, CH, epi)
                    ker.barrier()

        if STAGE >= 5 and 'ffn0' not in SKIP:
            ffn(0)

        def layer1():
            mq_d = dt_tmp("mq_d", [16, P, T], BF16)
            mkT_d = dt_tmp("mkT_d", [16, P, T], BF16)
            mk_d = dt_tmp("mk_d", [T, 2048], BF16)
            mv_d = dt_tmp("mv_d", [T, 4096], BF16)
            mo_d = dt_tmp("mo_d", [32, P, T])
            sumC_in = [dt_tmp(f"sumC_in{j}", [2 * DK, DV]) for j in range(4)]
            sumC_all = [dt_tmp(f"sumC_all{j}", [4 * 2 * DK, DV]) for j in range(4)]
            sumS_in = dt_tmp("sumS_in", [5, 512])
            sumS_all = dt_tmp("sumS_all", [4 * 5, 512])
            c_w_in = c_w_in_l()
            c_w_out = c_w_out_l()
            esel_d = dt_in("esel", [8, 8 * P])
            st_m_c = dt_in("st_m_c", [8, 1])
            pL = ExitStack()
            pL.__enter__()
            igT = sb(pL, "igT", [8, T])
            lfc = sb(pL, "lfc", [8, T])
            b_igT = ker.buf("igT")
            b_lfc = ker.buf("lfc")
            with ExitStack() as pA:
                A = sb(pA, "A1", [P, KT, T], BF16)
                b_A = ker.buf("A1")
                with ExitStack() as ps:
                    rmsnorm_to(ps, A, b_A, 2)
                    ker.barrier()
                with ExitStack() as ps:
                    alloc_wslots(ps)
                    stg = [sb(ps, f"stg{i}", [P, T]) for i in range(2)]
                    b_stg = [ker.buf(f"stg{i}", dma=True) for i in range(2)]
                    stb = [sb(ps, f"stb{i}", [P, T], BF16) for i in range(2)]
                    b_stb = [ker.buf(f"stb{i}", dma=True) for i in range(2)]
                    tkb = [sb(ps, f"tkb{i}", [P, 9, P], BF16) for i in range(2)]
                    b_tkb = [ker.buf(f"tkb{i}", dma=True) for i in range(2)]
                    bi_t = sb(ps, "bi_t", [8, 1])
                    nbf_t = sb(ps, "nbf_t", [8, 1])
                    b_bif = ker.buf("bif", dma=True)
                    ker.dma(sp, bi_t[:], c_bif[0:8, :], pwrites=[b_bif], sbuf=b_bif)
                    ker.dma(sp, nbf_t[:], c_bif[8:16, :], pwrites=[b_bif], sbuf=b_bif)
                    ker.op(dve, lambda: nc.vector.tensor_scalar(out=nbf_t[:], in0=nbf_t[:], scalar1=-1.0, scalar2=None, op0=ALU.mult),
                           reads=[b_bif], writes=[b_bif])
                    ctr = [0]
                    cur = {}

                    def epi(tag, ci, c0, cn, m, bk, bb):
                        kind, idx = tag
                        if ci == 0:
                            cur["s"] = ctr[0] % 2
                            ctr[0] += 1
                        s = cur["s"]
                        if kind == "q":
                            evac(stb[s][:, c0:c0 + cn], bk[:, :cn], [bb], [b_stb[s]])
                            if ci == 2:
                                ker.dma(sp, mq_d[idx], stb[s][:], reads=[b_stb[s]], sbuf=b_stb[s])
                            return
                        if kind == "o":
                            ker.op(act, lambda: nc.scalar.activation(out=stg[s][:, c0:c0 + cn], in_=bk[:, :cn], func=AF.Sigmoid),
                                   reads=[bb], pwrites=[b_stg[s]])
                            if ci == 2:
                                ker.dma(sp, mo_d[idx], stg[s][:], reads=[b_stg[s]], sbuf=b_stg[s])
                            return
                        if kind == "i":
                            ker.op(act, lambda: nc.scalar.activation(out=igT[:, c0:c0 + cn], in_=bk[:8, :cn], func=AF.Identity,
                                                                     scale=1.0, bias=bi_t[:, 0:1]),
                                   reads=[bb, b_bif], pwrites=[b_igT])
                            return
                        if kind == "f":
                            ker.op(act, lambda: nc.scalar.activation(out=lfc[:, c0:c0 + cn], in_=bk[:8, :cn], func=AF.Exp, scale=-1.0, bias=nbf_t[:, 0:1]),
                                   reads=[bb, b_bif], pwrites=[b_lfc])
                            ker.op(act, lambda: nc.scalar.activation(out=lfc[:, c0:c0 + cn], in_=lfc[:, c0:c0 + cn], func=AF.Ln, scale=1.0, bias=ones_f[:8, 0:1]),
                                   reads=[b_lfc, b_ones], pwrites=[b_lfc])
                            ker.op(dve, lambda: nc.vector.tensor_scalar(out=lfc[:, c0:c0 + cn], in0=lfc[:, c0:c0 + cn], scalar1=-1.0, scalar2=None, op0=ALU.mult),
                                   reads=[b_lfc], pwrites=[b_lfc])
                            return
                        if kind == "k":
                            ker.op(act, lambda: nc.scalar.mul(out=stg[s][:, c0:c0 + cn], in_=bk[:, :cn], mul=DK ** -0.5), reads=[bb], pwrites=[b_stg[s]])
                            ker.op(dve, lambda: nc.vector.tensor_copy(out=stb[s][:, c0:c0 + cn], in_=stg[s][:, c0:c0 + cn]),
                                   reads=[b_stg[s]], pwrites=[b_stb[s]])
                        else:
                            evac(stg[s][:, c0:c0 + cn], bk[:, :cn], [bb], [b_stg[s]])
                        if ci != 2:
                            return
                        if kind == "k":
                            ker.dma(sp, mkT_d[idx], stb[s][:], reads=[b_stb[s]], sbuf=b_stb[s])
                        for g0, g1 in ((0, 4), (4, 8), (8, 9)):
                            srcs = [((stg[s][:, TT[ti][0]:TT[ti][0] + TT[ti][1]], [b_stg[s]]), TT[ti][1], P) for ti in range(g0, g1)]

                            def wr(bk2, bb2, g0=g0, g1=g1, s=s):
                                if g0 < 8:
                                    evac(tkb[s][:, g0:g1, :], bk2[:, :].rearrange("p (j n) -> p j n", n=P), [bb2], [b_tkb[s]])
                                else:
                                    evac(tkb[s][:TS, 8, :], bk2[:TS, 0:P], [bb2], [b_tkb[s]])
                            transpose_group(srcs, wr)
                        dstd = mk_d if kind == "k" else mv_d
                        col0 = idx * P
                        ker.dma(sp, dstd[0:TP, col0:col0 + P].rearrange("(tt p) n -> p tt n", p=P), tkb[s][:, 0:8, :],
                                reads=[b_tkb[s]], sbuf=b_tkb[s])
                        ker.dma(sp, dstd[TP:T, col0:col0 + P], tkb[s][:TS, 8, :], reads=[b_tkb[s]], sbuf=b_tkb[s])

                    blocks, jobs = [], []
                    kinds = ["q"] * 16 + ["k"] * 16 + ["v"] * 32 + ["o"] * 32
                    base = {"q": 0, "k": 16, "v": 32, "o": 64}
                    for b in range(48):
                        blocks.append((c_w_in[:, b * 256:(b + 1) * 256], KT, 256))
                        for j in range(2):
                            oi = b * 2 + j
                            jobs.append((b, j * P, P, (kinds[oi], oi - base[kinds[oi]])))
                    blocks.append((c_w_in[:, 12176:12304], KT, 128))
                    jobs.append((48, 112, 8, ("i", 0)))
                    jobs.append((48, 120, 8, ("f", 0)))
                    gemm(lambda tag, kt: (A[:, kt, :], b_A), blocks, jobs, CH, epi)
                    ker.barrier()
            if STAGE < 7:
                pL.__exit__(None, None, None)
                return
            pC = ExitStack()
            pC.__enter__()
            moutT = sb(pC, "moutT", [P, KT, T], BF16)
            b_mout = ker.buf("moutT")
            negM = sb(pC, "negM", [8, T])
            negm = sb(pC, "negm", [8, T])
            wint = sb(pC, "wint", [8, T])
            b_rows = ker.buf("rows")
            a_tm = sb(pC, "a_tm", [P, 9, 8])
            wS_tm = sb(pC, "wS_tm", [P, 8, 8])
            b_tm = ker.buf("tm")
            esel = sb(pC, "esel", [8, 8, P])
            b_esel = ker.buf("esel", dma=True)
            hn = sb(pC, "hn", [P, 32])
            b_hn = ker.buf("hn", dma=True)
            pmk = sb(pC, "pmk1", [P, 8])
            b_pmk = ker.buf("pmk1", dma=True)
            stm = sb(pC, "stm", [P, 8])
            b_stm = ker.buf("stm", dma=True)
            msc = sb(pC, "msc", [8, 8])
            b_msc = ker.buf("msc", dma=True)
            minb = sb(pC, "minb", [P, 8])
            wrr = sb(pC, "wrr", [P, 4, 8])
            b_comb = ker.buf("comb")
            ker.dma(sp, esel[:], esel_d.rearrange("a (h m) -> a h m", m=P), writes=[b_esel], sbuf=b_esel)
            ker.dma(sp, hn[:], c_hn, writes=[b_hn], sbuf=b_hn)
            ker.dma(sp, pmk[:], pmask_b, writes=[b_pmk], sbuf=b_pmk)
            ker.dma(sp, stm[:], st_m_b, writes=[b_stm], sbuf=b_stm)
            ker.dma(sp, msc[:, 4:5], st_m_c, writes=[b_msc], sbuf=b_msc)
            pR = ExitStack()
            pR.__enter__()
            Bc = sb(pR, "Bc", [8, T])
            M0 = sb(pR, "M0", [8, T])
            b_BM = ker.buf("BM")
            with ExitStack() as ps:
                one8 = sb(ps, "one8", [8, T])
                av = sb(ps, "av", [8, T])
                wS = sb(ps, "wS", [8, TP])
                b_t = ker.buf("gtmp")
                ker.op(dve, lambda: nc.vector.memset(one8[:], 1.0), writes=[b_t])
                for (c0, cn) in ((0, TP), (TP, TS)):
                    ker.op(dve, lambda: nc.vector.tensor_tensor_scan(out=Bc[:, c0:c0 + cn], data0=one8[:, c0:c0 + cn], data1=lfc[:, c0:c0 + cn],
                                                                   initial=0.0, op0=ALU.mult, op1=ALU.add),
                           reads=[b_t, b_lfc], writes=[b_BM])
                ker.op(dve, lambda: nc.vector.tensor_tensor(out=av[:], in0=igT[:], in1=Bc[:], op=ALU.subtract), reads=[b_igT, b_BM], writes=[b_t])
                for (c0, cn) in ((0, TP), (TP, TS)):
                    ker.op(dve, lambda: nc.vector.tensor_tensor_scan(out=M0[:, c0:c0 + cn], data0=one8[:, c0:c0 + cn], data1=av[:, c0:c0 + cn],
                                                                   initial=-1e30, op0=ALU.mult, op1=ALU.max),
                           reads=[b_t], writes=[b_BM])
                ker.op(dve, lambda: nc.vector.tensor_copy(out=msc[:, 0:1], in_=Bc[:, TP - 1:TP]), reads=[b_BM], writes=[b_msc])
                ker.op(dve, lambda: nc.vector.tensor_copy(out=msc[:, 1:2], in_=M0[:, TP - 1:TP]), reads=[b_BM], writes=[b_msc])
                ker.op(dve, lambda: nc.vector.tensor_scalar(out=msc[:, 2:3], in0=M0[:, TP - 1:TP], scalar1=-1.0, scalar2=None, op0=ALU.mult),
                       reads=[b_BM], writes=[b_msc])
                ker.op(act, lambda: nc.scalar.activation(out=wS[:], in_=av[:, 0:TP], func=AF.Exp, scale=1.0, bias=msc[:, 2:3]),
                       reads=[b_t, b_msc], writes=[b_t])
                bk2, bb2 = tr_bank()
                for ti, (t0, tn) in enumerate(TT):
                    ker.op(pe, lambda: nc.tensor.transpose(bk2[:tn, ti * 8:(ti + 1) * 8], av[:, t0:t0 + tn], ident[:8, :8]),
                           reads=[b_t, b_cst], writes=[bb2] if ti == 0 else (), pwrites=() if ti == 0 else [bb2])
                ker.op(dve, lambda: nc.vector.memset(a_tm[:], 0.0), writes=[b_tm])
                evac(a_tm[:, 0:8, :], bk2[:, 0:64].rearrange("p (j n) -> p j n", n=8), [bb2], [b_tm])
                evac(a_tm[:TS, 8, :], bk2[:TS, 64:72], [bb2], [b_tm])
                bk2, bb2 = tr_bank()
                for ti in range(8):
                    ker.op(pe, lambda: nc.tensor.transpose(bk2[:, ti * 8:(ti + 1) * 8], wS[:, ti * P:(ti + 1) * P], ident[:8, :8]),
                           reads=[b_t, b_cst], writes=[bb2] if ti == 0 else (), pwrites=() if ti == 0 else [bb2])
                evac(wS_tm[:, :, :], bk2[:, 0:64].rearrange("p (j n) -> p j n", n=8), [bb2], [b_tm])
                b_sum_in = ker.buf("sum_in")
                zrow = sb(ps, "zrow", [1, 512])
                b_zrow = ker.buf("zrow", dma=True)
                ker.op(dve, lambda: nc.vector.memset(zrow[:], 0.0), writes=[b_zrow])
                ker.dma(sp, sumS_in[4:5, :], zrow[:], reads=[b_zrow], writes=[b_sum_in], sbuf=b_zrow)
                ker.dma(sp, sumS_in[4:5, 0:8].rearrange("a h -> h a"), msc[:, 0:1], reads=[b_msc], pwrites=[b_sum_in], sbuf=b_msc)
                ker.dma(sp, sumS_in[4:5, 8:16].rearrange("a h -> h a"), msc[:, 1:2], reads=[b_msc], pwrites=[b_sum_in], sbuf=b_msc)
                ker.barrier()
            with ExitStack() as ps:
                k1 = [sb(ps, f"k1_{i}", [P, 8, DK], BF16) for i in range(2)]
                v1 = [sb(ps, f"v1_{i}", [P, 8, DV], BF16) for i in range(2)]
                b_kv1 = [ker.buf(f"kv1_{i}", dma=True) for i in range(2)]
                kw1 = sb(ps, "kw1", [P, 8, DK], BF16)
                b_kw1 = ker.buf("kw1")
                cl = [sb(ps, f"cl{i}", [P, 2, DV]) for i in range(2)]
                b_cl = [ker.buf(f"cl{i}", dma=True) for i in range(2)]
                nl = [sb(ps, f"nl{i}", [P, 2]) for i in range(2)]
                b_nl = [ker.buf(f"nl{i}", dma=True) for i in range(2)]
                n_flat = sumS_in[0:4, :].rearrange("a (b p) -> p (a b)", p=P)
                for h in range(H_C):
                    s2 = h % 2
                    ker.dma(sp, k1[s2][:], mk_d[0:TP, h * DK:(h + 1) * DK].rearrange("(tt p) n -> p tt n", p=P), pwrites=[b_kv1[s2]], sbuf=b_kv1[s2])
                    ker.dma(sp, v1[s2][:], mv_d[0:TP, h * DV:(h + 1) * DV].rearrange("(tt p) n -> p tt n", p=P), pwrites=[b_kv1[s2]], sbuf=b_kv1[s2])
                    for tt in range(8):
                        ker.op(dve, lambda: nc.vector.tensor_scalar(out=kw1[:, tt, :], in0=k1[s2][:, tt, :], scalar1=wS_tm[:, tt, h:h + 1], scalar2=None, op0=ALU.mult),
                               reads=[b_kv1[s2], b_tm], writes=[b_kw1] if tt == 0 else (), pwrites=() if tt == 0 else [b_kw1])
                    for dkt in range(2):
                        bk, bb = next_acc()
                        for tt in range(8):
                            ker.op(pe, lambda: nc.tensor.matmul(bk[:, :DV], kw1[:, tt, dkt * P:(dkt + 1) * P], v1[s2][:, tt, :], start=(tt == 0), stop=(tt == 7)),
                                   reads=[b_kw1, b_kv1[s2]], writes=[bb], signal=(tt == 7))
                        evac(cl[s2][:, dkt, :], bk[:, :DV], [bb], [b_cl[s2]])
                    bk, bb = next_acc()
                    for dkt in range(2):
                        for tt in range(8):
                            ker.op(pe, lambda: nc.tensor.matmul(bk[:, dkt:dkt + 1], kw1[:, tt, dkt * P:(dkt + 1) * P], ones_b[:, 0:1], start=(tt == 0), stop=(tt == 7)),
                                   reads=[b_kw1, b_ones], writes=[bb] if (dkt == 0 and tt == 0) else (), pwrites=() if (dkt == 0 and tt == 0) else [bb],
                                   signal=(tt == 7))
                    evac(nl[s2][:, :], bk[:, 0:2], [bb], [b_nl[s2]])
                    ker.dma(sp, sumC_in[h // 2][(h % 2) * DK:(h % 2 + 1) * DK, :].rearrange("(t p) v -> p t v", p=P), cl[s2][:], reads=[b_cl[s2]], pwrites=[b_sum_in], sbuf=b_cl[s2])
                    ker.dma(sp, n_flat[:, h * 2:h * 2 + 2], nl[s2][:], reads=[b_nl[s2]], pwrites=[b_sum_in], sbuf=b_nl[s2], allow_slow_non_contiguous=True)
                ker.barrier()
            for j in range(4):
                all_gather(sumC_in[j], sumC_all[j], [b_sum_in], [b_gath["sum_all"]] if j == 0 else [], [] if j == 0 else [b_gath["sum_all"]])
            all_gather(sumS_in, sumS_all, [b_sum_in], [], [b_gath["sum_all"]])
            with ExitStack() as ps:
                scrow = sb(ps, "scrow", [P, 4, 16])
                b_scrow = ker.buf("scrow", dma=True)
                scb = sb(ps, "scb", [P, 4, 16])
                boffs = sb(ps, "boffs", [P, 5, 8])
                Lp = sb(ps, "Lp", [P, 4, 8])
                pen = sb(ps, "pen", [P, 4])
                mrel = sb(ps, "mrel", [P, 8])
                tmp8 = sb(ps, "tmp8", [P, 8])
                b_c = ker.buf("combtmp")
                ker.op(dve, lambda: nc.vector.memset(scrow[:], 0.0), writes=[b_scrow])
                ker.dma(sp, scrow[0:1, :, :], sumS_all.rearrange("(r a) v -> a r v", a=5)[4:5, :, 0:16], reads=[b_gath["sum_all"]], pwrites=[b_scrow], sbuf=b_scrow)
                bk, bb = next_acc()
                ker.op(pe, lambda: nc.tensor.matmul(bk[:, 0:64], ones_f[:, :], scrow[:, :, :].rearrange("p r v -> p (r v)"), start=True, stop=True),
                       reads=[b_scrow, b_ones], writes=[bb])
                ker.op(dve, lambda: nc.vector.tensor_copy(out=scb[:], in_=bk[:, 0:64].rearrange("p (r v) -> p r v", v=16)), reads=[bb], writes=[b_c])
                D_ = lambda f, **kw: ker.op(dve, f, reads=[b_c, b_pmk], writes=[b_c])
                D_(lambda: nc.vector.memset(boffs[:], 0.0))
                for r in range(3):
                    D_(lambda: nc.vector.tensor_tensor(out=boffs[:, r + 1, :], in0=boffs[:, r, :], in1=scb[:, r, 0:8], op=ALU.add))
                for r in range(4):
                    D_(lambda: nc.vector.scalar_tensor_tensor(out=boffs[:, 4, :], in0=scb[:, r, 0:8], scalar=pmk[:, r:r + 1], in1=boffs[:, 4, :],
                                                             op0=ALU.mult, op1=ALU.add))
                D_(lambda: nc.vector.tensor_scalar(out=pen[:], in0=pmk[:, 0:4], scalar1=-1.0, scalar2=1e30, op0=ALU.add, op1=ALU.mult))
                D_(lambda: nc.vector.memset(mrel[:], 0.0))
                for r in range(4):
                    D_(lambda: nc.vector.tensor_tensor(out=Lp[:, r, :], in0=scb[:, r, 8:16], in1=boffs[:, r, :], op=ALU.subtract))
                    D_(lambda: nc.vector.tensor_scalar(out=tmp8[:], in0=Lp[:, r, :], scalar1=pen[:, r:r + 1], scalar2=None, op0=ALU.add))
                    D_(lambda: nc.vector.tensor_tensor(out=mrel[:], in0=mrel[:], in1=tmp8[:], op=ALU.max))
                ker.op(dve, lambda: nc.vector.tensor_tensor(out=minb[:], in0=boffs[:, 4, :], in1=mrel[:], op=ALU.add), reads=[b_c], writes=[b_comb])
                for r in range(4):
                    D_(lambda: nc.vector.tensor_tensor(out=tmp8[:], in0=Lp[:, r, :], in1=mrel[:], op=ALU.subtract))
                    D_(lambda: nc.vector.tensor_scalar(out=tmp8[:], in0=tmp8[:], scalar1=0.0, scalar2=None, op0=ALU.min))
                    ker.op(act, lambda: nc.scalar.activation(out=tmp8[:], in_=tmp8[:], func=AF.Exp), reads=[b_c], writes=[b_c])
                    ker.op(dve, lambda: nc.vector.tensor_scalar(out=wrr[:, r, :], in0=tmp8[:], scalar1=pmk[:, r:r + 1], scalar2=None, op0=ALU.mult),
                           reads=[b_c, b_pmk], pwrites=[b_comb])
                dg = sb(ps, "dg", [8, 8])
                b_dg = ker.buf("dg")
                ker.op(dve, lambda: nc.vector.tensor_tensor(out=dg[:, :], in0=minb[:8, 0:8], in1=ident[:8, :8], op=ALU.mult),
                       reads=[b_comb, b_cst], writes=[b_dg])
                bk2, bb2 = next_acc()
                ker.op(pe, lambda: nc.tensor.matmul(bk2[:8, 0:1], dg[:, :], ones_f[:8, 0:1], start=True, stop=True), reads=[b_dg, b_ones], writes=[bb2])
                ker.op(dve, lambda: nc.vector.tensor_copy(out=msc[:, 3:4], in_=bk2[:8, 0:1]), reads=[bb2], writes=[b_msc])
                ker.barrier()
            for (c0, cn, mc) in ((0, TP, 3), (TP, TS, 4)):
                ker.op(dve, lambda: nc.vector.tensor_scalar(out=negM[:, c0:c0 + cn], in0=M0[:, c0:c0 + cn], scalar1=msc[:, mc:mc + 1], scalar2=-1.0,
                                                            op0=ALU.max, op1=ALU.mult),
                       reads=[b_BM, b_msc], pwrites=[b_rows])
                ker.op(dve, lambda: nc.vector.tensor_tensor(out=negm[:, c0:c0 + cn], in0=negM[:, c0:c0 + cn], in1=Bc[:, c0:c0 + cn], op=ALU.subtract),
                       reads=[b_BM, b_rows], pwrites=[b_rows])
                ker.op(act, lambda: nc.scalar.activation(out=wint[:, c0:c0 + cn], in_=negM[:, c0:c0 + cn], func=AF.Exp, scale=1.0, bias=msc[:, mc:mc + 1]),
                       reads=[b_rows, b_msc], pwrites=[b_rows])
                ker.op(dve, lambda: nc.vector.tensor_scalar(out=msc[:, mc + 2:mc + 3], in0=negm[:, c0 + cn - 1:c0 + cn], scalar1=-1.0, scalar2=None, op0=ALU.mult),
                       reads=[b_rows], writes=[b_msc])
            ker.dma(sp, o_m[0:1, :].rearrange("a h -> h a"), msc[:, 5:6], reads=[b_msc], sbuf=b_msc)
            ker.dma(sp, o_m[1:2, :].rearrange("a h -> h a"), msc[:, 6:7], reads=[b_msc], sbuf=b_msc)
            ker.barrier()
            pR.__exit__(None, None, None)
            with ExitStack() as ps:
                qT2 = [sb(ps, f"qT2_{i}", [P, 2, T], BF16) for i in range(2)]
                kT2 = [sb(ps, f"kT2_{i}", [P, 2, T], BF16) for i in range(2)]
                ktm = [sb(ps, f"ktm{i}", [P, 9, DK], BF16) for i in range(2)]
                vtm = [sb(ps, f"vtm{i}", [P, 9, DV], BF16) for i in range(2)]
                b_in2 = [ker.buf(f"in2_{i}", dma=True) for i in range(2)]
                so = sb(ps, "so", [P, 4, T])
                b_so = ker.buf("so", dma=True)
                Cst = sb(ps, "Cst", [P, 2, DV])
                Cb = sb(ps, "Cb", [P, 2, DV], BF16)
                nst = sb(ps, "nst", [P, 2])
                nb = sb(ps, "nb", [P, 2, P], BF16)
                Mprev = sb(ps, "Mprev", [P, 1])
                b_st = ker.buf("state", dma=True)
                clr = [sb(ps, f"clr{i}", [P, 2, DV]) for i in range(2)]
                b_clr = [ker.buf(f"clr{i}", dma=True) for i in range(2)]
                nlr = sb(ps, "nlr", [P, 4, 2])
                b_nlr = ker.buf("nlr", dma=True)
                bcs = sb(ps, "bcs", [P, 3, P])
                b_bcs = ker.buf("bcs")
                b_wib = ker.buf("wib")
                e1 = sb(ps, "e1", [P, P])
                Dm = sb(ps, "Dm", [P, P])
                Wt = sb(ps, "Wt", [P, P], BF16)
                qw = sb(ps, "qw", [P, 2, P], BF16)
                exm = sb(ps, "exm", [P, P])
                rd = sb(ps, "rd", [P, P])
                hT = sb(ps, "hT", [P, 4, P])
                hsq = sb(ps, "hsq", [P, 4, P])
                rs = sb(ps, "rs", [P, P])
                t1 = sb(ps, "t1", [P, P])
                wsc = sb(ps, "wsc", [P, 1])
                wcc = sb(ps, "wcc", [P, 1])
                kwt = sb(ps, "kwt", [P, DK], BF16)
                b_x = {n: ker.buf(n) for n in ("e1", "Dm", "Wt", "qw", "exm", "rd", "hT", "hsq", "rs", "t1", "wsc", "wcc", "kwt")}

                def load_head(h):
                    s2 = h % 2
                    ker.dma(sp, qT2[s2][:], mq_d[2 * h:2 * h + 2].rearrange("k p t -> p k t"), pwrites=[b_in2[s2]], sbuf=b_in2[s2])
                    ker.dma(sp, kT2[s2][:], mkT_d[2 * h:2 * h + 2].rearrange("k p t -> p k t"), pwrites=[b_in2[s2]], sbuf=b_in2[s2])
                    ker.dma(sp, ktm[s2][:, 0:8, :], mk_d[0:TP, h * DK:(h + 1) * DK].rearrange("(tt p) n -> p tt n", p=P), pwrites=[b_in2[s2]], sbuf=b_in2[s2])
                    ker.dma(sp, ktm[s2][:TS, 8, :], mk_d[TP:T, h * DK:(h + 1) * DK], pwrites=[b_in2[s2]], sbuf=b_in2[s2])
                    ker.dma(sp, vtm[s2][:, 0:8, :], mv_d[0:TP, h * DV:(h + 1) * DV].rearrange("(tt p) n -> p tt n", p=P), pwrites=[b_in2[s2]], sbuf=b_in2[s2])
                    ker.dma(sp, vtm[s2][:TS, 8, :], mv_d[TP:T, h * DV:(h + 1) * DV], pwrites=[b_in2[s2]], sbuf=b_in2[s2])

                def refresh_state_copies():
                    ker.op(act, lambda: nc.scalar.copy(out=Cb[:], in_=Cst[:]), reads=[b_st], pwrites=[b_st])
                    for dkt in range(2):
                        ker.op(dve, lambda: nc.vector.tensor_scalar(out=nb[:, dkt, :], in0=ones_f[:, :], scalar1=nst[:, dkt:dkt + 1], scalar2=None, op0=ALU.mult),
                               reads=[b_st, b_ones], pwrites=[b_st])

                def tile_step(h, s2, ti, t0, L):
                    bX, bbX = next_acc()
                    for j, row in enumerate((negM, negm)):
                        ker.op(pe, lambda: nc.tensor.matmul(bX[:, j * P:j * P + L], esel[:, h, :], row[:, t0:t0 + L], start=True, stop=True),
                               reads=[b_rows, b_esel], writes=[bbX] if j == 0 else (), pwrites=() if j == 0 else [bbX])
                    ker.op(act, lambda: nc.scalar.copy(out=bcs[:, 0:2, :L], in_=bX[:, 0:2 * P].rearrange("p (j n) -> p j n", n=P)[:, :, :L]),
                           reads=[bbX], writes=[b_bcs])
                    ker.op(dve, lambda: nc.vector.tensor_scalar(out=e1[:L, :L], in0=bcs[:L, 0, :L], scalar1=a_tm[:L, ti, h:h + 1], scalar2=0.0,
                                                                op0=ALU.add, op1=ALU.min),
                           reads=[b_bcs, b_tm], writes=[b_x["e1"]])
                    ker.op(act, lambda: nc.scalar.activation(out=e1[:L, :L], in_=e1[:L, :L], func=AF.Exp), reads=[b_x["e1"]], writes=[b_x["e1"]])
                    ker.op(dve, lambda: nc.vector.tensor_tensor(out=Dm[:L, :L], in0=e1[:L, :L], in1=cst[:L, 2, :L], op=ALU.mult),
                           reads=[b_x["e1"], b_cst], writes=[b_x["Dm"]])
                    bS, bbS = next_acc()
                    for dkt in range(2):
                        ker.op(pe, lambda: nc.tensor.matmul(bS[:L, :L], kT2[s2][:, dkt, t0:t0 + L], qT2[s2][:, dkt, t0:t0 + L], start=(dkt == 0), stop=(dkt == 1)),
                               reads=[b_in2[s2]], writes=[bbS], signal=(dkt == 1))
                    ker.op(dve, lambda: nc.vector.tensor_tensor(out=Wt[:L, :L], in0=bS[:L, :L], in1=Dm[:L, :L], op=ALU.mult),
                           reads=[bbS, b_x["Dm"]], writes=[b_x["Wt"]])
                    ker.op(act, lambda: nc.scalar.activation(out=bcs[:, 2, :L], in_=bcs[:, 0, :L], func=AF.Exp, scale=1.0, bias=Mprev[:, 0:1]),
                           reads=[b_bcs, b_st], writes=[b_wib])
                    for dkt in range(2):
                        ker.op(dve, lambda: nc.vector.tensor_tensor(out=qw[:, dkt, :L], in0=qT2[s2][:, dkt, t0:t0 + L], in1=bcs[:, 2, :L], op=ALU.mult),
                               reads=[b_in2[s2], b_wib], writes=[b_x["qw"]] if dkt == 0 else (), pwrites=() if dkt == 0 else [b_x["qw"]])
                    bN, bbN = next_acc()
                    first = True
                    for dvt in range(4):
                        reg = bN[:, dvt * P:dvt * P + L]
                        ker.op(pe, lambda: nc.tensor.matmul(reg, vtm[s2][:L, ti, dvt * P:(dvt + 1) * P], Wt[:L, :L], start=True, stop=False),
                               reads=[b_in2[s2], b_x["Wt"]], writes=[bbN] if first else (), pwrites=() if first else [bbN], signal=False)
                        first = False
                        for dkt in range(2):
                            ker.op(pe, lambda: nc.tensor.matmul(reg, Cb[:, dkt, dvt * P:(dvt + 1) * P], qw[:, dkt, :L], start=False, stop=(dkt == 1)),
                                   reads=[b_st, b_x["qw"]], pwrites=[bbN], signal=(dkt == 1 and dvt == 3))
                    bD, bbD = next_acc()
                    ker.op(pe, lambda: nc.tensor.matmul(bD[:, :L], ones_b[:L, :], Wt[:L, :L], start=True, stop=False),
                           reads=[b_ones, b_x["Wt"]], writes=[bbD], signal=False)
                    for dkt in range(2):
                        ker.op(pe, lambda: nc.tensor.matmul(bD[:, :L], nb[:, dkt, :], qw[:, dkt, :L], start=False, stop=(dkt == 1)),
                               reads=[b_st, b_x["qw"]], pwrites=[bbD], signal=(dkt == 1))
                    ker.op(act, lambda: nc.scalar.activation(out=exm[:, :L], in_=bcs[:, 1, :L], func=AF.Exp), reads=[b_bcs], writes=[b_x["exm"]])
                    ker.op(act, lambda: nc.scalar.activation(out=rd[:, :L], in_=bD[:, :L], func=AF.Abs),
                           reads=[bbD], writes=[b_x["rd"]])
                    ker.op(dve, lambda: nc.vector.tensor_tensor(out=rd[:, :L], in0=rd[:, :L], in1=exm[:, :L], op=ALU.max),
                           reads=[b_x["rd"], b_x["exm"]], writes=[b_x["rd"]])
                    ker.op(dve, lambda: nc.vector.reciprocal(out=rd[:, :L], in_=rd[:, :L]), reads=[b_x["rd"]], writes=[b_x["rd"]])
                    for dvt in range(4):
                        ker.op(dve, lambda: nc.vector.tensor_tensor(out=hT[:, dvt, :L], in0=bN[:, dvt * P:dvt * P + L], in1=rd[:, :L], op=ALU.mult),
                               reads=[bbN, b_x["rd"]], writes=[b_x["hT"]] if dvt == 0 else (), pwrites=() if dvt == 0 else [b_x["hT"]])
                    ker.op(dve, lambda: nc.vector.tensor_tensor(out=hsq[:, :, :L], in0=hT[:, :, :L], in1=hT[:, :, :L], op=ALU.mult),
                           reads=[b_x["hT"]], writes=[b_x["hsq"]])
                    bQ, bbQ = next_acc()
                    for dvt in range(4):
                        ker.op(pe, lambda: nc.tensor.matmul(bQ[:, :L], ones_f[:, :], hsq[:, dvt, :L], start=(dvt == 0), stop=(dvt == 3)),
                               reads=[b_x["hsq"], b_ones], writes=[bbQ], signal=(dvt == 3))
                    ker.op(act, lambda: nc.scalar.activation(out=rs[:, :L], in_=bQ[:, :L], func=AF.Ln, scale=1.0 / DV, bias=eps_t[:, 0:1]),
                           reads=[bbQ, b_ones], writes=[b_x["rs"]])
                    ker.op(act, lambda: nc.scalar.activation(out=rs[:, :L], in_=rs[:, :L], func=AF.Exp, scale=-0.5), reads=[b_x["rs"]], writes=[b_x["rs"]])
                    for dvt in range(4):
                        ker.op(dve, lambda: nc.vector.scalar_tensor_tensor(out=t1[:, :L], in0=hT[:, dvt, :L], scalar=hn[:, h * 4 + dvt:h * 4 + dvt + 1], in1=rs[:, :L],
                                                                          op0=ALU.mult, op1=ALU.mult),
                               reads=[b_x["hT"], b_x["rs"], b_hn], writes=[b_x["t1"]])
                        ker.op(dve, lambda: nc.vector.tensor_tensor(out=moutT[:, h * 4 + dvt, t0:t0 + L], in0=t1[:, :L], in1=so[:, dvt, t0:t0 + L], op=ALU.mult),
                               reads=[b_x["t1"], b_so], pwrites=[b_mout])
                    ker.op(act, lambda: nc.scalar.activation(out=wsc[:L, :], in_=a_tm[:L, ti, h:h + 1], func=AF.Exp, scale=1.0, bias=bcs[:L, 0, L - 1:L]),
                           reads=[b_tm, b_bcs], writes=[b_x["wsc"]])
                    ker.op(act, lambda: nc.scalar.activation(out=wcc[:, :], in_=bcs[:, 0, L - 1:L], func=AF.Exp, scale=1.0, bias=Mprev[:, 0:1]),
                           reads=[b_bcs, b_st], writes=[b_x["wcc"]])
                    ker.op(dve, lambda: nc.vector.tensor_scalar(out=kwt[:L, :], in0=ktm[s2][:L, ti, :], scalar1=wsc[:L, 0:1], scalar2=None, op0=ALU.mult),
                           reads=[b_in2[s2], b_x["wsc"]], writes=[b_x["kwt"]])
                    for dkt in range(2):
                        bC, bbC = next_acc()
                        ker.op(pe, lambda: nc.tensor.matmul(bC[:, :DV], kwt[:L, dkt * P:(dkt + 1) * P], vtm[s2][:L, ti, :], start=True, stop=True),
                               reads=[b_x["kwt"], b_in2[s2]], writes=[bbC])
                        ker.op(dve, lambda: nc.vector.scalar_tensor_tensor(out=Cst[:, dkt, :], in0=Cst[:, dkt, :], scalar=wcc[:, 0:1], in1=bC[:, :DV],
                                                                          op0=ALU.mult, op1=ALU.add),
                               reads=[bbC, b_x["wcc"], b_st], writes=[b_st])
                    bn_, bbn_ = next_acc()
                    for dkt in range(2):
                        ker.op(pe, lambda: nc.tensor.matmul(bn_[:, dkt:dkt + 1], kwt[:L, dkt * P:(dkt + 1) * P], ones_b[:L, 0:1], start=True, stop=True),
                               reads=[b_x["kwt"], b_ones], writes=[bbn_] if dkt == 0 else (), pwrites=() if dkt == 0 else [bbn_])
                    ker.op(dve, lambda: nc.vector.scalar_tensor_tensor(out=nst[:, :], in0=nst[:, :], scalar=wcc[:, 0:1], in1=bn_[:, 0:2], op0=ALU.mult, op1=ALU.add),
                           reads=[bbn_, b_x["wcc"], b_st], writes=[b_st])
                    ker.op(dve, lambda: nc.vector.tensor_scalar(out=Mprev[:, :], in0=bcs[:, 0, L - 1:L], scalar1=-1.0, scalar2=None, op0=ALU.mult),
                           reads=[b_bcs, b_x["wcc"]], writes=[b_st])
                    refresh_state_copies()

                load_head(0)
                for h in range(H_C):
                    s2 = h % 2
                    if h + 1 < H_C:
                        load_head(h + 1)
                    ker.dma(sp, so[:], mo_d[4 * h:4 * h + 4].rearrange("k p t -> p k t"), writes=[b_so], sbuf=b_so)
                    for r in range(4):
                        s3 = r % 2
                        n_r = sumS_all[r * 5:r * 5 + 4, :].rearrange("a (b p) -> p (a b)", p=P)
                        ker.dma(sp, nlr[:, r, :], n_r[:, 2 * h:2 * h + 2], reads=[b_gath["sum_all"]], writes=[b_nlr] if r == 0 else (),
                                pwrites=() if r == 0 else [b_nlr], sbuf=b_nlr, allow_slow_non_contiguous=True)
                        ker.dma(sp, clr[s3][:], sumC_all[h // 2][r * 2 * DK + (h % 2) * DK:r * 2 * DK + (h % 2 + 1) * DK, :].rearrange("(t p) v -> p t v", p=P), reads=[b_gath["sum_all"]],
                                writes=[b_clr[s3]], sbuf=b_clr[s3])
                        if r == 0:
                            ker.op(dve, lambda: nc.vector.tensor_scalar(out=Cst[:], in0=clr[s3][:], scalar1=wrr[:, r, h:h + 1], scalar2=None, op0=ALU.mult),
                                   reads=[b_clr[s3], b_comb], writes=[b_st])
                            ker.op(dve, lambda: nc.vector.tensor_scalar(out=nst[:], in0=nlr[:, r, :], scalar1=wrr[:, r, h:h + 1], scalar2=None, op0=ALU.mult),
                                   reads=[b_nlr, b_comb, b_st], writes=[b_st])
                        else:
                            ker.op(dve, lambda: nc.vector.scalar_tensor_tensor(out=Cst[:], in0=clr[s3][:], scalar=wrr[:, r, h:h + 1], in1=Cst[:],
                                                                              op0=ALU.mult, op1=ALU.add),
                                   reads=[b_clr[s3], b_comb, b_st], writes=[b_st])
                            ker.op(dve, lambda: nc.vector.scalar_tensor_tensor(out=nst[:], in0=nlr[:, r, :], scalar=wrr[:, r, h:h + 1], in1=nst[:],
                                                                              op0=ALU.mult, op1=ALU.add),
                                   reads=[b_nlr, b_comb, b_st], writes=[b_st])
                    ker.op(dve, lambda: nc.vector.tensor_copy(out=Mprev[:, :], in_=minb[:, h:h + 1]), reads=[b_comb, b_st], writes=[b_st])
                    refresh_state_copies()
                    for ti in range(8):
                        tile_step(h, s2, ti, ti * P, P)
                    ker.dma(sp, o_c[0, h].rearrange("(t p) v -> p t v", p=P), Cst[:], reads=[b_st], sbuf=b_st)
                    ker.dma(sp, o_n[0, h:h + 1, :].rearrange("a (t p) -> p (a t)", p=P), nst[:], reads=[b_st], sbuf=b_st, allow_slow_non_contiguous=True)
                    ker.dma(sp, Cst[:], st_c[h].rearrange("(t p) v -> p t v", p=P), writes=[b_st], sbuf=b_st)
                    ker.dma(sp, nst[:], st_n[h:h + 1, :].rearrange("a (t p) -> p (a t)", p=P), reads=[b_st], writes=[b_st], sbuf=b_st, allow_slow_non_contiguous=True)
                    ker.op(dve, lambda: nc.vector.tensor_copy(out=Mprev[:, :], in_=stm[:, h:h + 1]), reads=[b_stm, b_st], writes=[b_st])
                    refresh_state_copies()
                    tile_step(h, s2, 8, TP, TS)
                    ker.dma(sp, o_c[1, h].rearrange("(t p) v -> p t v", p=P), Cst[:], reads=[b_st], sbuf=b_st)
                    ker.dma(sp, o_n[1, h:h + 1, :].rearrange("a (t p) -> p (a t)", p=P), nst[:], reads=[b_st], sbuf=b_st, allow_slow_non_contiguous=True)
                ker.barrier()
            if os.environ.get("KDEBUG"):
                b_dbg = ker.buf("dbg2", dma=True)
                ker.dma(sp, dt_tmp("dbg_mo", [KT, P, T], BF16).rearrange("k p t -> p k t"), moutT[:], reads=[b_mout], sbuf=b_dbg)
                ker.barrier()
            if STAGE >= 8 and 'g4' not in SKIP:
              with ExitStack() as ps:
                alloc_wslots(ps)
                ost = [sb(ps, f"ost{i}", [P, T]) for i in range(2)]
                b_ost = [ker.buf(f"ost{i}", dma="sw") for i in range(2)]
                blocks = [(c_w_out[:, b * 256:(b + 1) * 256], KT, 256) for b in range(16)]
                jobs = [(b, j * P, P, ("o", b * 2 + j)) for b in range(16) for j in range(2)]
                gemm(lambda tag, kt: (moutT[:, kt, :], b_mout), blocks, jobs, CH, make_accum_epi(ost, b_ost))
                ker.barrier()
            pC.__exit__(None, None, None)
            pL.__exit__(None, None, None)

        if STAGE >= 6:
            layer1()
        if STAGE >= 9 and 'ffn1' not in SKIP:
            ffn(1)

        with ExitStack() as ps:
            yst = [sb(ps, f"yst{i}", [P, 2, T]) for i in range(1)]
            b_yst = [ker.buf("yst0"), ker.buf("yst1")]
            ytk = [sb(ps, f"ytk{i}", [P, 9, 2 * P]) for i in range(2)]
            b_ytk = [ker.buf(f"ytk{i}", dma=True) for i in range(2)]
            cnt = [0]

            def fin_cb(kt, xs_t, b_xs_t, rstd, b_rstd):
                h = kt % 2
                ker.op(dve, lambda: nc.vector.scalar_tensor_tensor(out=yst[0][:, h, :], in0=xs_t[:], scalar=nrm[:, 4, kt:kt + 1],
                                                                  in1=rstd[:], op0=ALU.mult, op1=ALU.mult),
                       reads=[b_xs_t, b_rstd, b_nrm], writes=[b_yst[h]])
                s = (kt // 2) % 2
                for g0, g1 in ((0, 4), (4, 8), (8, 9)):
                    srcs = [((yst[0][:, h, TT[ti][0]:TT[ti][0] + TT[ti][1]], [b_yst[h]]), TT[ti][1], P) for ti in range(g0, g1)]

                    def wr(bk, bb, g0=g0, g1=g1, s=s, h=h):
                        if g0 < 8:
                            evac(ytk[s][:, g0:g1, h * P:(h + 1) * P], bk[:, :].rearrange("p (j n) -> p j n", n=P), [bb], [b_ytk[s]])
                        else:
                            evac(ytk[s][:TS, 8, h * P:(h + 1) * P], bk[:TS, 0:P], [bb], [b_ytk[s]])
                    transpose_group(srcs, wr)
                if h == 1:
                    col0 = (kt - 1) * P
                    ker.dma(sp, y_tok[0:TP, col0:col0 + 2 * P].rearrange("(tt p) n -> p tt n", p=P), ytk[s][:, 0:8, :],
                            reads=[b_ytk[s]], sbuf=b_ytk[s])
                    ker.dma(sp, y_tok[TP:T, col0:col0 + 2 * P], ytk[s][:TS, 8, :], reads=[b_ytk[s]], sbuf=b_ytk[s])

            rmsnorm_to(ps, None, None, 4, out_f32_cb=fin_cb)
            ker.barrier()
    _CACHE['declared'] = declared
    return nc


_CACHE = {}


def _get_program():
    if "nc" not in _CACHE:
        _CACHE["nc"] = build_program()
    return _CACHE["nc"]


def kernel(x_prompt, x_sample, cache_fox_k, cache_fox_v, cache_fox_logf, state_mlstm_c, state_mlstm_n,
           state_mlstm_m, norm_mix, norm_ffn, norm_final, ab_w_in, ab_w_out, gmlp_w_s, gmlp_b, fox_b_f,
           c_w_in, c_b_i, c_b_f, c_head_norm, c_w_out, ffn_w_gate, ffn_w_up, ffn_w_down):
    f = lambda a: np.ascontiguousarray(np.asarray(a, dtype=np.float32))
    x_prompt, x_sample = f(x_prompt), f(x_sample)
    nc = _get_program()
    nrm_all = np.stack([f(norm_mix)[0], f(norm_ffn)[0], f(norm_mix)[1], f(norm_ffn)[1], f(norm_final)], 0)
    norms = np.ascontiguousarray(nrm_all.reshape(5, KT, P).transpose(2, 0, 1))
    consts = np.zeros((P, 5, P), np.float32)
    consts[15, 4, :] = 1.0
    consts[:, 0, :] = np.eye(P)
    consts[:, 1, :] = np.tril(np.ones((P, P)))
    consts[:, 2, :] = np.triu(np.ones((P, P)))
    consts[127, 3, :] = 1.0
    kpos = np.zeros((P, 40), np.float32)
    for kt in range(32):
        kpos[:, kt] = kt * 128 + np.arange(P)
    kpos[:, 32] = 1024 + np.arange(P)
    kpos[16:, 32] = 1e9
    shared = {
        "norms": norms,
        "ab_w_in": f(ab_w_in)[0], "ab_w_out": f(ab_w_out)[0],
        "gmlp_ws": f(gmlp_w_s)[0],
        "gmlp_bb": np.ascontiguousarray(np.broadcast_to(f(gmlp_b)[0][None], (P, 8, P))),
        "fox_bf": f(fox_b_f)[0].reshape(16, 1),
        "c_w_in": f(c_w_in)[0],
        "c_bif": np.concatenate([f(c_b_i)[0], f(c_b_f)[0]]).reshape(16, 1),
        "c_hn": np.ascontiguousarray(f(c_head_norm)[0].reshape(32, P).T),
        "c_w_out": f(c_w_out)[0],
        "ffn_wg": f(ffn_w_gate), "ffn_wu": f(ffn_w_up), "ffn_wd": f(ffn_w_down),
        "consts": consts, "kpos": kpos,
        "esel": np.ascontiguousarray(np.broadcast_to(np.eye(8, dtype=np.float32)[:, :, None], (8, 8, P))).reshape(8, 8 * P),
    }
    in_maps = []
    for c in range(NCORES):
        b, p = c // 4, c % 4
        qpos = np.concatenate([p * TP + np.arange(TP), 1024 + np.arange(TS)]).astype(np.float32)
        pm = np.zeros((P, 8), np.float32)
        for r in range(4):
            pm[:, r] = 1.0 if r < p else 0.0
            pm[:, 4 + r] = 1.0 if r == p else 0.0
        m = dict(shared)
        m.update({
            "x_tok": np.ascontiguousarray(np.concatenate([x_prompt[b, p * TP:(p + 1) * TP], x_sample[c]], 0)),
            "cache_k": f(cache_fox_k)[0, c].reshape(1024, 2048),
            "cache_v": f(cache_fox_v)[0, c].reshape(1024, 2048),
            "cache_lf": f(cache_fox_logf)[0, c],
            "st_c": f(state_mlstm_c)[0, c],
            "st_n": f(state_mlstm_n)[0, c],
            "st_m_c": f(state_mlstm_m)[0, c].reshape(8, 1),
            "st_m_b": np.ascontiguousarray(np.broadcast_to(f(state_mlstm_m)[0, c][None], (P, H_C))),
            "qpos_b": np.ascontiguousarray(np.broadcast_to(qpos[None], (P, T))),
            "pmask_b": pm,
        })
        in_maps.append(m)
    decl = set(_CACHE['declared'])
    in_maps = [{k: v for k, v in m.items() if k in decl} for m in in_maps]
    res = run_bass_kernel_spmd(nc, in_maps, core_ids=list(range(NCORES)))
    R = res.results
    B, S = 2, 4096

    def prompt_rows(name, width):
        return np.stack([np.concatenate([R[b * 4 + p][name][:TP] for p in range(4)], 0) for b in range(B)], 0).reshape(B, S, width)

    def sample_rows(name, width):
        return np.stack([R[c][name][TP:T] for c in range(NCORES)], 0).reshape(NCORES, TS, width)

    yp = prompt_rows("y_tok", D)
    ys = sample_rows("y_tok", D)
    pk = prompt_rows("o_k", 2048).reshape(1, B, S, 16, 128)
    pv = prompt_rows("o_v", 2048).reshape(1, B, S, 16, 128)
    plf = prompt_rows("o_lf", 16).reshape(1, B, S, 16)
    pc = np.stack([R[3]["o_c"][0], R[7]["o_c"][0]], 0)[None]
    pn = np.stack([R[3]["o_n"][0], R[7]["o_n"][0]], 0)[None]
    pm_ = np.stack([R[3]["o_m"][0], R[7]["o_m"][0]], 0)[None]
    sk = sample_rows("o_k", 2048).reshape(1, NCORES, TS, 16, 128)
    sv = sample_rows("o_v", 2048).reshape(1, NCORES, TS, 16, 128)
    slf = sample_rows("o_lf", 16).reshape(1, NCORES, TS, 16)
    sgv = np.stack([R[c]["o_gv"] for c in range(NCORES)], 0)[None]
    sc = np.stack([R[c]["o_c"][1] for c in range(NCORES)], 0)[None]
    sn = np.stack([R[c]["o_n"][1] for c in range(NCORES)], 0)[None]
    sm = np.stack([R[c]["o_m"][1] for c in range(NCORES)], 0)[None]
    outs = (yp, ys, pk, pv, plf, pc, pn, pm_, sk, sv, slf, sgv, sc, sn, sm)
    return tuple(np.ascontiguousarray(o, dtype=np.float32) for o in outs)
```
